# Optimizing a Trainium2 kernel written in Bass

```python
import math
import jax, jax.numpy as jnp
from jax import lax
import numpy as np

D_MODEL = 2048
BATCH = 32
SEQ = 256
DEPTH = 2
DEC_BATCH = 8
DEC_SEQ = 4096
PAST_LEN = 256

GRID_W = 64
D_FF = 4 * D_MODEL
N_MOD = 6
NORM_EPS = 1e-6

GLA_HEADS = 4
GLA_DK = 128
GLA_DV = 256
GLA_KEY_W = GLA_HEADS * GLA_DK
GLA_VAL_W = GLA_HEADS * GLA_DV
GLA_LOWRANK = 16
GLA_TAU = 16.0
GLA_CHUNK = 64
ROPE_BASE = 10000.0

NA_HEADS = 8
NA_HD = 128
NA_W = NA_HEADS * NA_HD
NA_KH = 8
NA_KW = 16
Q_BLOCK = 128

HY_W = 1024
HY_ORDER = 2
HY_SHORT = 3
HY_BANDS = 16
HY_POS_DIM = 1 + 2 * HY_BANDS
HY_HIDDEN = 64
HY_DECAY_TARGET = 1e-2
HY_DECAY_PCT_SHORT = 0.3
HY_DECAY_PCT_LONG = 1.5

N_BRANCH = 3
IN_SIZES = (GLA_KEY_W, GLA_KEY_W, GLA_VAL_W, GLA_VAL_W, 2 * GLA_LOWRANK,
            NA_W, NA_W, NA_W, (HY_ORDER + 1) * HY_W, N_BRANCH * D_MODEL)
IN_TOTAL = 2 * GLA_KEY_W + 2 * GLA_VAL_W + 2 * GLA_LOWRANK + 3 * NA_W + (HY_ORDER + 1) * HY_W + N_BRANCH * D_MODEL

kernel_name = 'hybrid_gla_natten_hyena_prefix_dit_step'

F32 = jnp.float32


def in_offsets():
    return [int(o) for o in np.cumsum(IN_SIZES)[:-1]]


def rms_norm(x, g):
    xf = x.astype(F32)
    y = xf * lax.rsqrt(jnp.mean(xf * xf, axis=-1, keepdims=True) + NORM_EPS)
    return (y * g.astype(F32)).astype(x.dtype)


def adaln(cond, w_mod, b_mod):
    m = jax.nn.silu(cond) @ w_mod + b_mod
    return jnp.split(m[:, None, :], N_MOD, axis=-1)


def to_heads(t, n_heads):
    B, L, W = t.shape
    return t.reshape(B, L, n_heads, W // n_heads).transpose(0, 2, 1, 3)


def from_heads(t):
    B, H, L, d = t.shape
    return t.transpose(0, 2, 1, 3).reshape(B, L, H * d)


def axial_rope(t):
    L, d = t.shape[2], t.shape[3]
    n_freq = d // 4
    pos = jnp.arange(L)
    row = (pos // GRID_W).astype(F32)
    col = (pos % GRID_W).astype(F32)
    inv = ROPE_BASE ** (-jnp.arange(n_freq, dtype=F32) / n_freq)
    ang = jnp.concatenate([row[:, None] * inv, col[:, None] * inv], axis=-1)
    cos, sin = jnp.cos(ang), jnp.sin(ang)
    t1, t2 = t[..., : d // 2], t[..., d // 2:]
    return jnp.concatenate([t1 * cos - t2 * sin, t1 * sin + t2 * cos], axis=-1)


def gla_scan(q, k, v, log_a, s0):
    B, H, L, _ = q.shape
    C = GLA_CHUNK
    n = L // C

    def to_chunks(t):
        return jnp.moveaxis(t.reshape(B, H, n, C, t.shape[-1]), 2, 0)

    causal = jnp.tril(jnp.ones((C, C), dtype=bool))[None, None, :, :, None]

    def step(s, inp):
        qi, ki, vi, ai = inp
        b = jnp.cumsum(ai, axis=2)
        o_inter = jnp.einsum('bhtd,bhde->bhte', qi * jnp.exp(b), s)
        diff = b[:, :, :, None, :] - b[:, :, None, :, :]
        decay = jnp.exp(jnp.where(causal, diff, -jnp.inf))
        attn = jnp.einsum('bhtd,bhsd,bhtsd->bhts', qi, ki, decay)
        o = o_inter + jnp.einsum('bhts,bhse->bhte', attn, vi)
        b_last = b[:, :, -1]
        s_new = jnp.exp(b_last)[..., None] * s + jnp.einsum(
            'bhsd,bhse->bhde', ki * jnp.exp(b_last[:, :, None, :] - b), vi)
        return s_new, o

    s_fin, o = lax.scan(step, s0, (to_chunks(q), to_chunks(k), to_chunks(v), to_chunks(log_a)))
    o = jnp.moveaxis(o, 0, 2).reshape(B, H, L, v.shape[-1])
    return o, s_fin


def gla_bidirectional(q, k, v, la_f, la_b, s0_f, s0_b):
    o_f, s_f = gla_scan(q, k, v, la_f, s0_f)
    rev = lambda t: jnp.flip(t, axis=2)
    o_b, s_b = gla_scan(rev(q), rev(k), rev(v), rev(la_b), s0_b)
    return o_f + rev(o_b), s_f, s_b


def dense_attention(q, k, v):
    B, H, L, hd = q.shape
    scale = hd ** -0.5
    qb = jnp.moveaxis(q.reshape(B, H, L // Q_BLOCK, Q_BLOCK, hd), 2, 0)

    def block(qi):
        s = jnp.einsum('bhqd,bhkd->bhqk', qi, k).astype(F32) * scale
        p = jax.nn.softmax(s, axis=-1).astype(v.dtype)
        return jnp.einsum('bhqk,bhkd->bhqd', p, v)

    o = lax.map(block, qb)
    return jnp.moveaxis(o, 0, 2).reshape(B, H, L, hd)


def neighbourhood_attention(q, k, v, k_ctx, v_ctx, rpb):
    B, H, L, hd = q.shape
    rows = L // GRID_W
    kh, kw = min(NA_KH, rows), NA_KW
    scale = hd ** -0.5
    qg = q.reshape(B, H, rows, GRID_W, hd)
    kg = k.reshape(B, H, rows, GRID_W, hd)
    vg = v.reshape(B, H, rows, GRID_W, hd)
    col = jnp.arange(GRID_W)
    c_start = jnp.clip(col - kw // 2, 0, GRID_W - kw)
    col_in = (col[None, :] >= c_start[:, None]) & (col[None, :] < c_start[:, None] + kw)
    col_idx = jnp.clip(col[None, :] - col[:, None] + kw - 1, 0, 2 * kw - 2)
    rpb_cols = rpb[:, :, col_idx]
    mask = col_in[None, None, :, None, :]

    def row_block(r):
        r_start = jnp.clip(r - kh // 2, 0, rows - kh)
        q_r = lax.dynamic_index_in_dim(qg, r, axis=2, keepdims=False)
        k_b = lax.dynamic_slice_in_dim(kg, r_start, kh, axis=2)
        v_b = lax.dynamic_slice_in_dim(vg, r_start, kh, axis=2)
        row_off = r_start + jnp.arange(kh) - r + (NA_KH - 1)
        bias = jnp.transpose(jnp.take(rpb_cols, row_off, axis=1), (0, 2, 1, 3))
        s_loc = jnp.einsum('bhqd,bhrkd->bhqrk', q_r, k_b).astype(F32) * scale + bias[None].astype(F32)
        s_loc = jnp.where(mask, s_loc, -jnp.inf)
        s_ctx = jnp.einsum('bhqd,bhcd->bhqc', q_r, k_ctx).astype(F32) * scale
        s = jnp.concatenate([s_loc.reshape(B, H, GRID_W, kh * GRID_W), s_ctx], axis=-1)
        p = jax.nn.softmax(s, axis=-1).astype(v.dtype)
        p_loc = p[..., : kh * GRID_W].reshape(B, H, GRID_W, kh, GRID_W)
        p_ctx = p[..., kh * GRID_W:]
        return (jnp.einsum('bhqrk,bhrkd->bhqd', p_loc, v_b)
                + jnp.einsum('bhqc,bhcd->bhqd', p_ctx, v_ctx))

    out = lax.map(row_block, jnp.arange(rows))
    return jnp.moveaxis(out, 0, 2).reshape(B, H, L, hd)


def hyena_filters(L, f1_w, f1_b, f2_w, f2_b, f3_w, freq):
    t = jnp.arange(L, dtype=F32)
    tn = t / L
    bands = jnp.arange(1, HY_BANDS + 1, dtype=F32)
    ang = 2.0 * jnp.pi * tn[:, None] * bands[None, :]
    z = jnp.concatenate([tn[:, None], jnp.cos(ang), jnp.sin(ang)], axis=-1)
    fr = freq.astype(F32)
    h = jnp.sin(fr * (z @ f1_w.astype(F32) + f1_b.astype(F32)))
    h = jnp.sin(fr * (h @ f2_w.astype(F32) + f2_b.astype(F32)))
    h = (h @ f3_w.astype(F32)).reshape(L, HY_ORDER, HY_W)
    dist = jnp.abs(t - L // 2) * (2.0 / L)
    deltas = jnp.abs(jnp.linspace(math.log(HY_DECAY_TARGET) / HY_DECAY_PCT_LONG,
                                  math.log(HY_DECAY_TARGET) / HY_DECAY_PCT_SHORT, HY_W, dtype=F32))
    h = h * jnp.exp(-dist[:, None, None] * deltas[None, None, :])
    return h * lax.rsqrt(jnp.sum(h * h, axis=0, keepdims=True) + NORM_EPS)


def hyena_operator(u, conv_w, conv_b, filters, bias):
    B, L, _ = u.shape
    pad = HY_SHORT // 2
    up = jnp.pad(u, ((0, 0), (pad, pad), (0, 0)))
    uc = conv_b + sum(up[:, j:j + L] * conv_w[j] for j in range(HY_SHORT))
    parts = jnp.split(uc.astype(F32), HY_ORDER + 1, axis=-1)
    gates, z = parts[:HY_ORDER], parts[HY_ORDER]
    h_f = jnp.fft.rfft(filters, n=2 * L, axis=0)
    for o in range(HY_ORDER):
        z_f = jnp.fft.rfft(z, n=2 * L, axis=1)
        conv = jnp.fft.irfft(z_f * h_f[None, :, o], n=2 * L, axis=1)[:, L // 2: L // 2 + L]
        z = gates[o] * (conv + bias[o].astype(F32) * z)
    return z


def token_mixing(x, shift, scale, p, latent_ctx):
    B, L, _ = x.shape
    h = rms_norm(x, p['norm1_g']) * (1 + scale) + shift
    gq, gk, gv, gg, ga, nq, nk, nv, hy, gates = jnp.split(h @ p['w_in'], in_offsets(), axis=-1)

    q = to_heads(gq, GLA_HEADS).astype(F32) * GLA_DK ** -0.5
    k = to_heads(gk, GLA_HEADS).astype(F32)
    v = to_heads(gv, GLA_HEADS).astype(F32)
    la = jnp.einsum('blir,ird->blid', ga.reshape(B, L, 2, GLA_LOWRANK), p['gla_wa2']) + p['gla_ba2']
    la = jax.nn.log_sigmoid(la.astype(F32)) / GLA_TAU
    la_f, la_b = to_heads(la[:, :, 0], GLA_HEADS), to_heads(la[:, :, 1], GLA_HEADS)
    if latent_ctx is None:
        s0_f = s0_b = jnp.zeros((B, GLA_HEADS, GLA_DK, GLA_DV), F32)
    else:
        s0_f, s0_b, k_ctx, v_ctx = latent_ctx
        q, k = axial_rope(q), axial_rope(k)
    o_a, s_f, s_b = gla_bidirectional(q, k, v, la_f, la_b, s0_f.astype(F32), s0_b.astype(F32))
    out_a = from_heads(rms_norm(o_a, p['gla_norm_g'])).astype(x.dtype) * jax.nn.silu(gg)

    qn = rms_norm(to_heads(nq, NA_HEADS), p['na_qnorm_g'])
    kn = rms_norm(to_heads(nk, NA_HEADS), p['na_knorm_g'])
    vn = to_heads(nv, NA_HEADS)
    if latent_ctx is None:
        o_b = dense_attention(qn, kn, vn)
    else:
        o_b = neighbourhood_attention(qn, kn, vn, k_ctx.astype(x.dtype), v_ctx.astype(x.dtype), p['na_rpb'])
    out_b = from_heads(o_b)

    filters = hyena_filters(L, p['hy_f1_w'], p['hy_f1_b'], p['hy_f2_w'], p['hy_f2_b'], p['hy_f3_w'], p['hy_freq'])
    out_c = hyena_operator(hy, p['hy_conv_w'], p['hy_conv_b'], filters, p['hy_bias']).astype(x.dtype)

    g_a, g_b, g_c = jnp.split(jax.nn.sigmoid(gates), N_BRANCH, axis=-1)
    merged = g_a * (out_a @ p['w_br_a']) + g_b * (out_b @ p['w_br_b']) + g_c * (out_c @ p['w_br_c'])
    y = merged @ p['w_out']
    new_ctx = (jnp.stack([s_f, s_b], axis=1), kn, vn) if latent_ctx is None else None
    return y, new_ctx


def trunk_layer(x, cond, p, latent_ctx):
    sh1, sc1, g1, sh2, sc2, g2 = adaln(cond, p['w_mod'], p['b_mod'])
    y, new_ctx = token_mixing(x, sh1, sc1, p, latent_ctx)
    x = x + g1 * y
    h = rms_norm(x, p['norm2_g']) * (1 + sc2) + sh2
    x = x + g2 * (jnp.square(jax.nn.relu(h @ p['w_mlp1'])) @ p['w_mlp2'])
    return x, new_ctx


def setup_inputs(seed: int = 0) -> dict:
    key = jax.random.key(seed)
    ks = iter(jax.random.split(key, 40))

    def nrm(shape, scale):
        return jax.random.normal(next(ks), shape, F32) * scale

    def gain(shape):
        return 1.0 + nrm(shape, 0.02)

    return {
        'x_prompt': nrm((BATCH, SEQ, D_MODEL), 1.0),
        'x_sample': nrm((DEC_BATCH, DEC_SEQ, D_MODEL), 1.0),
        'state_gla': nrm((DEC_BATCH, DEPTH, 2, GLA_HEADS, GLA_DK, GLA_DV), 0.5),
        'cache_na_k': nrm((DEC_BATCH, DEPTH, NA_HEADS, PAST_LEN, NA_HD), 1.0),
        'cache_na_v': nrm((DEC_BATCH, DEPTH, NA_HEADS, PAST_LEN, NA_HD), 1.0),
        'c': nrm((DEC_BATCH, D_MODEL), 1.0),
        'c_ctx': nrm((D_MODEL,), 1.0),
        'w_mod': nrm((DEPTH, D_MODEL, N_MOD * D_MODEL), D_MODEL ** -0.5),
        'b_mod': nrm((DEPTH, N_MOD * D_MODEL), 0.02),
        'norm1_g': gain((DEPTH, D_MODEL)),
        'norm2_g': gain((DEPTH, D_MODEL)),
        'w_in': nrm((DEPTH, D_MODEL, IN_TOTAL), D_MODEL ** -0.5),
        'gla_wa2': nrm((DEPTH, 2, GLA_LOWRANK, GLA_KEY_W), GLA_LOWRANK ** -0.5),
        'gla_ba2': nrm((DEPTH, 2, GLA_KEY_W), 0.02),
        'gla_norm_g': gain((DEPTH, GLA_DV)),
        'na_qnorm_g': gain((DEPTH, NA_HD)),
        'na_knorm_g': gain((DEPTH, NA_HD)),
        'na_rpb': nrm((DEPTH, NA_HEADS, 2 * NA_KH - 1, 2 * NA_KW - 1), 0.1),
        'hy_conv_w': nrm((DEPTH, HY_SHORT, (HY_ORDER + 1) * HY_W), HY_SHORT ** -0.5),
        'hy_conv_b': nrm((DEPTH, (HY_ORDER + 1) * HY_W), 0.02),
        'hy_f1_w': nrm((DEPTH, HY_POS_DIM, HY_HIDDEN), HY_POS_DIM ** -0.5),
        'hy_f1_b': nrm((DEPTH, HY_HIDDEN), 0.02),
        'hy_f2_w': nrm((DEPTH, HY_HIDDEN, HY_HIDDEN), HY_HIDDEN ** -0.5),
        'hy_f2_b': nrm((DEPTH, HY_HIDDEN), 0.02),
        'hy_f3_w': nrm((DEPTH, HY_HIDDEN, HY_ORDER * HY_W), HY_HIDDEN ** -0.5),
        'hy_freq': gain((DEPTH, HY_HIDDEN)),
        'hy_bias': nrm((DEPTH, HY_ORDER, HY_W), 0.1),
        'w_br_a': nrm((DEPTH, GLA_VAL_W, D_MODEL), GLA_VAL_W ** -0.5),
        'w_br_b': nrm((DEPTH, NA_W, D_MODEL), NA_W ** -0.5),
        'w_br_c': nrm((DEPTH, HY_W, D_MODEL), HY_W ** -0.5),
        'w_out': nrm((DEPTH, D_MODEL, D_MODEL), D_MODEL ** -0.5),
        'w_mlp1': nrm((DEPTH, D_MODEL, D_FF), D_MODEL ** -0.5),
        'w_mlp2': nrm((DEPTH, D_FF, D_MODEL), D_FF ** -0.5),
    }


def reference(x_prompt, x_sample, state_gla, cache_na_k, cache_na_v, c, c_ctx,
              w_mod, b_mod, norm1_g, norm2_g, w_in, gla_wa2, gla_ba2, gla_norm_g,
              na_qnorm_g, na_knorm_g, na_rpb, hy_conv_w, hy_conv_b, hy_f1_w, hy_f1_b,
              hy_f2_w, hy_f2_b, hy_f3_w, hy_freq, hy_bias, w_br_a, w_br_b, w_br_c,
              w_out, w_mlp1, w_mlp2):
    xp, xs = x_prompt, x_sample
    ctx_cond = c_ctx[None, :]
    new_gla, new_k, new_v = [], [], []
    for l in range(DEPTH):
        p = dict(w_mod=w_mod[l], b_mod=b_mod[l], norm1_g=norm1_g[l], norm2_g=norm2_g[l],
                 w_in=w_in[l], gla_wa2=gla_wa2[l], gla_ba2=gla_ba2[l], gla_norm_g=gla_norm_g[l],
                 na_qnorm_g=na_qnorm_g[l], na_knorm_g=na_knorm_g[l], na_rpb=na_rpb[l],
                 hy_conv_w=hy_conv_w[l], hy_conv_b=hy_conv_b[l], hy_f1_w=hy_f1_w[l],
                 hy_f1_b=hy_f1_b[l], hy_f2_w=hy_f2_w[l], hy_f2_b=hy_f2_b[l],
                 hy_f3_w=hy_f3_w[l], hy_freq=hy_freq[l], hy_bias=hy_bias[l],
                 w_br_a=w_br_a[l], w_br_b=w_br_b[l], w_br_c=w_br_c[l], w_out=w_out[l],
                 w_mlp1=w_mlp1[l], w_mlp2=w_mlp2[l])
        xp, (s_gla, k_ctx, v_ctx) = trunk_layer(xp, ctx_cond, p, None)
        new_gla.append(s_gla)
        new_k.append(k_ctx)
        new_v.append(v_ctx)
        latent_ctx = (state_gla[:, l, 0], state_gla[:, l, 1], cache_na_k[:, l], cache_na_v[:, l])
        xs, _ = trunk_layer(xs, c, p, latent_ctx)
    new_state_gla = jnp.stack(new_gla, axis=1)
    new_cache_na_k = jnp.stack(new_k, axis=1)
    new_cache_na_v = jnp.stack(new_v, axis=1)
    return (xp, xs, new_state_gla, new_cache_na_k, new_cache_na_v)
```

```python
import math
from contextlib import ExitStack
import numpy as np
import ml_dtypes
import concourse.bass as bass
import concourse.mybir as mybir
from concourse.bass_utils import run_bass_kernel_spmd

F32 = mybir.dt.float32
BF16 = mybir.dt.bfloat16
I32 = mybir.dt.int32
AF = mybir.ActivationFunctionType
ALU = mybir.AluOpType
AX = mybir.AxisListType

ENGS = ("sync", "act", "pool", "dve", "pe")
N_DMA_SEMS = 64
SAME_ENGINE_SYNC = True

D = 2048
NTOK = 5120
NT = 40
LS = 4096
LP = 256
DEPTH = 2
EPS = 1e-6
W_IN = 15392
NEG = -30000.0
import os
STOP = os.environ.get('KSTOP', '')
KGLA = int(os.environ.get('KGLA', '9'))
NOPOOL = int(os.environ.get('NOPOOL', '1'))
NCORES = int(os.environ.get('KCORES', '8'))


class Buf:
    __slots__ = ("w", "r", "const")

    def __init__(self, const=False):
        self.w = None
        self.r = {}
        self.const = const


class T:
    __slots__ = ("t", "b")

    def __init__(self, t, const=False):
        self.t = t
        self.b = Buf(const)

    def __getitem__(self, k):
        return self.t[k]


class KB:
    def __init__(self, nc, stack):
        self.nc = nc
        self.stack = stack
        self.ops = {e: [] for e in ENGS}
        self.sems = []
        for e in ENGS:
            self.sems.append(stack.enter_context(nc.semaphore("s_" + e)))
        self.own = {e: i for i, e in enumerate(ENGS)}
        self.cnt = {e: 0 for e in ENGS}
        self.seen = {e: {} for e in ENGS}
        self.dma_pools = {}
        for e, n in (("sync", 48), ("pool", 40), ("act", 8)):
            base = len(self.sems)
            for i in range(n):
                self.sems.append(stack.enter_context(nc.semaphore("s_dma_%s%d" % (e, i))))
            self.dma_pools[e] = {"base": base, "n": n, "tgt": [0] * n, "k": 0}
        self.n_inst = 0
        self.uid = 0

    def sb(self, shape, dtype, stack=None, const=False, name=None):
        st = stack or self.stack
        self.uid += 1
        return T(st.enter_context(self.nc.sbuf_tensor("%s_%d" % (name or "sb", self.uid), list(shape), dtype)), const)

    def ps(self, shape, dtype=F32, stack=None):
        st = stack or self.stack
        self.uid += 1
        return T(st.enter_context(self.nc.psum_tensor("ps_%d" % self.uid, list(shape), dtype)))

    def _wait(self, eng, ev):
        sk, v = ev
        if self.seen[eng].get(sk, 0) >= v:
            return
        self.seen[eng][sk] = v
        sem = self.sems[sk]
        self.ops[eng].append(lambda e, sem=sem, v=v: e.wait_ge(sem, v))

    def issue(self, eng, fns, reads=(), writes=(), dma=False):
        if callable(fns):
            fns = [fns]
        deps = []
        for t in reads:
            b = t.b
            if b.w is not None:
                deps.append(b.w)
        for t in writes:
            b = t.b
            if b.w is not None:
                deps.append(b.w)
            for sk, v in b.r.items():
                deps.append((sk, v))
        own = self.own[eng]
        for ev in deps:
            if ev[0] == own and not dma:
                if eng == "pe" or not SAME_ENGINE_SYNC:
                    continue
            self._wait(eng, ev)
        if dma:
            dp = self.dma_pools[eng]
            slot = dp["k"] % dp["n"]
            dp["k"] += 1
            sk = dp["base"] + slot
            if dp["tgt"][slot] > 0:
                self._wait(eng, (sk, dp["tgt"][slot]))
            dp["tgt"][slot] += 16
            ev = (sk, dp["tgt"][slot])
            inc = 16
        else:
            self.cnt[eng] += 1
            sk = own
            ev = (sk, self.cnt[eng])
            inc = 1
        sem = self.sems[sk]
        for f in fns[:-1]:
            self.ops[eng].append(f)
        last = fns[-1]
        self.ops[eng].append(lambda e, last=last, sem=sem, inc=inc: last(e).then_inc(sem, inc))
        self.n_inst += len(fns)
        for t in reads:
            b = t.b
            if not b.const:
                if b.r.get(ev[0], 0) < ev[1]:
                    b.r[ev[0]] = ev[1]
        for t in writes:
            t.b.w = ev
            t.b.r = {}
        return ev

    def dma(self, eng, out, in_, r=(), w=(), **kw):
        return self.issue(eng, lambda e: e.dma_start(out=out, in_=in_, **kw), r, w, dma=True)

    def barrier(self):
        evs = [(self.own[e], self.cnt[e]) for e in ENGS if self.cnt[e] > 0]
        for dp in self.dma_pools.values():
            for i in range(dp["n"]):
                if dp["tgt"][i] > 0:
                    evs.append((dp["base"] + i, dp["tgt"][i]))
        for e in ENGS:
            for ev in evs:
                if ev[0] == self.own[e]:
                    continue
                self._wait(e, ev)

    def mm(self, out, pairs, r, w, start=True, stop=True):
        n = len(pairs)
        fns = []
        for i, (l, rh) in enumerate(pairs):
            fns.append(lambda e, l=l, rh=rh, s0=(start and i == 0), s1=(stop and i == n - 1):
                       e.matmul(out, lhsT=l, rhs=rh, start=s0, stop=s1))
        return self.issue("pe", fns, r, w)

    def tr(self, out, in_, ident, r, w):
        return self.issue("pe", lambda e: e.transpose(out, in_, ident), r, w)

    def act(self, out, in_, func, r, w, bias=0.0, scale=1.0, accum_out=None):
        if accum_out is None:
            return self.issue("act", lambda e: e.activation(out=out, in_=in_, func=func, bias=bias, scale=scale), r, w)
        return self.issue("act", lambda e: e.activation(out=out, in_=in_, func=func, bias=bias, scale=scale,
                                                        accum_out=accum_out), r, w)

    def tt(self, out, in0, in1, op, r, w, eng="dve"):
        if NOPOOL:
            eng = "dve"
        return self.issue(eng, lambda e: e.tensor_tensor(out=out, in0=in0, in1=in1, op=op), r, w)

    def ts(self, out, in0, s1, s2, op0, op1, r, w, eng="dve"):
        if s2 is None:
            return self.issue(eng, lambda e: e.tensor_scalar(out=out, in0=in0, scalar1=s1, scalar2=None, op0=op0), r, w)
        return self.issue(eng, lambda e: e.tensor_scalar(out=out, in0=in0, scalar1=s1, scalar2=s2, op0=op0, op1=op1), r, w)

    def stt(self, out, in0, scalar, in1, op0, op1, r, w, eng="dve"):
        return self.issue(eng, lambda e: e.scalar_tensor_tensor(out=out, in0=in0, scalar=scalar, in1=in1, op0=op0, op1=op1), r, w)

    def cp(self, out, in_, r, w, eng="dve"):
        if NOPOOL:
            eng = "dve"
        return self.issue(eng, lambda e: e.tensor_copy(out=out, in_=in_), r, w)

    def recip(self, out, in_, r, w):
        return self.issue("dve", lambda e: e.reciprocal(out=out, in_=in_), r, w)

    def memset(self, ap, val, w, eng="pool"):
        return self.issue(eng, lambda e: e.memset(ap, val), (), w)

    def finish(self):
        self.barrier()
        nc = self.nc
        ops = self.ops
        with nc.Block() as block:
            @block.sync
            def _(e):
                for f in ops["sync"]:
                    f(e)

            @block.scalar
            def _(e):
                for f in ops["act"]:
                    f(e)

            @block.gpsimd
            def _(e):
                for f in ops["pool"]:
                    f(e)

            @block.vector
            def _(e):
                for f in ops["dve"]:
                    f(e)

            @block.tensor
            def _(e):
                for f in ops["pe"]:
                    f(e)


def _div_le(n, cap):
    for d in range(min(n, cap), 0, -1):
        if n % d == 0:
            return d
    return 1


IN_SPECS = [
    ("x", [NTOK, D], F32), ("condT", [128, 16, 2], F32),
    ("state0", [2, 2, 4, 128, 256], F32), ("ck", [2, 8, 256, 128], F32), ("cv", [2, 8, 256, 128], F32),
    ("w_mod", [2, D, 6 * D], F32), ("b_modT", [2, 128, 96], F32), ("b_mod", [2, 6 * D], F32),
    ("n1T", [2, 128, 16], F32), ("n2T", [2, 128, 16], F32),
    ("w_in", [2, D, W_IN], F32), ("wa2", [2, 2, 16, 512], F32), ("ba2", [2, 2, 512], F32),
    ("gng4", [2, 1024], F32), ("qg8", [2, 1024], F32), ("kg8", [2, 1024], F32),
    ("rpbp", [2, 8, 15, 127], F32),
    ("hcw", [2, 3, 3072], F32), ("hcb", [2, 3072], F32), ("f1w", [2, 33, 64], F32), ("f1b", [2, 64, 1], F32),
    ("f2w", [2, 64, 64], F32), ("f2b", [2, 64, 1], F32), ("f3w", [2, 64, 2048], F32), ("hfr", [2, 64, 1], F32),
    ("hbias", [2, 2, 1024], F32),
    ("w_br_a", [2, 1024, D], F32), ("w_br_b", [2, 1024, D], F32), ("w_br_c", [2, 1024, D], F32),
    ("w_out", [2, D, D], F32), ("w_mlp1", [2, D, 4 * D], F32), ("w_mlp2", [2, 4 * D, D], F32),
    ("cpack", [128, 640], F32), ("cos4", [LS, 256], F32), ("sin4", [LS, 256], F32), ("colmask", [128, 64], F32),
    ("zposL", [33, LS], F32), ("zposS", [33, LP], F32), ("ndistL", [128, 32], F32), ("ndistS", [128, 2], F32),
    ("deltas", [1024], F32),
    ("FCL", [32, 128, 4096], BF16), ("FSL", [32, 128, 4096], BF16), ("GCL", [32, 128, 4096], BF16), ("GSL", [32, 128, 4096], BF16),
    ("FCS", [2, 128, 256], BF16), ("FSS", [2, 128, 256], BF16), ("GCS", [2, 128, 256], BF16), ("GSS", [2, 128, 256], BF16),
]
OUT_SPECS = [
    ("ys", [LS, D], F32), ("yp", [4 * LP, D], F32), ("nst", [4, 2, 2, 4, 128, 256], F32),
    ("nk", [4, 2, 8, 256, 128], F32), ("nv", [4, 2, 8, 256, 128], F32),
]


def build_program():
    nc = bass.Bass("TRN2", target_bir_lowering=False)
    I = {n: nc.dram_tensor(n, s, d, kind="ExternalInput") for n, s, d in IN_SPECS}
    O = {n: nc.dram_tensor(n, s, d, kind="ExternalOutput") for n, s, d in OUT_SPECS}
    A = {n: t.ap() for n, t in I.items()}
    OA = {n: t.ap() for n, t in O.items()}

    def dram(name, shape, dt):
        return nc.dram_tensor(name, shape, dt, kind="Internal").ap()

    WB = {
        "w_in": dram("wb_in", [2, D, W_IN], BF16), "w_br_a": dram("wb_a", [2, 1024, D], BF16),
        "w_br_b": dram("wb_b", [2, 1024, D], BF16), "w_br_c": dram("wb_c", [2, 1024, D], BF16),
        "w_out": dram("wb_o", [2, D, D], BF16), "w_mlp1": dram("wb_1", [2, D, 4 * D], BF16),
        "w_mlp2": dram("wb_2", [2, 4 * D, D], BF16),
    }
    SC = dram("sc", [NTOK, 9216], F32)
    SGA = dram("sga", [2, 16, NTOK], F32)
    H1T = dram("h1t", [16, 128, NTOK], BF16)
    OF = dram("of", [NTOK, 1024], F32)
    OAT = dram("oat", [8, 128, NTOK], BF16)
    OBT = dram("obt", [8, 128, NTOK], BF16)
    OCT = dram("oct", [8, 128, NTOK], BF16)
    QNT = dram("qnt", [8, 128, NTOK], BF16)
    KNT = dram("knt", [8, 128, NTOK], BF16)
    XS = dram("xs", [NTOK, D], F32)
    X1 = dram("x1", [NTOK, D], F32)
    GSC = dram("gsc", [2, 2, 128, D], F32)
    HG = dram("hg", [NTOK, 2048], F32)
    ZF = [dram("zf0", [NTOK, 1024], F32), dram("zf1", [NTOK, 1024], F32)]
    ZB = [dram("zb0", [NTOK, 1024], BF16), dram("zb1", [NTOK, 1024], BF16)]
    HB = {LS: dram("hbL", [LS, 2048], BF16), LP: dram("hbS", [LP, 2048], BF16)}
    HF = {LS: dram("hfL", [2, 32, 128, 2, 1024], F32), LP: dram("hfS", [2, 2, 128, 2, 1024], F32)}

    with ExitStack() as st:
        kb = KB(nc, st)
        cp32 = kb.sb([128, 640], F32, const=True)
        cpb = kb.sb([128, 640], BF16, const=True)
        P = [kb.ps([128, 512], F32) for _ in range(6)]
        PB = [kb.ps([128, 1024], BF16) for _ in range(2)]
        kb.dma("sync", cp32[:], A["cpack"][:, :], w=[cp32])
        kb.cp(cpb[:], cp32[:], [cp32], [cpb])
        IDb = cpb[:, 0:128]
        TRIf = {0: cp32[:, 128:256], 1: cp32[:, 256:384]}
        TRIb = {0: cpb[:, 128:256], 1: cpb[:, 256:384]}
        ONESf = cp32[:, 384:512]
        ONESb = cpb[:, 384:512]
        JREV = cp32[:, 512:640]
        CB = [cp32, cpb]
        rr = {"pb": 0}
        A1 = kb.sb([128, 16, 2], F32, st, name="A1")
        A2 = kb.sb([128, 16, 2], F32, st, name="A2")
        B1 = kb.sb([128, 16, 2], F32, st, name="B1")
        B2 = kb.sb([128, 16, 2], F32, st, name="B2")

        def evac_engine(i):
            return "act" if i % 2 == 0 else "dve"

        def copy_any(i, out, in_, r, w):
            if i % 2 == 0:
                kb.act(out, in_, AF.Copy, r, w)
            else:
                kb.cp(out, in_, r, w)

        for l in range(DEPTH):
            for name, wb in WB.items():
                src = A[name]
                rows, cols = src.shape[1], src.shape[2]
                cw = _div_le(cols, 2048)
                for r0 in range(0, rows, 128):
                    s_ap = src[l, r0:r0 + 128, :].rearrange("p (a b) -> p a b", b=cw)
                    d_ap = wb[l, r0:r0 + 128, :].rearrange("p (a b) -> p a b", b=cw)
                    kb.dma("pool", d_ap, s_ap)
        kb.barrier()
        stop_flag = {'s': STOP == 'cast'}

        def norm_to_T(xt, hT, col0, Am, Bm, j, tmp):
            _, ssq, rstd, xn = tmp
            kb.act(xn[:], xt[:], AF.Square, [xt], [xn, ssq], accum_out=ssq[:, 0:1])
            kb.act(rstd[:], ssq[:, 0:1], AF.Sqrt, [ssq], [rstd], bias=EPS, scale=1.0 / D)
            kb.recip(rstd[:], rstd[:], [rstd], [rstd])
            kb.ts(xn[:], xt[:], rstd[:, 0:1], None, ALU.mult, None, [xt, rstd], [xn])
            for half in range(2):
                pb = PB[rr["pb"] % 2]
                rr["pb"] += 1
                for q in range(8):
                    kc = half * 8 + q
                    kb.tr(pb[:, q * 128:(q + 1) * 128], xn[:, kc * 128:(kc + 1) * 128], IDb, [xn, cpb], [pb])
                for q in range(8):
                    kc = half * 8 + q
                    kb.act(hT[:, kc, col0:col0 + 128], pb[:, q * 128:(q + 1) * 128], AF.Identity, [pb, Am, Bm], [hT],
                           bias=Bm[:, kc, j:j + 1], scale=Am[:, kc, j:j + 1])

        def layer(l):
            if stop_flag['s']:
                return
            x_in = A["x"] if l == 0 else XS
            with ExitStack() as ph:
                mT = kb.sb([128, 96, 2], F32, ph)
                ct = kb.sb([128, 16, 2], F32, ph)
                sc = kb.sb([128, 16, 2], BF16, ph)
                sB = [kb.sb([128, 16, 128], BF16, ph) for _ in range(2)]
                bmT = kb.sb([128, 96], F32, ph)
                kb.dma("sync", ct[:], A["condT"][:, :, :], w=[ct])
                kb.dma("sync", bmT[:], A["b_modT"][l], w=[bmT])
                ctf = kb.sb([128, 16, 2], F32, ph)
                kb.act(ctf[:], ct[:], AF.Silu, [ct], [ctf])
                kb.cp(sc[:], ctf[:], [ctf], [sc])
                for j in range(2):
                    for kc in range(16):
                        kb.cp(sB[j][:, kc, :], ctf[:, kc, j:j + 1].to_broadcast([128, 128]), [ctf], [sB[j]])
                wm = [kb.sb([128, 16, 512], BF16, ph) for _ in range(2)]
                bb = [kb.sb([128, 512], F32, ph) for _ in range(2)]
                gst = [kb.sb([128, 512], F32, ph) for _ in range(2)]
                wsrc = A["w_mod"][l].rearrange("(kc p) c -> p kc c", p=128)
                gi = 0
                for cb in range(24):
                    w_ = wm[cb % 2]
                    kb.dma("pool", w_[:], wsrc[:, :, cb * 512:(cb + 1) * 512], w=[w_])
                    for sub in range(4):
                        ch = cb * 4 + sub
                        pp = P[ch % 2]
                        kb.mm(pp[:, 0:2], [(w_[:, kc, sub * 128:(sub + 1) * 128], sc[:, kc, :]) for kc in range(16)], [w_, sc], [pp])
                        kb.ts(mT[:, ch, :], pp[:, 0:2], bmT[:, ch:ch + 1], None, ALU.add, None, [pp, bmT], [mT])
                    gsel = {8: 0, 9: 0, 10: 0, 11: 0, 20: 1, 21: 1, 22: 1, 23: 1}.get(cb)
                    if gsel is not None:
                        b_ = bb[cb % 2]
                        kb.dma("sync", b_[:], A["b_mod"][l, cb * 512:(cb + 1) * 512].partition_broadcast(128), w=[b_])
                        for j in range(2):
                            pp = P[2 + j]
                            g_ = gst[gi % 2]
                            gi += 1
                            kb.mm(pp[:], [(sB[j][:, kc, :], w_[:, kc, :]) for kc in range(16)], [w_, sB[j]], [pp])
                            kb.tt(g_[:], pp[:], b_[:], ALU.add, [pp, b_], [g_])
                            c0 = (cb % 4) * 512
                            kb.dma("pool", GSC[j, gsel, :, c0:c0 + 512], g_[:], r=[g_])
                g1 = kb.sb([128, 16], F32, ph)
                g2 = kb.sb([128, 16], F32, ph)
                kb.dma("sync", g1[:], A["n1T"][l], w=[g1])
                kb.dma("sync", g2[:], A["n2T"][l], w=[g2])
                for j in range(2):
                    kb.stt(A1[:, :, j], mT[:, 16:32, j], 1.0, g1[:], ALU.add, ALU.mult, [mT, g1], [A1])
                    kb.stt(A2[:, :, j], mT[:, 64:80, j], 1.0, g2[:], ALU.add, ALU.mult, [mT, g2], [A2])
                    kb.cp(B1[:, :, j], mT[:, 0:16, j], [mT], [B1])
                    kb.cp(B2[:, :, j], mT[:, 48:64, j], [mT], [B2])
                kb.barrier()

            if STOP == "M" + str(l):
                stop_flag['s'] = True
                return
            with ExitStack() as ph:
                G = 1024
                hT = kb.sb([128, 16, G], BF16, ph)
                xt2 = [kb.sb([128, D], F32, ph) for _ in range(2)]
                tmp = (None, kb.sb([128, 1], F32, ph), kb.sb([128, 1], F32, ph), kb.sb([128, D], BF16, ph))
                wbuf = [kb.sb([128, 16, 512], BF16, ph) for _ in range(3)]
                stg = [kb.sb([128, 512], F32, ph) for _ in range(4)]
                wga = kb.sb([128, 16, 32], BF16, ph)
                gst_ = [kb.sb([16, 512], F32, ph) for _ in range(2)]
                wsrc = WB["w_in"][l].rearrange("(kc p) c -> p kc c", p=128)
                kb.dma("sync", wga[:], wsrc[:, :, 3072:3104], w=[wga])
                si = 0
                wi = 0
                for g in range(NTOK // G):
                    j = 0 if g < 4 else 1
                    for ti in range(8):
                        tile_i = g * 8 + ti
                        xt = xt2[tile_i % 2]
                        kb.dma("sync", xt[:], x_in[tile_i * 128:(tile_i + 1) * 128, :], w=[xt])
                        norm_to_T(xt, hT, ti * 128, A1, B1, j, tmp)
                    kb.dma("pool", H1T[:, :, g * G:(g + 1) * G].rearrange("kc p t -> p kc t"), hT[:], r=[hT])
                    for blk in range(18):
                        wcol = blk * 512 if blk < 6 else blk * 512 + 32
                        w_ = wbuf[wi % 3]
                        wi += 1
                        kb.dma("sync", w_[:], wsrc[:, :, wcol:wcol + 512], w=[w_])
                        for ti in range(8):
                            tile_i = g * 8 + ti
                            pp = P[si % 4]
                            s_ = stg[si % 4]
                            kb.mm(pp[:], [(hT[:, kc, ti * 128:(ti + 1) * 128], w_[:, kc, :]) for kc in range(16)], [hT, w_], [pp])
                            copy_any(si, s_[:], pp[:], [pp], [s_])
                            si += 1
                            kb.dma("pool", SC[tile_i * 128:(tile_i + 1) * 128, blk * 512:(blk + 1) * 512], s_[:], r=[s_])
                            if j == 1 and blk in (10, 11):
                                pi = (tile_i - 32) // 2
                                s0 = ((tile_i - 32) % 2) * 128
                                for hh in range(4):
                                    h = (blk - 10) * 4 + hh
                                    kb.dma("pool", OA["nv"][pi, l, h, s0:s0 + 128, :], s_[:, hh * 128:(hh + 1) * 128], r=[s_])
                    for dr in range(2):
                        for tq in range(G // 512):
                            pp = P[4 + (tq % 2)]
                            g_ = gst_[tq % 2]
                            kb.mm(pp[0:16, :], [(wga[:, kc, dr * 16:(dr + 1) * 16], hT[:, kc, tq * 512:(tq + 1) * 512]) for kc in range(16)],
                                  [hT, wga], [pp])
                            kb.cp(g_[:], pp[0:16, :], [pp], [g_])
                            kb.dma("pool", SGA[dr, :, g * G + tq * 512: g * G + (tq + 1) * 512], g_[:], r=[g_])
                kb.barrier()

            if STOP == "A" + str(l):
                stop_flag['s'] = True
                return
            for nm, fn in (("gla", gla_phase), ("na", na_phase), ("hy", hyena_phase), ("C", lambda l_: phase_c(l_, A2, B2))):
                if stop_flag['s']:
                    return
                fn(l)
                if STOP == nm + str(l):
                    stop_flag['s'] = True

        def gla_phase(l):
            with ExitStack() as ph:
                wa = kb.sb([17, 2, 512], F32, ph)
                kb.dma("sync", wa[0:16, :, :], A["wa2"][l].rearrange("d r c -> r d c"), w=[wa])
                kb.dma("sync", wa[16:17, :, :], A["ba2"][l:l + 1, :, :], w=[wa])
                gn = kb.sb([128, 1024], F32, ph)
                kb.dma("sync", gn[:], A["gng4"][l].partition_broadcast(128), w=[gn])
                S = [kb.sb([128, 256], F32, ph) for _ in range(4)]
                Sb = [kb.sb([128, 256], BF16, ph) for _ in range(4)]
                NB = 2
                qf = [kb.sb([128, 512], F32, ph) for _ in range(NB)]
                kf = [kb.sb([128, 512], F32, ph) for _ in range(NB)]
                vf = [kb.sb([128, 1024], F32, ph) for _ in range(NB)]
                vb = [kb.sb([128, 1024], BF16, ph) for _ in range(NB)]
                gaT = [kb.sb([17, 128], F32, ph) for _ in range(NB)]
                for g_ in gaT:
                    kb.memset(g_[:], 1.0, [g_])
                cs4 = [kb.sb([128, 256], F32, ph) for _ in range(NB)]
                sn4 = [kb.sb([128, 256], F32, ph) for _ in range(NB)]
                e1 = kb.sb([128, 512], F32, ph)
                sp = kb.sb([128, 512], F32, ph)
                tots = kb.sb([128, 512], F32, ph)
                dd = kb.sb([128, 512], F32, ph)
                EQ = kb.sb([128, 512], F32, ph)
                EK = kb.sb([128, 512], F32, ph)
                EH = kb.sb([128, 512], F32, ph)
                dec = kb.sb([128, 4], F32, ph)
                qr = kb.sb([128, 512], F32, ph)
                kr = kb.sb([128, 512], F32, ph)
                t1 = kb.sb([128, 256], F32, ph)
                t2 = kb.sb([128, 256], F32, ph)
                qt = kb.sb([128, 512], BF16, ph)
                kt = kb.sb([128, 512], BF16, ph)
                kh = kb.sb([128, 512], BF16, ph)
                qT = kb.sb([128, 512], BF16, ph)
                kT = kb.sb([128, 512], BF16, ph)
                AT = [kb.sb([128, 128], BF16, ph) for _ in range(2)]
                ost = kb.sb([128, 1024], F32, ph)
                ofl = kb.sb([128, 1024], F32, ph)
                gg = kb.sb([128, 1024], F32, ph)
                gs = kb.sb([128, 1024], F32, ph)
                junk = kb.sb([128, 256], F32, ph)
                ss4 = kb.sb([128, 4], F32, ph)
                rs4 = kb.sb([128, 4], F32, ph)
                oa = kb.sb([128, 1024], BF16, ph)
                oaT = kb.sb([128, 8, 128], BF16, ph)

                def rope(dst, src, c_, s_):
                    s3 = src[:].rearrange("p (h d) -> p h d", h=4)
                    d3 = dst[:].rearrange("p (h d) -> p h d", h=4)
                    c3 = c_[:].rearrange("p (h d) -> p h d", h=4)
                    n3 = s_[:].rearrange("p (h d) -> p h d", h=4)
                    a3 = t1[:].rearrange("p (h d) -> p h d", h=4)
                    b3 = t2[:].rearrange("p (h d) -> p h d", h=4)
                    kb.tt(a3, s3[:, :, 0:64], c3, ALU.mult, [src, c_], [t1])
                    kb.tt(b3, s3[:, :, 64:128], n3, ALU.mult, [src, s_], [t2], eng="pool")
                    kb.tt(d3[:, :, 0:64], a3, b3, ALU.subtract, [t1, t2], [dst])
                    kb.tt(a3, s3[:, :, 0:64], n3, ALU.mult, [src, s_], [t1])
                    kb.tt(b3, s3[:, :, 64:128], c3, ALU.mult, [src, c_], [t2], eng="pool")
                    kb.tt(d3[:, :, 64:128], a3, b3, ALU.add, [t1, t2], [dst])

                seqs = [(0, 32, True, None)] + [(32 + 2 * p, 2, False, p) for p in range(4)]
                if os.environ.get('KSEQ') == 'p':
                    seqs = seqs[1:]
                cnt = 0
                for (tile0, ntile, latent, pidx) in seqs:
                    for dr in range(2):
                        for h in range(4):
                            if latent:
                                kb.dma("sync", S[h][:], A["state0"][l, dr, h], w=[S[h]])
                            else:
                                kb.memset(S[h][:], 0.0, [S[h]])
                            kb.cp(Sb[h][:], S[h][:], [S[h]], [Sb[h]], eng="pool")
                        order = range(ntile) if dr == 0 else range(ntile - 1, -1, -1)
                        for ti in order:
                            tile_i = tile0 + ti
                            r0 = tile_i * 128
                            bi = cnt % NB
                            cnt += 1
                            q_, k_, v_, vb_, ga_ = qf[bi], kf[bi], vf[bi], vb[bi], gaT[bi]
                            kb.dma("sync", q_[:], SC[r0:r0 + 128, 0:512], w=[q_])
                            kb.dma("sync", k_[:], SC[r0:r0 + 128, 512:1024], w=[k_])
                            kb.dma("sync", v_[:], SC[r0:r0 + 128, 1024:2048], w=[v_])
                            kb.dma("sync", ga_[0:16, :], SGA[dr, :, r0:r0 + 128], w=[ga_])
                            kb.act(vb_[:], v_[:], AF.Copy, [v_], [vb_])
                            kb.mm(P[0][:], [(ga_[:], wa[:, dr, :])], [ga_, wa], [P[0]])
                            kb.act(e1[:], P[0][:], AF.Exp, [P[0]], [e1], scale=-1.0)
                            kb.act(sp[:], e1[:], AF.Ln, [e1], [sp], bias=1.0)
                            if KGLA < 2:
                                continue
                            kb.mm(P[1][:], [(TRIf[dr], sp[:])], [sp, cp32], [P[1]])
                            kb.mm(P[2][:], [(ONESf, sp[:])], [sp, cp32], [P[2]])
                            for h in range(4):
                                kb.mm(P[0][:, 2 * h:2 * h + 2], [(sp[:, h * 128:(h + 1) * 128], ONESf[:, 0:2])], [sp, cp32], [P[0]])
                            kb.act(dec[:], P[0][:, 0:8:2], AF.Exp, [P[0]], [dec], scale=-1.0 / 16)
                            kb.act(EQ[:], P[1][:], AF.Exp, [P[1]], [EQ], scale=-1.0 / 16)
                            kb.act(EK[:], P[1][:], AF.Exp, [P[1]], [EK], scale=1.0 / 16)
                            kb.act(tots[:], P[2][:], AF.Copy, [P[2]], [tots])
                            kb.tt(dd[:], P[1][:], tots[:], ALU.subtract, [P[1], tots], [dd])
                            kb.act(EH[:], dd[:], AF.Exp, [dd], [EH], scale=1.0 / 16)
                            if KGLA < 3:
                                continue
                            if latent:
                                c_, s_ = cs4[bi], sn4[bi]
                                kb.dma("sync", c_[:], A["cos4"][r0:r0 + 128, :], w=[c_])
                                kb.dma("sync", s_[:], A["sin4"][r0:r0 + 128, :], w=[s_])
                                rope(qr, q_, c_, s_)
                                rope(kr, k_, c_, s_)
                                qs, ks = qr, kr
                            else:
                                qs, ks = q_, k_
                            kb.stt(qt[:], qs[:], 128.0 ** -0.5, EQ[:], ALU.mult, ALU.mult, [qs, EQ], [qt])
                            kb.tt(kt[:], ks[:], EK[:], ALU.mult, [ks, EK], [kt])
                            kb.tt(kh[:], ks[:], EH[:], ALU.mult, [ks, EH], [kh], eng="pool")
                            if KGLA == 30:
                                continue
                            pb = PB[0]
                            for h in range(4):
                                kb.tr(pb[:, h * 128:(h + 1) * 128], qt[:, h * 128:(h + 1) * 128], IDb, [qt, cpb], [pb])
                                kb.tr(pb[:, 512 + h * 128:512 + (h + 1) * 128], kt[:, h * 128:(h + 1) * 128], IDb, [kt, cpb], [pb])
                            if KGLA == 31:
                                continue
                            kb.act(qT[:], pb[:, 0:512], AF.Identity, [pb], [qT])
                            kb.act(kT[:], pb[:, 512:1024], AF.Identity, [pb], [kT])
                            if KGLA < 4 or KGLA >= 30:
                                continue
                            for h in range(4):
                                hs = slice(h * 128, (h + 1) * 128)
                                at = AT[h % 2]
                                kb.mm(P[3][:, 0:128], [(kT[:, hs], qT[:, hs])], [kT, qT], [P[3]])
                                kb.tt(at[:], P[3][:, 0:128], TRIf[dr], ALU.mult, [P[3], cp32], [at])
                                po = P[4 + h // 2]
                                oc = (h % 2) * 256
                                kb.mm(po[:, oc:oc + 256], [(at[:], vb_[:, h * 256:(h + 1) * 256]), (qT[:, hs], Sb[h][:])],
                                      [at, vb_, qT, Sb[h]], [po])
                                kb.mm(P[3][:, 256:512], [(kh[:, hs], vb_[:, h * 256:(h + 1) * 256])], [kh, vb_], [P[3]])
                                kb.stt(S[h][:], S[h][:], dec[:, h:h + 1], P[3][:, 256:512], ALU.mult, ALU.add, [S[h], dec, P[3]], [S[h]])
                                kb.act(Sb[h][:], S[h][:], AF.Copy, [S[h]], [Sb[h]])
                            if KGLA < 5:
                                continue
                            if dr == 0:
                                kb.act(ost[:, 0:512], P[4][:], AF.Copy, [P[4]], [ost])
                                kb.cp(ost[:, 512:1024], P[5][:], [P[5]], [ost])
                                kb.dma("pool", OF[r0:r0 + 128, :], ost[:], r=[ost])
                            else:
                                kb.dma("sync", ofl[:], OF[r0:r0 + 128, :], w=[ofl])
                                kb.dma("sync", gg[:], SC[r0:r0 + 128, 2048:3072], w=[gg])
                                kb.tt(ost[:, 0:512], P[4][:], ofl[:, 0:512], ALU.add, [P[4], ofl], [ost])
                                kb.tt(ost[:, 512:1024], P[5][:], ofl[:, 512:1024], ALU.add, [P[5], ofl], [ost])
                                for h in range(4):
                                    kb.act(junk[:], ost[:, h * 256:(h + 1) * 256], AF.Square, [ost], [junk, ss4], accum_out=ss4[:, h:h + 1])
                                kb.act(rs4[:], ss4[:], AF.Sqrt, [ss4], [rs4], bias=EPS, scale=1.0 / 256)
                                kb.recip(rs4[:], rs4[:], [rs4], [rs4])
                                kb.act(gs[:], gg[:], AF.Silu, [gg], [gs])
                                kb.tt(gs[:], gs[:], gn[:], ALU.mult, [gs, gn], [gs], eng="pool")
                                for h in range(4):
                                    hs2 = slice(h * 256, (h + 1) * 256)
                                    kb.stt(oa[:, hs2], ost[:, hs2], rs4[:, h:h + 1], gs[:, hs2], ALU.mult, ALU.mult, [ost, rs4, gs], [oa])
                                pb = PB[1]
                                for q in range(8):
                                    kb.tr(pb[:, q * 128:(q + 1) * 128], oa[:, q * 128:(q + 1) * 128], IDb, [oa, cpb], [pb])
                                kb.act(oaT[:].rearrange("p a b -> p (a b)"), pb[:], AF.Identity, [pb], [oaT])
                                kb.dma("pool", OAT[:, :, r0:r0 + 128].rearrange("kc p t -> p kc t"), oaT[:], r=[oaT])
                        if not latent:
                            for h in range(4):
                                kb.dma("pool", OA["nst"][pidx, l, dr, h], S[h][:], r=[S[h]])
                        kb.barrier()

        def na_phase(l):
            with ExitStack() as ph:
                qg = kb.sb([128, 1024], F32, ph)
                kg = kb.sb([128, 1024], F32, ph)
                kb.dma("sync", qg[:], A["qg8"][l].partition_broadcast(128), w=[qg])
                kb.dma("sync", kg[:], A["kg8"][l].partition_broadcast(128), w=[kg])
                xin = [[kb.sb([128, 1024], F32, ph) for _ in range(2)] for _ in range(2)]
                sq = kb.sb([128, 1024], F32, ph)
                ss8 = kb.sb([128, 8], F32, ph)
                rs8 = kb.sb([128, 8], F32, ph)
                nf = [kb.sb([128, 1024], F32, ph) for _ in range(2)]
                nb = kb.sb([128, 1024], BF16, ph)
                nT = [kb.sb([128, 8, 128], BF16, ph) for _ in range(2)]
                k = 0
                for tile_i in range(NT):
                    r0 = tile_i * 128
                    for which in range(2):
                        xi = xin[which][tile_i % 2]
                        c0 = 3072 + which * 1024
                        kb.dma("sync", xi[:], SC[r0:r0 + 128, c0:c0 + 1024], w=[xi])
                        kb.act(sq[:], xi[:], AF.Square, [xi], [sq])
                        kb.issue("dve", lambda e: e.tensor_reduce(out=ss8[:], in_=sq[:].rearrange("p (h d) -> p h d", h=8),
                                                                  axis=AX.X, op=ALU.add), [sq], [ss8])
                        kb.act(rs8[:], ss8[:], AF.Sqrt, [ss8], [rs8], bias=EPS, scale=1.0 / 128)
                        kb.recip(rs8[:], rs8[:], [rs8], [rs8])
                        n_ = nf[k % 2]
                        gsel = qg if which == 0 else kg
                        for h in range(8):
                            hs = slice(h * 128, (h + 1) * 128)
                            kb.stt(n_[:, hs], xi[:, hs], rs8[:, h:h + 1], gsel[:, hs], ALU.mult, ALU.mult, [xi, rs8, gsel], [n_])
                        if which == 1 and tile_i >= 32:
                            pi = (tile_i - 32) // 2
                            s0 = ((tile_i - 32) % 2) * 128
                            kb.dma("pool", OA["nk"][pi, l, :, s0:s0 + 128, :].rearrange("h s d -> s h d"),
                                   n_[:].rearrange("p (h d) -> p h d", h=8), r=[n_])
                        if which == 0:
                            kb.act(nb[:], n_[:], AF.Copy, [n_], [nb], scale=128.0 ** -0.5)
                        else:
                            kb.act(nb[:], n_[:], AF.Copy, [n_], [nb])
                        pb = PB[k % 2]
                        t_ = nT[k % 2]
                        for h in range(8):
                            kb.tr(pb[:, h * 128:(h + 1) * 128], nb[:, h * 128:(h + 1) * 128], IDb, [nb, cpb], [pb])
                        kb.act(t_[:].rearrange("p a b -> p (a b)"), pb[:], AF.Identity, [pb], [t_])
                        dst = QNT if which == 0 else KNT
                        kb.dma("pool", dst[:, :, r0:r0 + 128].rearrange("h p t -> p h t"), t_[:], r=[t_])
                        k += 1
                kb.barrier()
            with ExitStack() as ph:
                BT = kb.sb([128, 14, 64], F32, ph)
                BTr = kb.sb([128, 14, 64], F32, ph)
                cm = kb.sb([128, 64], F32, ph)
                kb.dma("sync", cm[:], A["colmask"][:, :], w=[cm])
                qTh = [kb.sb([128, NTOK], BF16, ph) for _ in range(2)]
                kTh = [kb.sb([128, NTOK], BF16, ph) for _ in range(2)]
                vh = [kb.sb([128, NT, 128], BF16, ph) for _ in range(2)]
                kcf = kb.sb([128, 2, 128], BF16, ph)
                kcT = [kb.sb([128, 256], BF16, ph) for _ in range(2)]
                vc = [kb.sb([128, 2, 128], BF16, ph) for _ in range(2)]
                sl = [kb.sb([128, 4, 64], F32, ph) for _ in range(2)]
                El = [kb.sb([128, 4, 64], BF16, ph) for _ in range(2)]
                Ec = [kb.sb([128, 2, 64], BF16, ph) for _ in range(2)]
                Ed = [kb.sb([128, 2, 256], BF16, ph) for _ in range(2)]
                den = [kb.sb([128, 256], F32, ph) for _ in range(2)]
                obT = [kb.sb([128, NTOK], BF16, ph) for _ in range(2)]
                it = 0
                for h in range(8):
                    hb = h % 2
                    q_, k_, v_, kc_, vc_, ob_ = qTh[hb], kTh[hb], vh[hb], kcT[hb], vc[hb], obT[hb]
                    kb.dma("sync", q_[:], QNT[h], w=[q_])
                    kb.dma("sync", k_[:], KNT[h], w=[k_])
                    kb.dma("pool", v_[:], SC[:, 5120 + h * 128:5120 + (h + 1) * 128].rearrange("(t p) d -> p t d", p=128), w=[v_])
                    kb.dma("pool", kcf[:], A["ck"][l, h].rearrange("(t p) d -> p t d", p=128), w=[kcf])
                    kb.dma("pool", vc_[:], A["cv"][l, h].rearrange("(t p) d -> p t d", p=128), w=[vc_])
                    pb = PB[h % 2]
                    for t_ in range(2):
                        kb.tr(pb[:, t_ * 128:(t_ + 1) * 128], kcf[:, t_, :], IDb, [kcf, cpb], [pb])
                    kb.act(kc_[:], pb[:, 0:256], AF.Identity, [pb], [kc_])
                    for d0 in range(14):
                        for di in range(2):
                            src = bass.AP(tensor=I["rpbp"], offset=((l * 8 + h) * 15 + d0 + di) * 127, ap=[[1, 64], [1, 64]])
                            kb.dma("sync", BTr[di * 64:(di + 1) * 64, d0, :], src, w=[BTr])
                    for hf in range(2):
                        pj = P[hf]
                        kb.mm(pj[:, 0:448], [(JREV, BTr[:, hf * 7:(hf + 1) * 7, :].rearrange("p a b -> p (a b)"))], [cp32, BTr], [pj])
                        kb.tt(BT[:, hf * 7:(hf + 1) * 7, :], pj[:, 0:448].rearrange("p (a b) -> p a b", a=7),
                              cm[:].unsqueeze(1).to_broadcast([128, 7, 64]), ALU.add, [pj, cm], [BT])
                    for r in range(64):
                        rs_ = min(max(r - 4, 0), 56)
                        d0 = rs_ - r + 7
                        qc = slice(r * 64, (r + 1) * 64)
                        i2 = it % 2
                        it += 1
                        pl, pc = P[0 + i2], P[2 + i2]
                        for j in range(4):
                            kc0 = (rs_ + 2 * j) * 64
                            kb.mm(pl[:, j * 64:(j + 1) * 64], [(k_[:, kc0:kc0 + 128], q_[:, qc])], [k_, q_], [pl])
                        for j in range(2):
                            kb.mm(pc[:, j * 64:(j + 1) * 64], [(kc_[:, j * 128:(j + 1) * 128], q_[:, qc])], [kc_, q_], [pc])
                        kb.tt(sl[i2][:], pl[:, 0:256].rearrange("p (j c) -> p j c", j=4), BT[:, d0:d0 + 7:2, :], ALU.add, [pl, BT], [sl[i2]])
                        kb.act(El[i2][:], sl[i2][:], AF.Exp, [sl[i2]], [El[i2]])
                        kb.act(Ec[i2][:].rearrange("p j c -> p (j c)"), pc[:, 0:128], AF.Exp, [pc], [Ec[i2]])
                        po, pd = P[4], P[5]
                        oc = i2 * 64
                        prs = []
                        prd = []
                        for j in range(4):
                            kt_i = (rs_ + 2 * j) // 2
                            prs.append((v_[:, kt_i, :], El[i2][:, j, :]))
                            prd.append((ONESb, El[i2][:, j, :]))
                        for j in range(2):
                            prs.append((vc_[:, j, :], Ec[i2][:, j, :]))
                            prd.append((ONESb, Ec[i2][:, j, :]))
                        kb.mm(po[:, oc:oc + 64], prs, [v_, vc_, El[i2], Ec[i2]], [po])
                        kb.mm(pd[:, oc:oc + 64], prd, [cpb, El[i2], Ec[i2]], [pd])
                        kb.recip(den[i2][:, 0:64], pd[:, oc:oc + 64], [pd], [den[i2]])
                        kb.tt(ob_[:, qc], po[:, oc:oc + 64], den[i2][:, 0:64], ALU.mult, [po, den[i2]], [ob_])
                    for p in range(4):
                        t0 = LS + p * 256
                        i2 = it % 2
                        it += 1
                        pl = P[0 + i2]
                        for j in range(2):
                            kb.mm(pl[:, j * 256:(j + 1) * 256], [(k_[:, t0 + j * 128:t0 + (j + 1) * 128], q_[:, t0:t0 + 256])], [k_, q_], [pl])
                        kb.act(Ed[i2][:].rearrange("p j c -> p (j c)"), pl[:], AF.Exp, [pl], [Ed[i2]])
                        po, pd = P[2 + i2], P[4 + i2]
                        kb.mm(po[:, 0:256], [(v_[:, 32 + 2 * p + j, :], Ed[i2][:, j, :]) for j in range(2)], [v_, Ed[i2]], [po])
                        kb.mm(pd[:, 0:256], [(ONESb, Ed[i2][:, j, :]) for j in range(2)], [cpb, Ed[i2]], [pd])
                        kb.recip(den[i2][:], pd[:, 0:256], [pd], [den[i2]])
                        kb.tt(ob_[:, t0:t0 + 256], po[:, 0:256], den[i2][:], ALU.mult, [po, den[i2]], [ob_])
                    kb.dma("pool", OBT[h], ob_[:], r=[ob_])
                kb.barrier()

        def hyena_phase(l):
            TWO_PI = 2.0 * math.pi
            def fwd_dft(ph, zb, n_sc, FC, FS, ftab, consume):
                for ft in range(n_sc):
                    fc_, fs_ = ftab[0][ft % 2], ftab[1][ft % 2]
                    kb.dma("sync", fc_[:, 0:n_sc * 128], FC[ft], w=[fc_])
                    kb.dma("sync", fs_[:, 0:n_sc * 128], FS[ft], w=[fs_])
                    pre, pim = P[(ft % 2) * 2], P[(ft % 2) * 2 + 1]
                    kb.mm(pre[:], [(fc_[:, s * 128:(s + 1) * 128], zb[:, s, :]) for s in range(n_sc)], [fc_, zb], [pre])
                    kb.mm(pim[:], [(fs_[:, s * 128:(s + 1) * 128], zb[:, s, :]) for s in range(n_sc)], [fs_, zb], [pim])
                    consume(ft, pre, pim)

            for (L, zpos, ndist) in ((LS, A["zposL"], A["ndistL"]), (LP, A["zposS"], A["ndistS"])):
                n_tt = L // 128
                FC, FS = (A["FCL"], A["FSL"]) if L == LS else (A["FCS"], A["FSS"])
                with ExitStack() as ph:
                    zp = kb.sb([33, L], F32, ph)
                    kb.dma("sync", zp[:], zpos[:, :], w=[zp])
                    f1w = kb.sb([33, 64], F32, ph)
                    f2w = kb.sb([64, 64], F32, ph)
                    f3w = kb.sb([64, 2048], F32, ph)
                    fb = kb.sb([64, 4], F32, ph)
                    kb.dma("sync", f1w[:], A["f1w"][l], w=[f1w])
                    kb.dma("sync", f2w[:], A["f2w"][l], w=[f2w])
                    kb.dma("sync", f3w[:], A["f3w"][l], w=[f3w])
                    kb.dma("sync", fb[:, 0:1], A["f1b"][l], w=[fb])
                    kb.dma("sync", fb[:, 1:2], A["f2b"][l], w=[fb])
                    kb.dma("sync", fb[:, 2:3], A["hfr"][l], w=[fb])
                    sc_ = kb.sb([64, 4], F32, ph)
                    kb.ts(sc_[:, 0:1], fb[:, 2:3], 1.0 / TWO_PI, None, ALU.mult, None, [fb], [sc_])
                    for q in range(2):
                        kb.ts(sc_[:, 1 + q:2 + q], fb[:, q:q + 1], sc_[:, 0:1], 8.0, ALU.mult, ALU.add, [fb, sc_], [sc_])
                    h2T = kb.sb([64, L], F32, ph)
                    h1c = kb.sb([64, 512], F32, ph)
                    u = kb.sb([64, 512], F32, ph)
                    ki = kb.sb([64, 512], I32, ph)
                    kf_ = kb.sb([64, 512], F32, ph)
                    gq_ = kb.sb([64, 512], F32, ph)

                    def sine_layer(out_ap, ps_ap, q, CW):
                        kb.ts(u[:, 0:CW], ps_ap, sc_[:, 0:1], sc_[:, 1 + q:2 + q], ALU.mult, ALU.add, [P[0], sc_], [u])
                        kb.cp(ki[:, 0:CW], u[:, 0:CW], [u], [ki])
                        kb.cp(kf_[:, 0:CW], ki[:, 0:CW], [ki], [kf_])
                        kb.tt(u[:, 0:CW], u[:, 0:CW], kf_[:, 0:CW], ALU.subtract, [u, kf_], [u])
                        kb.ts(gq_[:, 0:CW], u[:, 0:CW], 0.5, None, ALU.is_gt, None, [u], [gq_])
                        kb.tt(u[:, 0:CW], u[:, 0:CW], gq_[:, 0:CW], ALU.subtract, [u, gq_], [u])
                        kb.ts(gq_[:, 0:CW], u[:, 0:CW], -0.5, None, ALU.is_lt, None, [u], [gq_])
                        kb.tt(u[:, 0:CW], u[:, 0:CW], gq_[:, 0:CW], ALU.add, [u, gq_], [u])
                        kb.act(out_ap, u[:, 0:CW], AF.Sin, [u], [h1c, h2T], scale=TWO_PI)

                    CW = min(512, L)
                    for c in range(L // CW):
                        cs_ = slice(c * CW, (c + 1) * CW)
                        kb.mm(P[0][0:64, 0:CW], [(f1w[:], zp[:, cs_])], [f1w, zp], [P[0]])
                        sine_layer(h1c[:, 0:CW], P[0][0:64, 0:CW], 0, CW)
                        kb.mm(P[0][0:64, 0:CW], [(f2w[:], h1c[:, 0:CW])], [f2w, h1c], [P[0]])
                        sine_layer(h2T[:, cs_], P[0][0:64, 0:CW], 1, CW)
                    dl = kb.sb([128, 1024], F32, ph)
                    nd = kb.sb([128, n_tt], F32, ph)
                    kb.dma("sync", dl[:], A["deltas"].partition_broadcast(128), w=[dl])
                    kb.dma("sync", nd[:], ndist[:, :], w=[nd])
                    win = kb.sb([128, 1024], F32, ph)
                    hw = [kb.sb([128, 512], F32, ph) for _ in range(2)]
                    hsq = [kb.sb([128, 512], F32, ph) for _ in range(2)]
                    hbf = [kb.sb([128, 512], BF16, ph) for _ in range(2)]
                    rn = kb.sb([128, 2048], F32, ph)
                    ssb = kb.sb([128, 2048], F32, ph)
                    kb.memset(ssb[:], 0.0, [ssb])
                    k = 0
                    for tt_ in range(n_tt):
                        kb.act(win[:], dl[:], AF.Exp, [dl, nd], [win], scale=nd[:, tt_:tt_ + 1])
                        for cb in range(4):
                            i2 = k % 2
                            k += 1
                            pp = P[i2]
                            kb.mm(pp[:], [(h2T[:, tt_ * 128:(tt_ + 1) * 128], f3w[:, cb * 512:(cb + 1) * 512])], [h2T, f3w], [pp])
                            kb.tt(hw[i2][:], pp[:], win[:, (cb % 2) * 512:(cb % 2 + 1) * 512], ALU.mult, [pp, win], [hw[i2]])
                            kb.act(hsq[i2][:], hw[i2][:], AF.Square, [hw[i2]], [hsq[i2]])
                            kb.act(hbf[i2][:], hw[i2][:], AF.Copy, [hw[i2]], [hbf[i2]])
                            kb.dma("pool", HB[L][tt_ * 128:(tt_ + 1) * 128, cb * 512:(cb + 1) * 512], hbf[i2][:], r=[hbf[i2]])
                            p2 = P[2 + i2]
                            kb.mm(p2[:], [(ONESf, hsq[i2][:])], [cp32, hsq[i2]], [p2])
                            kb.tt(ssb[:, cb * 512:(cb + 1) * 512], ssb[:, cb * 512:(cb + 1) * 512], p2[:], ALU.add, [ssb, p2], [ssb])
                    kb.act(rn[:], ssb[:], AF.Sqrt, [ssb], [rn], bias=EPS)
                    kb.recip(rn[:], rn[:], [rn], [rn])
                    kb.barrier()
                    zbt = [kb.sb([128, n_tt, 512], BF16, ph) for _ in range(1)]
                    ftab = ([kb.sb([128, 4096], BF16, ph) for _ in range(2)], [kb.sb([128, 4096], BF16, ph) for _ in range(2)])
                    hst = [kb.sb([128, 2, 512], F32, ph) for _ in range(2)]
                    kk = [0]
                    for o in range(2):
                        for chalf in range(2):
                            c0 = o * 1024 + chalf * 512
                            zb = zbt[0]
                            kb.dma("sync", zb[:], HB[L][:, c0:c0 + 512].rearrange("(s p) c -> p s c", p=128), w=[zb])

                            def consume(ft, pre, pim, o=o, chalf=chalf, c0=c0):
                                hs_ = hst[kk[0] % 2]
                                kk[0] += 1
                                kb.tt(hs_[:, 0, :], pre[:], rn[:, c0:c0 + 512], ALU.mult, [pre, rn], [hs_])
                                kb.tt(hs_[:, 1, :], pim[:], rn[:, c0:c0 + 512], ALU.mult, [pim, rn], [hs_])
                                kb.dma("pool", HF[L][o, ft, :, :, chalf * 512:(chalf + 1) * 512], hs_[:], r=[hs_])
                            fwd_dft(ph, zb, n_tt, FC, FS, ftab, consume)
                    kb.barrier()

            with ExitStack() as ph:
                cw = kb.sb([128, 3, 3072], F32, ph)
                cbias = kb.sb([128, 3072], F32, ph)
                for j in range(3):
                    kb.dma("sync", cw[:, j, :], A["hcw"][l, j].partition_broadcast(128), w=[cw])
                kb.dma("sync", cbias[:], A["hcb"][l].partition_broadcast(128), w=[cbias])
                um = [kb.sb([128, 3072], F32, ph) for _ in range(2)]
                uc = [kb.sb([128, 3072], F32, ph) for _ in range(2)]
                up = [kb.sb([128, 3072], F32, ph) for _ in range(2)]
                acc = [kb.sb([128, 3072], F32, ph) for _ in range(1)]
                tm = [kb.sb([128, 3072], F32, ph) for _ in range(1)]
                zbf = [kb.sb([128, 1024], BF16, ph) for _ in range(2)]
                for tile_i in range(NT):
                    i2 = tile_i % 2
                    r0 = tile_i * 128
                    if tile_i < 32:
                        s_lo, s_hi = 0, LS
                    else:
                        s_lo = LS + ((tile_i - 32) // 2) * 256
                        s_hi = s_lo + 256
                    a_, b_, c_ = um[i2], uc[i2], up[i2]
                    kb.dma("sync", b_[:], SC[r0:r0 + 128, 6144:9216], w=[b_])
                    if r0 == s_lo:
                        kb.memset(a_[0:1, :], 0.0, [a_])
                        kb.dma("sync", a_[1:128, :], SC[r0:r0 + 127, 6144:9216], w=[a_])
                    else:
                        kb.dma("sync", a_[:], SC[r0 - 1:r0 + 127, 6144:9216], w=[a_])
                    if r0 + 128 == s_hi:
                        kb.memset(c_[:], 0.0, [c_])
                        kb.dma("sync", c_[0:127, :], SC[r0 + 1:r0 + 128, 6144:9216], w=[c_])
                    else:
                        kb.dma("sync", c_[:], SC[r0 + 1:r0 + 129, 6144:9216], w=[c_])
                    ac, t_ = acc[0], tm[0]
                    kb.tt(ac[:], a_[:], cw[:, 0, :], ALU.mult, [a_, cw], [ac])
                    kb.tt(t_[:], b_[:], cw[:, 1, :], ALU.mult, [b_, cw], [t_], eng="pool")
                    kb.tt(ac[:], ac[:], cbias[:], ALU.add, [ac, cbias], [ac])
                    kb.tt(ac[:], ac[:], t_[:], ALU.add, [ac, t_], [ac])
                    kb.tt(t_[:], c_[:], cw[:, 2, :], ALU.mult, [c_, cw], [t_], eng="pool")
                    kb.tt(ac[:], ac[:], t_[:], ALU.add, [ac, t_], [ac])
                    kb.act(zbf[i2][:], ac[:, 2048:3072], AF.Copy, [ac], [zbf[i2]])
                    kb.dma("pool", HG[r0:r0 + 128, :], ac[:, 0:2048], r=[ac])
                    kb.dma("pool", ZF[0][r0:r0 + 128, :], ac[:, 2048:3072], r=[ac])
                    kb.dma("pool", ZB[0][r0:r0 + 128, :], zbf[i2][:], r=[zbf[i2]])
                kb.barrier()

            with ExitStack() as ph:
                hbs = kb.sb([128, 2, 1024], F32, ph)
                for o in range(2):
                    kb.dma("sync", hbs[:, o, :], A["hbias"][l, o].partition_broadcast(128), w=[hbs])
                zbt = [kb.sb([128, 32, 512], BF16, ph) for _ in range(1)]
                ftab = ([kb.sb([128, 4096], BF16, ph) for _ in range(2)], [kb.sb([128, 4096], BF16, ph) for _ in range(2)])
                Yf = kb.sb([128, 32, 2, 512], BF16, ph)
                hft = [kb.sb([128, 2, 512], F32, ph) for _ in range(2)]
                m1 = kb.sb([128, 512], F32, ph)
                m2 = kb.sb([128, 512], F32, ph)
                gt = [kb.sb([128, 512], F32, ph) for _ in range(2)]
                zo = [kb.sb([128, 512], F32, ph) for _ in range(2)]
                zn = [kb.sb([128, 512], F32, ph) for _ in range(2)]
                znb = [kb.sb([128, 512], BF16, ph) for _ in range(2)]
                ocT = [kb.sb([128, 4, 128], BF16, ph) for _ in range(2)]
                cc = [0, 0, 0]
                seqs = [(0, LS)] + [(LS + p * 256, LP) for p in range(4)]
                for o in range(2):
                    for (t0, L) in seqs:
                        n_sc = L // 128
                        FC, FS, GC, GS = (A["FCL"], A["FSL"], A["GCL"], A["GSL"]) if L == LS else (A["FCS"], A["FSS"], A["GCS"], A["GSS"])
                        for chalf in range(2):
                            c0 = chalf * 512
                            zb = zbt[0]
                            cc[0] += 1
                            kb.dma("sync", zb[:, 0:n_sc, :], ZB[o][t0:t0 + L, c0:c0 + 512].rearrange("(s p) c -> p s c", p=128), w=[zb])

                            def consume(ft, pre, pim, o=o, L=L, c0=c0):
                                hf_ = hft[cc[1] % 2]
                                cc[1] += 1
                                kb.dma("sync", hf_[:], HF[L][o, ft, :, :, c0:c0 + 512], w=[hf_])
                                kb.tt(m1[:], pre[:], hf_[:, 0, :], ALU.mult, [pre, hf_], [m1])
                                kb.tt(m2[:], pim[:], hf_[:, 1, :], ALU.mult, [pim, hf_], [m2])
                                kb.tt(Yf[:, ft, 0, :], m1[:], m2[:], ALU.subtract, [m1, m2], [Yf])
                                kb.tt(m1[:], pre[:], hf_[:, 1, :], ALU.mult, [pre, hf_], [m1])
                                kb.tt(m2[:], pim[:], hf_[:, 0, :], ALU.mult, [pim, hf_], [m2])
                                kb.tt(Yf[:, ft, 1, :], m1[:], m2[:], ALU.add, [m1, m2], [Yf])
                                if ft == 0:
                                    kb.tt(Yf[0:1, 0, 0, :], pre[0:1, :], hf_[0:1, 0, :], ALU.mult, [pre, hf_], [Yf])
                                    kb.tt(Yf[0:1, 0, 1, :], pim[0:1, :], hf_[0:1, 1, :], ALU.mult, [pim, hf_], [Yf])
                            fwd_dft(ph, zb, n_sc, FC, FS, ftab, consume)
                            for tt_ in range(n_sc):
                                i2 = cc[2] % 2
                                cc[2] += 1
                                r0 = t0 + tt_ * 128
                                gc_, gs_ = ftab[0][tt_ % 2], ftab[1][tt_ % 2]
                                kb.dma("sync", gc_[:, 0:n_sc * 128], GC[tt_], w=[gc_])
                                kb.dma("sync", gs_[:, 0:n_sc * 128], GS[tt_], w=[gs_])
                                kb.dma("sync", gt[i2][:], HG[r0:r0 + 128, o * 1024 + c0:o * 1024 + c0 + 512], w=[gt[i2]])
                                kb.dma("sync", zo[i2][:], ZF[o][r0:r0 + 128, c0:c0 + 512], w=[zo[i2]])
                                py = P[4 + i2]
                                prs = []
                                for f in range(n_sc):
                                    prs.append((gc_[:, f * 128:(f + 1) * 128], Yf[:, f, 0, :]))
                                    prs.append((gs_[:, f * 128:(f + 1) * 128], Yf[:, f, 1, :]))
                                kb.mm(py[:], prs, [gc_, gs_, Yf], [py])
                                kb.tt(zo[i2][:], zo[i2][:], hbs[:, o, c0:c0 + 512], ALU.mult, [zo[i2], hbs], [zo[i2]], eng="pool")
                                kb.tt(zn[i2][:], py[:], zo[i2][:], ALU.add, [py, zo[i2]], [zn[i2]])
                                if o == 0:
                                    kb.tt(zn[i2][:], zn[i2][:], gt[i2][:], ALU.mult, [zn[i2], gt[i2]], [zn[i2]])
                                    kb.act(znb[i2][:], zn[i2][:], AF.Copy, [zn[i2]], [znb[i2]])
                                    kb.dma("pool", ZF[1][r0:r0 + 128, c0:c0 + 512], zn[i2][:], r=[zn[i2]])
                                    kb.dma("pool", ZB[1][r0:r0 + 128, c0:c0 + 512], znb[i2][:], r=[znb[i2]])
                                else:
                                    kb.tt(znb[i2][:], zn[i2][:], gt[i2][:], ALU.mult, [zn[i2], gt[i2]], [znb[i2]])
                                    pb = PB[i2]
                                    for q in range(4):
                                        kb.tr(pb[:, q * 128:(q + 1) * 128], znb[i2][:, q * 128:(q + 1) * 128], IDb, [znb[i2], cpb], [pb])
                                    kb.act(ocT[i2][:].rearrange("p a b -> p (a b)"), pb[:, 0:512], AF.Identity, [pb], [ocT[i2]])
                                    kb.dma("pool", OCT[chalf * 4:(chalf + 1) * 4, :, r0:r0 + 128].rearrange("kc p t -> p kc t"), ocT[i2][:], r=[ocT[i2]])
                    kb.barrier()

        def phase_c(l, A2, B2):
            G = 512
            with ExitStack() as ph:
                arena = kb.sb([128, 32768], BF16, ph)
                h1T = arena[:, 0:8192].rearrange("p (k t) -> p k t", k=16)
                oT = [arena[:, 8192 + b * 4096: 8192 + (b + 1) * 4096].rearrange("p (k t) -> p k t", k=8) for b in range(3)]
                hidT = arena[:, :].rearrange("p (k t) -> p k t", k=64)
                mgT = kb.sb([128, 16, G], BF16, ph)
                h2T = kb.sb([128, 16, G], BF16, ph)
                gch = [kb.sb([128, 512], F32, ph) for _ in range(2)]
                xch = [kb.sb([128, 512], F32, ph) for _ in range(2)]
                xt = kb.sb([128, D], F32, ph)
                tmp = (None, kb.sb([128, 1], F32, ph), kb.sb([128, 1], F32, ph), kb.sb([128, D], BF16, ph))
                wg = [kb.sb([128, 16, 128], BF16, ph) for _ in range(3)]
                wbr = [kb.sb([128, 8, 128], BF16, ph) for _ in range(3)]
                wo = kb.sb([128, 16, 512], BF16, ph)
                w2 = [kb.sb([128, 512], BF16, ph) for _ in range(6)]
                sg = [kb.sb([128, G], F32, ph) for _ in range(2)]
                pr = [kb.sb([128, G], F32, ph) for _ in range(2)]
                macc = kb.sb([128, G], F32, ph)
                ysb = [kb.sb([128, 512], F32, ph) for _ in range(2)]
                hs1 = [kb.sb([128, G], F32, ph) for _ in range(2)]
                x1b = T(None)
                win_src = WB["w_in"][l].rearrange("(kc p) c -> p kc c", p=128)
                brs = [WB["w_br_a"][l].rearrange("(kc p) c -> p kc c", p=128), WB["w_br_b"][l].rearrange("(kc p) c -> p kc c", p=128),
                       WB["w_br_c"][l].rearrange("(kc p) c -> p kc c", p=128)]
                wo_src = WB["w_out"][l].rearrange("(kc p) c -> p kc c", p=128)
                w1_src = WB["w_mlp1"][l].rearrange("(kc p) c -> p kc c", p=128)
                w2_src = WB["w_mlp2"][l]
                srcT = [OAT, OBT, OCT]
                c = {"wg": 0, "w2": 0, "k": 0, "y": 0}
                x_in = A["x"] if l == 0 else XS
                for g in range(NTOK // G):
                    j = 0 if g < 8 else 1
                    tk = slice(g * G, (g + 1) * G)
                    kb.dma("sync", h1T, H1T[:, :, tk].rearrange("kc p t -> p kc t"), w=[arena])
                    for b in range(3):
                        kb.dma("sync", oT[b], srcT[b][:, :, tk].rearrange("kc p t -> p kc t"), w=[arena])
                    for mf in range(16):
                        for b in range(3):
                            wg_ = wg[c["wg"] % 3]
                            wb_ = wbr[c["wg"] % 3]
                            c["wg"] += 1
                            gcol = 9248 + b * 2048 + mf * 128
                            kb.dma("sync", wg_[:], win_src[:, :, gcol:gcol + 128], w=[wg_])
                            kb.dma("sync", wb_[:], brs[b][:, :, mf * 128:(mf + 1) * 128], w=[wb_])
                            i2 = c["k"] % 2
                            c["k"] += 1
                            pg, pp = P[i2], P[2 + i2]
                            kb.mm(pg[:], [(wg_[:, kc, :], h1T[:, kc, :]) for kc in range(16)], [wg_, arena], [pg])
                            kb.mm(pp[:], [(wb_[:, kc, :], oT[b][:, kc, :]) for kc in range(8)], [wb_, arena], [pp])
                            kb.act(sg[i2][:], pg[:], AF.Sigmoid, [pg], [sg[i2]])
                            if b == 0:
                                kb.tt(macc[:], pp[:], sg[i2][:], ALU.mult, [pp, sg[i2]], [macc])
                            else:
                                kb.tt(pr[i2][:], pp[:], sg[i2][:], ALU.mult, [pp, sg[i2]], [pr[i2]])
                                if b == 1:
                                    kb.tt(macc[:], macc[:], pr[i2][:], ALU.add, [macc, pr[i2]], [macc], eng="pool")
                                else:
                                    kb.tt(mgT[:, mf, :], macc[:], pr[i2][:], ALU.add, [macc, pr[i2]], [mgT], eng="pool")
                    for oc in range(4):
                        ocs = slice(oc * 512, (oc + 1) * 512)
                        kb.dma("sync", wo[:], wo_src[:, :, ocs], w=[wo])
                        for ti in range(4):
                            r0 = (g * 4 + ti) * 128
                            i2 = c["y"] % 2
                            c["y"] += 1
                            py, ys_, g_, x_ = P[4 + i2], ysb[i2], gch[i2], xch[i2]
                            kb.dma("sync", g_[:], GSC[j, 0, :, ocs], w=[g_])
                            kb.dma("sync", x_[:], x_in[r0:r0 + 128, ocs], w=[x_])
                            kb.mm(py[:], [(mgT[:, mf, ti * 128:(ti + 1) * 128], wo[:, mf, :]) for mf in range(16)], [mgT, wo], [py])
                            kb.tt(ys_[:], py[:], g_[:], ALU.mult, [py, g_], [ys_])
                            kb.tt(ys_[:], ys_[:], x_[:], ALU.add, [ys_, x_], [ys_], eng="pool")
                            kb.dma("pool", X1[r0:r0 + 128, ocs], ys_[:], r=[ys_], w=[x1b])
                    for ti in range(4):
                        r0 = (g * 4 + ti) * 128
                        kb.dma("sync", xt[:], X1[r0:r0 + 128, :], r=[x1b], w=[xt])
                        norm_to_T(xt, h2T, ti * 128, A2, B2, j, tmp)
                    for ff in range(64):
                        w1_ = wg[c["wg"] % 3]
                        c["wg"] += 1
                        kb.dma("sync", w1_[:], w1_src[:, :, ff * 128:(ff + 1) * 128], w=[w1_])
                        i2 = c["k"] % 2
                        c["k"] += 1
                        pg = P[i2]
                        kb.mm(pg[:], [(w1_[:, kc, :], h2T[:, kc, :]) for kc in range(16)], [w1_, h2T], [pg])
                        kb.act(hs1[i2][:], pg[:], AF.Relu, [pg], [hs1[i2]])
                        kb.tt(hidT[:, ff, :], hs1[i2][:], hs1[i2][:], ALU.mult, [hs1[i2]], [arena], eng=("dve" if ff % 2 == 0 else "pool"))
                    for oc in range(4):
                        ocs = slice(oc * 512, (oc + 1) * 512)
                        pys = [P[2], P[3], P[4], P[5]]
                        for ff in range(64):
                            w2_ = w2[c["w2"] % 6]
                            c["w2"] += 1
                            kb.dma("sync", w2_[:], w2_src[ff * 128:(ff + 1) * 128, ocs], w=[w2_])
                            for ti in range(4):
                                kb.mm(pys[ti][:], [(hidT[:, ff, ti * 128:(ti + 1) * 128], w2_[:])], [arena, w2_], [pys[ti]],
                                      start=(ff == 0), stop=(ff == 63))
                        for ti in range(4):
                            tile_i = g * 4 + ti
                            r0 = tile_i * 128
                            i2 = c["y"] % 2
                            c["y"] += 1
                            ys_, g_, x_ = ysb[i2], gch[i2], xch[i2]
                            kb.dma("sync", g_[:], GSC[j, 1, :, ocs], w=[g_])
                            kb.dma("sync", x_[:], X1[r0:r0 + 128, ocs], r=[x1b], w=[x_])
                            kb.tt(ys_[:], pys[ti][:], g_[:], ALU.mult, [pys[ti], g_], [ys_])
                            kb.tt(ys_[:], ys_[:], x_[:], ALU.add, [ys_, x_], [ys_], eng="pool")
                            if l == DEPTH - 1:
                                if tile_i < 32:
                                    dst = OA["ys"][r0:r0 + 128, ocs]
                                else:
                                    dst = OA["yp"][r0 - LS:r0 - LS + 128, ocs]
                            else:
                                dst = XS[r0:r0 + 128, ocs]
                            kb.dma("pool", dst, ys_[:], r=[ys_])
                kb.barrier()

        for l in range(DEPTH):
            layer(l)
        kb.finish()
        print("n_inst", kb.n_inst, {e: len(kb.ops[e]) for e in ENGS})
    return nc


def _constants():
    bf = ml_dtypes.bfloat16
    c = {}
    p = np.arange(128)
    ident = np.eye(128, dtype=np.float32)
    tri_le = (p[:, None] <= p[None, :]).astype(np.float32)
    tri_ge = (p[:, None] >= p[None, :]).astype(np.float32)
    jrev = ((p[:, None] // 64 == p[None, :] // 64) & (p[:, None] % 64 == 63 - p[None, :] % 64)).astype(np.float32)
    c["cpack"] = np.concatenate([ident, tri_le, tri_ge, np.ones((128, 128), np.float32), jrev], axis=1)
    pos = np.arange(LS)
    row = (pos // 64).astype(np.float32)
    col = (pos % 64).astype(np.float32)
    inv = (np.float32(10000.0) ** (-np.arange(32, dtype=np.float32) / np.float32(32))).astype(np.float32)
    ang = np.concatenate([row[:, None] * inv, col[:, None] * inv], axis=-1).astype(np.float32)
    c["cos4"] = np.tile(np.cos(ang).astype(np.float32), (1, 4))
    c["sin4"] = np.tile(np.sin(ang).astype(np.float32), (1, 4))
    cc = np.arange(64)
    cstart = np.clip(cc - 8, 0, 48)
    col_in = (cc[None, :] >= cstart[:, None]) & (cc[None, :] < cstart[:, None] + 16)
    cmask = np.where(col_in.T, 0.0, NEG).astype(np.float32)
    c["colmask"] = np.concatenate([cmask, cmask], axis=0)
    for L, nm in ((LS, "L"), (LP, "S")):
        t = np.arange(L, dtype=np.float32)
        tn = (t / np.float32(L)).astype(np.float32)
        bands = np.arange(1, 17, dtype=np.float32)
        a = (np.float32(2.0 * np.pi) * tn[:, None] * bands[None, :]).astype(np.float32)
        z = np.concatenate([tn[:, None], np.cos(a), np.sin(a)], axis=-1).astype(np.float32)
        c["zpos" + nm] = np.ascontiguousarray(z.T)
        dist = (np.abs(t - L // 2) * np.float32(2.0 / L)).astype(np.float32)
        c["ndist" + nm] = np.ascontiguousarray((-dist).reshape(L // 128, 128).T)
        N = 2 * L
        s = np.arange(L, dtype=np.float64)
        f = np.arange(L, dtype=np.float64)
        th = 2.0 * np.pi * np.outer(s, f) / N
        FC = np.cos(th)
        FS = -np.sin(th)
        FS[:, 0] = np.cos(np.pi * s)
        n = np.arange(L, dtype=np.float64) + L // 2
        th2 = 2.0 * np.pi * np.outer(f, n) / N
        GC = (2.0 / N) * np.cos(th2)
        GS = -(2.0 / N) * np.sin(th2)
        GC[0, :] = 1.0 / N
        GS[0, :] = (1.0 / N) * np.cos(np.pi * n)
        nt = L // 128

        def lay(M):
            return np.ascontiguousarray(M.reshape(nt, 128, nt, 128).transpose(2, 1, 0, 3).reshape(nt, 128, nt * 128)).astype(bf)
        c["FC" + nm], c["FS" + nm], c["GC" + nm], c["GS" + nm] = lay(FC), lay(FS), lay(GC), lay(GS)
    c["deltas"] = np.abs(np.linspace(math.log(1e-2) / 1.5, math.log(1e-2) / 0.3, 1024, dtype=np.float32)).astype(np.float32)
    return c


_CACHE = {}


def kernel(**inp):
    f32 = np.float32
    g = {k: np.asarray(v) for k, v in inp.items()}
    if "nc" not in _CACHE:
        _CACHE["nc"] = build_program()
        _CACHE["const"] = _constants()
    nc = _CACHE["nc"]
    const = _CACHE["const"]

    def colsT(v, n):
        return np.ascontiguousarray(v.reshape(2, n, 128).transpose(0, 2, 1))
    rp = g["na_rpb"][..., ::-1]
    rpbp = np.zeros((2, 8, 15, 127), f32)
    rpbp[..., 48:79] = rp
    shared = {
        "w_mod": g["w_mod"], "b_modT": colsT(g["b_mod"], 96), "b_mod": g["b_mod"],
        "n1T": colsT(g["norm1_g"], 16), "n2T": colsT(g["norm2_g"], 16),
        "w_in": g["w_in"], "wa2": g["gla_wa2"], "ba2": g["gla_ba2"],
        "gng4": np.tile(g["gla_norm_g"], (1, 4)), "qg8": np.tile(g["na_qnorm_g"], (1, 8)), "kg8": np.tile(g["na_knorm_g"], (1, 8)),
        "rpbp": rpbp, "hcw": g["hy_conv_w"], "hcb": g["hy_conv_b"], "f1w": g["hy_f1_w"], "f1b": g["hy_f1_b"][..., None],
        "f2w": g["hy_f2_w"], "f2b": g["hy_f2_b"][..., None], "f3w": g["hy_f3_w"], "hfr": g["hy_freq"][..., None],
        "hbias": g["hy_bias"], "w_br_a": g["w_br_a"], "w_br_b": g["w_br_b"], "w_br_c": g["w_br_c"],
        "w_out": g["w_out"], "w_mlp1": g["w_mlp1"], "w_mlp2": g["w_mlp2"],
    }
    shared.update(const)
    shared = {k: np.ascontiguousarray(v) for k, v in shared.items()}
    in_maps = []
    for i in range(NCORES):
        m = dict(shared)
        m["x"] = np.ascontiguousarray(np.concatenate([g["x_sample"][i], g["x_prompt"][4 * i:4 * i + 4].reshape(4 * LP, D)], axis=0))
        cond = np.stack([g["c"][i], g["c_ctx"]], axis=0)
        m["condT"] = np.ascontiguousarray(cond.reshape(2, 16, 128).transpose(2, 1, 0))
        m["state0"] = np.ascontiguousarray(g["state_gla"][i])
        m["ck"] = np.ascontiguousarray(g["cache_na_k"][i])
        m["cv"] = np.ascontiguousarray(g["cache_na_v"][i])
        in_maps.append(m)
    res = run_bass_kernel_spmd(nc, in_maps, core_ids=list(range(NCORES)), trace=bool(os.environ.get('KTRACE')))
    if NCORES != 8:
        print('exec_time_ns', res.exec_time_ns)
        return res.results
    R = res.results
    y_sample = np.stack([R[i]["ys"] for i in range(8)], axis=0)
    y_prompt = np.concatenate([R[i]["yp"].reshape(4, LP, D) for i in range(8)], axis=0)
    nst = np.concatenate([R[i]["nst"] for i in range(8)], axis=0)
    nk = np.concatenate([R[i]["nk"] for i in range(8)], axis=0)
    nv = np.concatenate([R[i]["nv"] for i in range(8)], axis=0)
    return (y_prompt.astype(f32), y_sample.astype(f32), nst.astype(f32), nk.astype(f32), nv.astype(f32))
```

```python
import math
from contextlib import ExitStack
import numpy as np
import ml_dtypes
import concourse.bass as bass
import concourse.mybir as mybir
from concourse.bass_utils import run_bass_kernel_spmd

F32 = mybir.dt.float32
BF16 = mybir.dt.bfloat16
I32 = mybir.dt.int32
AF = mybir.ActivationFunctionType
ALU = mybir.AluOpType
AX = mybir.AxisListType

ENGS = ("sync", "act", "pool", "dve", "pe")
N_DMA_SEMS = 64
SAME_ENGINE_SYNC = True

D = 2048
NTOK = 5120
NT = 40
LS = 4096
LP = 256
DEPTH = 2
EPS = 1e-6
W_IN = 15392
NEG = -30000.0
NFT = {4096: 24, 256: 2}
import os
STOP = os.environ.get('KSTOP', '')
KGLA = int(os.environ.get('KGLA', '9'))
NOPOOL = int(os.environ.get('NOPOOL', '1'))
NCORES = int(os.environ.get('KCORES', '8'))


class Buf:
    __slots__ = ("w", "r", "const")

    def __init__(self, const=False):
        self.w = None
        self.r = {}
        self.const = const


class T:
    __slots__ = ("t", "b")

    def __init__(self, t, const=False):
        self.t = t
        self.b = Buf(const)

    def __getitem__(self, k):
        return self.t[k]


class KB:
    def __init__(self, nc, stack):
        self.nc = nc
        self.stack = stack
        self.ops = {e: [] for e in ENGS}
        self.sems = []
        for e in ENGS:
            self.sems.append(stack.enter_context(nc.semaphore("s_" + e)))
        self.own = {e: i for i, e in enumerate(ENGS)}
        self.cnt = {e: 0 for e in ENGS}
        self.seen = {e: {} for e in ENGS}
        self.dma_pools = {}
        for e, n in (("sync", 48), ("pool", 40), ("act", 8)):
            base = len(self.sems)
            for i in range(n):
                self.sems.append(stack.enter_context(nc.semaphore("s_dma_%s%d" % (e, i))))
            self.dma_pools[e] = {"base": base, "n": n, "tgt": [0] * n, "k": 0}
        self.n_inst = 0
        self.uid = 0

    def sb(self, shape, dtype, stack=None, const=False, name=None):
        st = stack or self.stack
        self.uid += 1
        return T(st.enter_context(self.nc.sbuf_tensor("%s_%d" % (name or "sb", self.uid), list(shape), dtype)), const)

    def ps(self, shape, dtype=F32, stack=None):
        st = stack or self.stack
        self.uid += 1
        return T(st.enter_context(self.nc.psum_tensor("ps_%d" % self.uid, list(shape), dtype)))

    def _wait(self, eng, ev):
        sk, v = ev
        if self.seen[eng].get(sk, 0) >= v:
            return
        self.seen[eng][sk] = v
        sem = self.sems[sk]
        self.ops[eng].append(lambda e, sem=sem, v=v: e.wait_ge(sem, v))

    def issue(self, eng, fns, reads=(), writes=(), dma=False):
        if callable(fns):
            fns = [fns]
        deps = []
        for t in reads:
            b = t.b
            if b.w is not None:
                deps.append(b.w)
        for t in writes:
            b = t.b
            if b.w is not None:
                deps.append(b.w)
            for sk, v in b.r.items():
                deps.append((sk, v))
        own = self.own[eng]
        for ev in deps:
            if ev[0] == own and not dma:
                if eng == "pe" or not SAME_ENGINE_SYNC:
                    continue
            self._wait(eng, ev)
        if dma:
            dp = self.dma_pools[eng]
            slot = dp["k"] % dp["n"]
            dp["k"] += 1
            sk = dp["base"] + slot
            if dp["tgt"][slot] > 0:
                self._wait(eng, (sk, dp["tgt"][slot]))
            dp["tgt"][slot] += 16
            ev = (sk, dp["tgt"][slot])
            inc = 16
        else:
            self.cnt[eng] += 1
            sk = own
            ev = (sk, self.cnt[eng])
            inc = 1
        sem = self.sems[sk]
        for f in fns[:-1]:
            self.ops[eng].append(f)
        last = fns[-1]
        self.ops[eng].append(lambda e, last=last, sem=sem, inc=inc: last(e).then_inc(sem, inc))
        self.n_inst += len(fns)
        for t in reads:
            b = t.b
            if not b.const:
                if b.r.get(ev[0], 0) < ev[1]:
                    b.r[ev[0]] = ev[1]
        for t in writes:
            t.b.w = ev
            t.b.r = {}
        return ev

    def dma(self, eng, out, in_, r=(), w=(), **kw):
        return self.issue(eng, lambda e: e.dma_start(out=out, in_=in_, **kw), r, w, dma=True)

    def barrier(self):
        evs = [(self.own[e], self.cnt[e]) for e in ENGS if self.cnt[e] > 0]
        for dp in self.dma_pools.values():
            for i in range(dp["n"]):
                if dp["tgt"][i] > 0:
                    evs.append((dp["base"] + i, dp["tgt"][i]))
        for e in ENGS:
            for ev in evs:
                if ev[0] == self.own[e]:
                    continue
                self._wait(e, ev)

    def mm(self, out, pairs, r, w, start=True, stop=True):
        n = len(pairs)
        fns = []
        for i, (l, rh) in enumerate(pairs):
            fns.append(lambda e, l=l, rh=rh, s0=(start and i == 0), s1=(stop and i == n - 1):
                       e.matmul(out, lhsT=l, rhs=rh, start=s0, stop=s1))
        return self.issue("pe", fns, r, w)

    def tr(self, out, in_, ident, r, w):
        return self.issue("pe", lambda e: e.transpose(out, in_, ident), r, w)

    def act(self, out, in_, func, r, w, bias=0.0, scale=1.0, accum_out=None):
        if accum_out is None:
            return self.issue("act", lambda e: e.activation(out=out, in_=in_, func=func, bias=bias, scale=scale), r, w)
        return self.issue("act", lambda e: e.activation(out=out, in_=in_, func=func, bias=bias, scale=scale,
                                                        accum_out=accum_out), r, w)

    def tt(self, out, in0, in1, op, r, w, eng="dve"):
        if NOPOOL:
            eng = "dve"
        return self.issue(eng, lambda e: e.tensor_tensor(out=out, in0=in0, in1=in1, op=op), r, w)

    def ts(self, out, in0, s1, s2, op0, op1, r, w, eng="dve"):
        if s2 is None:
            return self.issue(eng, lambda e: e.tensor_scalar(out=out, in0=in0, scalar1=s1, scalar2=None, op0=op0), r, w)
        return self.issue(eng, lambda e: e.tensor_scalar(out=out, in0=in0, scalar1=s1, scalar2=s2, op0=op0, op1=op1), r, w)

    def stt(self, out, in0, scalar, in1, op0, op1, r, w, eng="dve"):
        return self.issue(eng, lambda e: e.scalar_tensor_tensor(out=out, in0=in0, scalar=scalar, in1=in1, op0=op0, op1=op1), r, w)

    def cp(self, out, in_, r, w, eng="dve"):
        if NOPOOL:
            eng = "dve"
        return self.issue(eng, lambda e: e.tensor_copy(out=out, in_=in_), r, w)

    def recip(self, out, in_, r, w):
        return self.issue("dve", lambda e: e.reciprocal(out=out, in_=in_), r, w)

    def memset(self, ap, val, w, eng="pool"):
        return self.issue(eng, lambda e: e.memset(ap, val), (), w)

    def finish(self):
        self.barrier()
        nc = self.nc
        ops = self.ops
        with nc.Block() as block:
            @block.sync
            def _(e):
                for f in ops["sync"]:
                    f(e)

            @block.scalar
            def _(e):
                for f in ops["act"]:
                    f(e)

            @block.gpsimd
            def _(e):
                for f in ops["pool"]:
                    f(e)

            @block.vector
            def _(e):
                for f in ops["dve"]:
                    f(e)

            @block.tensor
            def _(e):
                for f in ops["pe"]:
                    f(e)


def _div_le(n, cap):
    for d in range(min(n, cap), 0, -1):
        if n % d == 0:
            return d
    return 1


IN_SPECS = [
    ("x", [NTOK, D], F32), ("condT", [128, 16, 2], F32),
    ("state0", [2, 2, 4, 128, 256], F32), ("ck", [2, 8, 256, 128], F32), ("cv", [2, 8, 256, 128], F32),
    ("w_mod", [2, D, 6 * D], F32), ("b_modT", [2, 128, 96], F32), ("b_mod", [2, 6 * D], F32),
    ("n1T", [2, 128, 16], F32), ("n2T", [2, 128, 16], F32),
    ("w_in", [2, D, W_IN], F32), ("wa2", [2, 2, 16, 512], F32), ("ba2", [2, 2, 512], F32),
    ("gng4", [2, 1024], F32), ("qg8", [2, 1024], F32), ("kg8", [2, 1024], F32),
    ("rpbp", [2, 8, 15, 127], F32),
    ("hcw", [2, 3, 3072], F32), ("hcb", [2, 3072], F32), ("f1w", [2, 33, 64], F32), ("f1b", [2, 64, 1], F32),
    ("f2w", [2, 64, 64], F32), ("f2b", [2, 64, 1], F32), ("f3w", [2, 64, 2048], F32), ("hfr", [2, 64, 1], F32),
    ("hbias", [2, 2, 1024], F32),
    ("w_br_a", [2, 1024, D], F32), ("w_br_b", [2, 1024, D], F32), ("w_br_c", [2, 1024, D], F32),
    ("w_out", [2, D, D], F32), ("w_mlp1", [2, D, 4 * D], F32), ("w_mlp2", [2, 4 * D, D], F32),
    ("cpack", [128, 640], F32), ("cos4", [LS, 256], F32), ("sin4", [LS, 256], F32), ("colmask", [128, 64], F32),
    ("zposL", [33, LS], F32), ("zposS", [33, LP], F32), ("ndistL", [128, 32], F32), ("ndistS", [128, 2], F32),
    ("deltas", [1024], F32),
    ("FCL", [24, 128, 4096], BF16), ("FSL", [24, 128, 4096], BF16), ("GCL", [32, 128, 3072], BF16), ("GSL", [32, 128, 3072], BF16),
    ("FCS", [2, 128, 256], BF16), ("FSS", [2, 128, 256], BF16), ("GCS", [2, 128, 256], BF16), ("GSS", [2, 128, 256], BF16),
]
OUT_SPECS = [
    ("ys", [LS, D], F32), ("yp", [4 * LP, D], F32), ("nst", [4, 2, 2, 4, 128, 256], F32),
    ("nk", [4, 2, 8, 256, 128], F32), ("nv", [4, 2, 8, 256, 128], F32),
]


def build_program():
    nc = bass.Bass("TRN2", target_bir_lowering=False)
    I = {n: nc.dram_tensor(n, s, d, kind="ExternalInput") for n, s, d in IN_SPECS}
    O = {n: nc.dram_tensor(n, s, d, kind="ExternalOutput") for n, s, d in OUT_SPECS}
    A = {n: t.ap() for n, t in I.items()}
    OA = {n: t.ap() for n, t in O.items()}

    def dram(name, shape, dt):
        return nc.dram_tensor(name, shape, dt, kind="Internal").ap()

    WB = {
        "w_in": dram("wb_in", [2, D, W_IN], BF16), "w_br_a": dram("wb_a", [2, 1024, D], BF16),
        "w_br_b": dram("wb_b", [2, 1024, D], BF16), "w_br_c": dram("wb_c", [2, 1024, D], BF16),
        "w_out": dram("wb_o", [2, D, D], BF16), "w_mlp1": dram("wb_1", [2, D, 4 * D], BF16),
        "w_mlp2": dram("wb_2", [2, 4 * D, D], BF16),
    }
    SC = dram("sc", [NTOK, 9216], F32)
    SGA = dram("sga", [2, 16, NTOK], F32)
    H1T = dram("h1t", [16, 128, NTOK], BF16)
    OF = dram("of", [NTOK, 1024], F32)
    OAT = dram("oat", [8, 128, NTOK], BF16)
    OBT = dram("obt", [8, 128, NTOK], BF16)
    OCT = dram("oct", [8, 128, NTOK], BF16)
    QNT = dram("qnt", [8, 128, NTOK], BF16)
    KNT = dram("knt", [8, 128, NTOK], BF16)
    XS = dram("xs", [NTOK, D], F32)
    X1 = dram("x1", [NTOK, D], F32)
    GSC = dram("gsc", [2, 2, 128, D], F32)
    HG = dram("hg", [NTOK, 2048], F32)
    ZF = [dram("zf0", [NTOK, 1024], F32), dram("zf1", [NTOK, 1024], F32)]
    ZB = [dram("zb0", [NTOK, 1024], BF16), dram("zb1", [NTOK, 1024], BF16)]
    HB = {LS: dram("hbL", [LS, 2048], BF16), LP: dram("hbS", [LP, 2048], BF16)}
    HF = {LS: dram("hfL", [2, 24, 128, 2, 1024], F32), LP: dram("hfS", [2, 2, 128, 2, 1024], F32)}

    with ExitStack() as st:
        kb = KB(nc, st)
        cp32 = kb.sb([128, 640], F32, const=True)
        cpb = kb.sb([128, 640], BF16, const=True)
        P = [kb.ps([128, 512], F32) for _ in range(6)]
        PB = [kb.ps([128, 1024], BF16) for _ in range(2)]
        kb.dma("sync", cp32[:], A["cpack"][:, :], w=[cp32])
        kb.cp(cpb[:], cp32[:], [cp32], [cpb])
        IDb = cpb[:, 0:128]
        TRIf = {0: cp32[:, 128:256], 1: cp32[:, 256:384]}
        TRIb = {0: cpb[:, 128:256], 1: cpb[:, 256:384]}
        ONESf = cp32[:, 384:512]
        ONESb = cpb[:, 384:512]
        JREV = cp32[:, 512:640]
        CB = [cp32, cpb]
        rr = {"pb": 0}
        A1 = kb.sb([128, 16, 2], F32, st, name="A1")
        A2 = kb.sb([128, 16, 2], F32, st, name="A2")
        B1 = kb.sb([128, 16, 2], F32, st, name="B1")
        B2 = kb.sb([128, 16, 2], F32, st, name="B2")

        def evac_engine(i):
            return "act" if i % 2 == 0 else "dve"

        def copy_any(i, out, in_, r, w):
            if i % 2 == 0:
                kb.act(out, in_, AF.Copy, r, w)
            else:
                kb.cp(out, in_, r, w)

        for l in range(DEPTH):
            for name, wb in WB.items():
                src = A[name]
                rows, cols = src.shape[1], src.shape[2]
                cw = _div_le(cols, 2048)
                for r0 in range(0, rows, 128):
                    s_ap = src[l, r0:r0 + 128, :].rearrange("p (a b) -> p a b", b=cw)
                    d_ap = wb[l, r0:r0 + 128, :].rearrange("p (a b) -> p a b", b=cw)
                    kb.dma("pool", d_ap, s_ap)
        kb.barrier()
        stop_flag = {'s': STOP == 'cast'}

        def norm_to_T(xt, hT, col0, Am, Bm, j, tmp):
            _, ssq, rstd, xn = tmp
            kb.act(xn[:], xt[:], AF.Square, [xt], [xn, ssq], accum_out=ssq[:, 0:1])
            kb.act(rstd[:], ssq[:, 0:1], AF.Sqrt, [ssq], [rstd], bias=EPS, scale=1.0 / D)
            kb.recip(rstd[:], rstd[:], [rstd], [rstd])
            kb.ts(xn[:], xt[:], rstd[:, 0:1], None, ALU.mult, None, [xt, rstd], [xn])
            for half in range(2):
                pb = PB[rr["pb"] % 2]
                rr["pb"] += 1
                for q in range(8):
                    kc = half * 8 + q
                    kb.tr(pb[:, q * 128:(q + 1) * 128], xn[:, kc * 128:(kc + 1) * 128], IDb, [xn, cpb], [pb])
                for q in range(8):
                    kc = half * 8 + q
                    kb.act(hT[:, kc, col0:col0 + 128], pb[:, q * 128:(q + 1) * 128], AF.Identity, [pb, Am, Bm], [hT],
                           bias=Bm[:, kc, j:j + 1], scale=Am[:, kc, j:j + 1])

        def layer(l):
            if stop_flag['s']:
                return
            x_in = A["x"] if l == 0 else XS
            with ExitStack() as ph:
                mT = kb.sb([128, 96, 2], F32, ph)
                ct = kb.sb([128, 16, 2], F32, ph)
                sc = kb.sb([128, 16, 2], BF16, ph)
                sB = [kb.sb([128, 16, 128], BF16, ph) for _ in range(2)]
                bmT = kb.sb([128, 96], F32, ph)
                kb.dma("sync", ct[:], A["condT"][:, :, :], w=[ct])
                kb.dma("sync", bmT[:], A["b_modT"][l], w=[bmT])
                ctf = kb.sb([128, 16, 2], F32, ph)
                kb.act(ctf[:], ct[:], AF.Silu, [ct], [ctf])
                kb.cp(sc[:], ctf[:], [ctf], [sc])
                for j in range(2):
                    for kc in range(16):
                        kb.cp(sB[j][:, kc, :], ctf[:, kc, j:j + 1].to_broadcast([128, 128]), [ctf], [sB[j]])
                wm = [kb.sb([128, 16, 512], BF16, ph) for _ in range(2)]
                bb = [kb.sb([128, 512], F32, ph) for _ in range(2)]
                gst = [kb.sb([128, 512], F32, ph) for _ in range(2)]
                wsrc = A["w_mod"][l].rearrange("(kc p) c -> p kc c", p=128)
                gi = 0
                for cb in range(24):
                    w_ = wm[cb % 2]
                    kb.dma("pool", w_[:], wsrc[:, :, cb * 512:(cb + 1) * 512], w=[w_])
                    for sub in range(4):
                        ch = cb * 4 + sub
                        pp = P[ch % 2]
                        kb.mm(pp[:, 0:2], [(w_[:, kc, sub * 128:(sub + 1) * 128], sc[:, kc, :]) for kc in range(16)], [w_, sc], [pp])
                        kb.ts(mT[:, ch, :], pp[:, 0:2], bmT[:, ch:ch + 1], None, ALU.add, None, [pp, bmT], [mT])
                    gsel = {8: 0, 9: 0, 10: 0, 11: 0, 20: 1, 21: 1, 22: 1, 23: 1}.get(cb)
                    if gsel is not None:
                        b_ = bb[cb % 2]
                        kb.dma("sync", b_[:], A["b_mod"][l, cb * 512:(cb + 1) * 512].partition_broadcast(128), w=[b_])
                        for j in range(2):
                            pp = P[2 + j]
                            g_ = gst[gi % 2]
                            gi += 1
                            kb.mm(pp[:], [(sB[j][:, kc, :], w_[:, kc, :]) for kc in range(16)], [w_, sB[j]], [pp])
                            kb.tt(g_[:], pp[:], b_[:], ALU.add, [pp, b_], [g_])
                            c0 = (cb % 4) * 512
                            kb.dma("pool", GSC[j, gsel, :, c0:c0 + 512], g_[:], r=[g_])
                g1 = kb.sb([128, 16], F32, ph)
                g2 = kb.sb([128, 16], F32, ph)
                kb.dma("sync", g1[:], A["n1T"][l], w=[g1])
                kb.dma("sync", g2[:], A["n2T"][l], w=[g2])
                for j in range(2):
                    kb.stt(A1[:, :, j], mT[:, 16:32, j], 1.0, g1[:], ALU.add, ALU.mult, [mT, g1], [A1])
                    kb.stt(A2[:, :, j], mT[:, 64:80, j], 1.0, g2[:], ALU.add, ALU.mult, [mT, g2], [A2])
                    kb.cp(B1[:, :, j], mT[:, 0:16, j], [mT], [B1])
                    kb.cp(B2[:, :, j], mT[:, 48:64, j], [mT], [B2])
                kb.barrier()

            if STOP == "M" + str(l):
                stop_flag['s'] = True
                return
            with ExitStack() as ph:
                G = 1024
                hT = kb.sb([128, 16, G], BF16, ph)
                xt2 = [kb.sb([128, D], F32, ph) for _ in range(2)]
                tmp = (None, kb.sb([128, 1], F32, ph), kb.sb([128, 1], F32, ph), kb.sb([128, D], BF16, ph))
                wbuf = [kb.sb([128, 16, 512], BF16, ph) for _ in range(3)]
                stg = [kb.sb([128, 512], F32, ph) for _ in range(4)]
                wga = kb.sb([128, 16, 32], BF16, ph)
                gst_ = [kb.sb([16, 512], F32, ph) for _ in range(2)]
                wsrc = WB["w_in"][l].rearrange("(kc p) c -> p kc c", p=128)
                kb.dma("sync", wga[:], wsrc[:, :, 3072:3104], w=[wga])
                si = 0
                wi = 0
                for g in range(NTOK // G):
                    j = 0 if g < 4 else 1
                    for ti in range(8):
                        tile_i = g * 8 + ti
                        xt = xt2[tile_i % 2]
                        kb.dma("sync", xt[:], x_in[tile_i * 128:(tile_i + 1) * 128, :], w=[xt])
                        norm_to_T(xt, hT, ti * 128, A1, B1, j, tmp)
                    kb.dma("pool", H1T[:, :, g * G:(g + 1) * G].rearrange("kc p t -> p kc t"), hT[:], r=[hT])
                    for blk in range(18):
                        wcol = blk * 512 if blk < 6 else blk * 512 + 32
                        w_ = wbuf[wi % 3]
                        wi += 1
                        kb.dma("sync", w_[:], wsrc[:, :, wcol:wcol + 512], w=[w_])
                        for ti in range(8):
                            tile_i = g * 8 + ti
                            pp = P[si % 4]
                            s_ = stg[si % 4]
                            kb.mm(pp[:], [(hT[:, kc, ti * 128:(ti + 1) * 128], w_[:, kc, :]) for kc in range(16)], [hT, w_], [pp])
                            copy_any(si, s_[:], pp[:], [pp], [s_])
                            si += 1
                            kb.dma("pool", SC[tile_i * 128:(tile_i + 1) * 128, blk * 512:(blk + 1) * 512], s_[:], r=[s_])
                            if j == 1 and blk in (10, 11):
                                pi = (tile_i - 32) // 2
                                s0 = ((tile_i - 32) % 2) * 128
                                for hh in range(4):
                                    h = (blk - 10) * 4 + hh
                                    kb.dma("pool", OA["nv"][pi, l, h, s0:s0 + 128, :], s_[:, hh * 128:(hh + 1) * 128], r=[s_])
                    for dr in range(2):
                        for tq in range(G // 512):
                            pp = P[4 + (tq % 2)]
                            g_ = gst_[tq % 2]
                            kb.mm(pp[0:16, :], [(wga[:, kc, dr * 16:(dr + 1) * 16], hT[:, kc, tq * 512:(tq + 1) * 512]) for kc in range(16)],
                                  [hT, wga], [pp])
                            kb.cp(g_[:], pp[0:16, :], [pp], [g_])
                            kb.dma("pool", SGA[dr, :, g * G + tq * 512: g * G + (tq + 1) * 512], g_[:], r=[g_])
                kb.barrier()

            if STOP == "A" + str(l):
                stop_flag['s'] = True
                return
            for nm, fn in (("gla", gla_phase), ("na", na_phase), ("hy", hyena_phase), ("C", lambda l_: phase_c(l_, A2, B2))):
                if stop_flag['s']:
                    return
                fn(l)
                if STOP == nm + str(l):
                    stop_flag['s'] = True

        def gla_phase(l):
            with ExitStack() as ph:
                wa = kb.sb([17, 2, 512], F32, ph)
                kb.dma("sync", wa[0:16, :, :], A["wa2"][l].rearrange("d r c -> r d c"), w=[wa])
                kb.dma("sync", wa[16:17, :, :], A["ba2"][l:l + 1, :, :], w=[wa])
                gn = kb.sb([128, 1024], F32, ph)
                kb.dma("sync", gn[:], A["gng4"][l].partition_broadcast(128), w=[gn])
                S = [kb.sb([128, 256], F32, ph) for _ in range(4)]
                Sb = [kb.sb([128, 256], BF16, ph) for _ in range(4)]
                NB = 2
                qf = [kb.sb([128, 512], F32, ph) for _ in range(NB)]
                kf = [kb.sb([128, 512], F32, ph) for _ in range(NB)]
                vf = [kb.sb([128, 1024], F32, ph) for _ in range(NB)]
                vb = [kb.sb([128, 1024], BF16, ph) for _ in range(NB)]
                gaT = [kb.sb([17, 128], F32, ph) for _ in range(NB)]
                for g_ in gaT:
                    kb.memset(g_[:], 1.0, [g_])
                cs4 = [kb.sb([128, 256], F32, ph) for _ in range(NB)]
                sn4 = [kb.sb([128, 256], F32, ph) for _ in range(NB)]
                e1 = kb.sb([128, 512], F32, ph)
                sp = kb.sb([128, 512], F32, ph)
                tots = kb.sb([128, 512], F32, ph)
                dd = kb.sb([128, 512], F32, ph)
                EQ = kb.sb([128, 512], F32, ph)
                EK = kb.sb([128, 512], F32, ph)
                EH = kb.sb([128, 512], F32, ph)
                dec = kb.sb([128, 4], F32, ph)
                qr = kb.sb([128, 512], F32, ph)
                kr = kb.sb([128, 512], F32, ph)
                t1 = kb.sb([128, 256], F32, ph)
                t2 = kb.sb([128, 256], F32, ph)
                qt = kb.sb([128, 512], BF16, ph)
                kt = kb.sb([128, 512], BF16, ph)
                kh = kb.sb([128, 512], BF16, ph)
                qT = kb.sb([128, 512], BF16, ph)
                kT = kb.sb([128, 512], BF16, ph)
                AT = [kb.sb([128, 128], BF16, ph) for _ in range(2)]
                ost = kb.sb([128, 1024], F32, ph)
                ofl = kb.sb([128, 1024], F32, ph)
                gg = kb.sb([128, 1024], F32, ph)
                gs = kb.sb([128, 1024], F32, ph)
                junk = kb.sb([128, 256], F32, ph)
                ss4 = kb.sb([128, 4], F32, ph)
                rs4 = kb.sb([128, 4], F32, ph)
                oa = kb.sb([128, 1024], BF16, ph)
                oaT = kb.sb([128, 8, 128], BF16, ph)

                def rope(dst, src, c_, s_):
                    s3 = src[:].rearrange("p (h d) -> p h d", h=4)
                    d3 = dst[:].rearrange("p (h d) -> p h d", h=4)
                    c3 = c_[:].rearrange("p (h d) -> p h d", h=4)
                    n3 = s_[:].rearrange("p (h d) -> p h d", h=4)
                    a3 = t1[:].rearrange("p (h d) -> p h d", h=4)
                    b3 = t2[:].rearrange("p (h d) -> p h d", h=4)
                    kb.tt(a3, s3[:, :, 0:64], c3, ALU.mult, [src, c_], [t1])
                    kb.tt(b3, s3[:, :, 64:128], n3, ALU.mult, [src, s_], [t2], eng="pool")
                    kb.tt(d3[:, :, 0:64], a3, b3, ALU.subtract, [t1, t2], [dst])
                    kb.tt(a3, s3[:, :, 0:64], n3, ALU.mult, [src, s_], [t1])
                    kb.tt(b3, s3[:, :, 64:128], c3, ALU.mult, [src, c_], [t2], eng="pool")
                    kb.tt(d3[:, :, 64:128], a3, b3, ALU.add, [t1, t2], [dst])

                seqs = [(0, 32, True, None)] + [(32 + 2 * p, 2, False, p) for p in range(4)]
                if os.environ.get('KSEQ') == 'p':
                    seqs = seqs[1:]
                cnt = [0]
                kh2 = [kh, kb.sb([128, 512], BF16, ph)]
                qT2 = [qT, kb.sb([128, 512], BF16, ph)]
                kT2 = [kT, kb.sb([128, 512], BF16, ph)]
                dec2 = [dec, kb.sb([128, 4], F32, ph)]

                def prep(tile_i, dr, latent):
                    r0 = tile_i * 128
                    bi = cnt[0] % NB
                    cnt[0] += 1
                    q_, k_, v_, vb_, ga_ = qf[bi], kf[bi], vf[bi], vb[bi], gaT[bi]
                    kh_, qT_, kT_, dec_ = kh2[bi], qT2[bi], kT2[bi], dec2[bi]
                    kb.dma("sync", q_[:], SC[r0:r0 + 128, 0:512], w=[q_])
                    kb.dma("sync", k_[:], SC[r0:r0 + 128, 512:1024], w=[k_])
                    kb.dma("sync", v_[:], SC[r0:r0 + 128, 1024:2048], w=[v_])
                    kb.dma("sync", ga_[0:16, :], SGA[dr, :, r0:r0 + 128], w=[ga_])
                    if latent:
                        c_, s_ = cs4[bi], sn4[bi]
                        kb.dma("sync", c_[:], A["cos4"][r0:r0 + 128, :], w=[c_])
                        kb.dma("sync", s_[:], A["sin4"][r0:r0 + 128, :], w=[s_])
                    kb.act(vb_[:], v_[:], AF.Copy, [v_], [vb_])
                    kb.mm(P[0][:], [(ga_[:], wa[:, dr, :])], [ga_, wa], [P[0]])
                    kb.act(e1[:], P[0][:], AF.Exp, [P[0]], [e1], scale=-1.0)
                    kb.act(sp[:], e1[:], AF.Ln, [e1], [sp], bias=1.0)
                    kb.mm(P[1][:], [(TRIf[dr], sp[:])], [sp, cp32], [P[1]])
                    kb.mm(P[2][:], [(ONESf, sp[:])], [sp, cp32], [P[2]])
                    for h in range(4):
                        kb.mm(P[0][:, 2 * h:2 * h + 2], [(sp[:, h * 128:(h + 1) * 128], ONESf[:, 0:2])], [sp, cp32], [P[0]])
                    kb.act(dec_[:], P[0][:, 0:8:2], AF.Exp, [P[0]], [dec_], scale=-1.0 / 16)
                    kb.act(EQ[:], P[1][:], AF.Exp, [P[1]], [EQ], scale=-1.0 / 16)
                    kb.act(EK[:], P[1][:], AF.Exp, [P[1]], [EK], scale=1.0 / 16)
                    kb.act(tots[:], P[2][:], AF.Copy, [P[2]], [tots])
                    kb.tt(dd[:], P[1][:], tots[:], ALU.subtract, [P[1], tots], [dd])
                    kb.act(EH[:], dd[:], AF.Exp, [dd], [EH], scale=1.0 / 16)
                    if latent:
                        rope(qr, q_, c_, s_)
                        rope(kr, k_, c_, s_)
                        qs, ks = qr, kr
                    else:
                        qs, ks = q_, k_
                    kb.stt(qt[:], qs[:], 128.0 ** -0.5, EQ[:], ALU.mult, ALU.mult, [qs, EQ], [qt])
                    kb.tt(kt[:], ks[:], EK[:], ALU.mult, [ks, EK], [kt])
                    kb.tt(kh_[:], ks[:], EH[:], ALU.mult, [ks, EH], [kh_], eng="pool")
                    pb = PB[0]
                    for h in range(4):
                        kb.tr(pb[:, h * 128:(h + 1) * 128], qt[:, h * 128:(h + 1) * 128], IDb, [qt, cpb], [pb])
                        kb.tr(pb[:, 512 + h * 128:512 + (h + 1) * 128], kt[:, h * 128:(h + 1) * 128], IDb, [kt, cpb], [pb])
                    kb.act(qT_[:], pb[:, 0:512], AF.Identity, [pb], [qT_])
                    kb.act(kT_[:], pb[:, 512:1024], AF.Identity, [pb], [kT_])
                    return (r0, vb_, kh_, qT_, kT_, dec_)

                def heads(ctx, dr):
                    r0, vb_, kh_, qT_, kT_, dec_ = ctx
                    if dr == 1:
                        kb.dma("sync", ofl[:], OF[r0:r0 + 128, :], w=[ofl])
                        kb.dma("sync", gg[:], SC[r0:r0 + 128, 2048:3072], w=[gg])
                    for h in range(4):
                        hs = slice(h * 128, (h + 1) * 128)
                        at = AT[h % 2]
                        kb.mm(P[3][:, 0:128], [(kT_[:, hs], qT_[:, hs])], [kT_, qT_], [P[3]])
                        kb.tt(at[:], P[3][:, 0:128], TRIf[dr], ALU.mult, [P[3], cp32], [at])
                        po = P[4 + h // 2]
                        oc = (h % 2) * 256
                        kb.mm(po[:, oc:oc + 256], [(at[:], vb_[:, h * 256:(h + 1) * 256]), (qT_[:, hs], Sb[h][:])],
                              [at, vb_, qT_, Sb[h]], [po])
                        kb.mm(P[3][:, 256:512], [(kh_[:, hs], vb_[:, h * 256:(h + 1) * 256])], [kh_, vb_], [P[3]])
                        kb.stt(S[h][:], S[h][:], dec_[:, h:h + 1], P[3][:, 256:512], ALU.mult, ALU.add, [S[h], dec_, P[3]], [S[h]])
                        kb.act(Sb[h][:], S[h][:], AF.Copy, [S[h]], [Sb[h]])
                    if dr == 0:
                        kb.act(ost[:, 0:512], P[4][:], AF.Copy, [P[4]], [ost])
                        kb.cp(ost[:, 512:1024], P[5][:], [P[5]], [ost])
                        kb.dma("pool", OF[r0:r0 + 128, :], ost[:], r=[ost])
                    else:
                        kb.tt(ost[:, 0:512], P[4][:], ofl[:, 0:512], ALU.add, [P[4], ofl], [ost])
                        kb.tt(ost[:, 512:1024], P[5][:], ofl[:, 512:1024], ALU.add, [P[5], ofl], [ost])
                        for h in range(4):
                            kb.act(junk[:], ost[:, h * 256:(h + 1) * 256], AF.Square, [ost], [junk, ss4], accum_out=ss4[:, h:h + 1])
                        kb.act(rs4[:], ss4[:], AF.Sqrt, [ss4], [rs4], bias=EPS, scale=1.0 / 256)
                        kb.recip(rs4[:], rs4[:], [rs4], [rs4])
                        kb.act(gs[:], gg[:], AF.Silu, [gg], [gs])
                        kb.tt(gs[:], gs[:], gn[:], ALU.mult, [gs, gn], [gs], eng="pool")
                        for h in range(4):
                            hs2 = slice(h * 256, (h + 1) * 256)
                            kb.stt(oa[:, hs2], ost[:, hs2], rs4[:, h:h + 1], gs[:, hs2], ALU.mult, ALU.mult, [ost, rs4, gs], [oa])
                        pb = PB[1]
                        for q in range(8):
                            kb.tr(pb[:, q * 128:(q + 1) * 128], oa[:, q * 128:(q + 1) * 128], IDb, [oa, cpb], [pb])
                        kb.act(oaT[:].rearrange("p a b -> p (a b)"), pb[:], AF.Identity, [pb], [oaT])
                        kb.dma("pool", OAT[:, :, r0:r0 + 128].rearrange("kc p t -> p kc t"), oaT[:], r=[oaT])

                for (tile0, ntile, latent, pidx) in seqs:
                    for dr in range(2):
                        for h in range(4):
                            if latent:
                                kb.dma("sync", S[h][:], A["state0"][l, dr, h], w=[S[h]])
                            else:
                                kb.memset(S[h][:], 0.0, [S[h]])
                            kb.cp(Sb[h][:], S[h][:], [S[h]], [Sb[h]], eng="pool")
                        order = list(range(ntile)) if dr == 0 else list(range(ntile - 1, -1, -1))
                        cur = prep(tile0 + order[0], dr, latent)
                        for idx in range(len(order)):
                            nxt = prep(tile0 + order[idx + 1], dr, latent) if idx + 1 < len(order) else None
                            heads(cur, dr)
                            cur = nxt
                        if not latent:
                            for h in range(4):
                                kb.dma("pool", OA["nst"][pidx, l, dr, h], S[h][:], r=[S[h]])
                        kb.barrier()

        def na_phase(l):
            with ExitStack() as ph:
                qg = kb.sb([128, 1024], F32, ph)
                kg = kb.sb([128, 1024], F32, ph)
                kb.dma("sync", qg[:], A["qg8"][l].partition_broadcast(128), w=[qg])
                kb.dma("sync", kg[:], A["kg8"][l].partition_broadcast(128), w=[kg])
                xin = [[kb.sb([128, 1024], F32, ph) for _ in range(2)] for _ in range(2)]
                sq = kb.sb([128, 1024], F32, ph)
                ss8 = kb.sb([128, 8], F32, ph)
                rs8 = kb.sb([128, 8], F32, ph)
                nf = [kb.sb([128, 1024], F32, ph) for _ in range(2)]
                nb = kb.sb([128, 1024], BF16, ph)
                nT = [kb.sb([128, 8, 128], BF16, ph) for _ in range(2)]
                k = 0
                for tile_i in range(NT):
                    r0 = tile_i * 128
                    for which in range(2):
                        xi = xin[which][tile_i % 2]
                        c0 = 3072 + which * 1024
                        kb.dma("sync", xi[:], SC[r0:r0 + 128, c0:c0 + 1024], w=[xi])
                        kb.act(sq[:], xi[:], AF.Square, [xi], [sq])
                        kb.issue("dve", lambda e: e.tensor_reduce(out=ss8[:], in_=sq[:].rearrange("p (h d) -> p h d", h=8),
                                                                  axis=AX.X, op=ALU.add), [sq], [ss8])
                        kb.act(rs8[:], ss8[:], AF.Sqrt, [ss8], [rs8], bias=EPS, scale=1.0 / 128)
                        kb.recip(rs8[:], rs8[:], [rs8], [rs8])
                        n_ = nf[k % 2]
                        gsel = qg if which == 0 else kg
                        for h in range(8):
                            hs = slice(h * 128, (h + 1) * 128)
                            kb.stt(n_[:, hs], xi[:, hs], rs8[:, h:h + 1], gsel[:, hs], ALU.mult, ALU.mult, [xi, rs8, gsel], [n_])
                        if which == 1 and tile_i >= 32:
                            pi = (tile_i - 32) // 2
                            s0 = ((tile_i - 32) % 2) * 128
                            kb.dma("pool", OA["nk"][pi, l, :, s0:s0 + 128, :].rearrange("h s d -> s h d"),
                                   n_[:].rearrange("p (h d) -> p h d", h=8), r=[n_])
                        if which == 0:
                            kb.act(nb[:], n_[:], AF.Copy, [n_], [nb], scale=128.0 ** -0.5)
                        else:
                            kb.act(nb[:], n_[:], AF.Copy, [n_], [nb])
                        pb = PB[k % 2]
                        t_ = nT[k % 2]
                        for h in range(8):
                            kb.tr(pb[:, h * 128:(h + 1) * 128], nb[:, h * 128:(h + 1) * 128], IDb, [nb, cpb], [pb])
                        kb.act(t_[:].rearrange("p a b -> p (a b)"), pb[:], AF.Identity, [pb], [t_])
                        dst = QNT if which == 0 else KNT
                        kb.dma("pool", dst[:, :, r0:r0 + 128].rearrange("h p t -> p h t"), t_[:], r=[t_])
                        k += 1
                kb.barrier()
            with ExitStack() as ph:
                BT = kb.sb([128, 14, 64], F32, ph)
                BTr = kb.sb([128, 14, 64], F32, ph)
                cm = kb.sb([128, 64], F32, ph)
                kb.dma("sync", cm[:], A["colmask"][:, :], w=[cm])
                qTh = [kb.sb([128, NTOK], BF16, ph) for _ in range(2)]
                kTh = [kb.sb([128, NTOK], BF16, ph) for _ in range(2)]
                vh = [kb.sb([128, NT, 128], BF16, ph) for _ in range(2)]
                kcf = kb.sb([128, 2, 128], BF16, ph)
                kcT = [kb.sb([128, 256], BF16, ph) for _ in range(2)]
                vc = [kb.sb([128, 2, 128], BF16, ph) for _ in range(2)]
                sl = [kb.sb([128, 4, 64], F32, ph) for _ in range(2)]
                El = [kb.sb([128, 4, 64], BF16, ph) for _ in range(2)]
                Ec = [kb.sb([128, 2, 64], BF16, ph) for _ in range(2)]
                Ed = [kb.sb([128, 2, 256], BF16, ph) for _ in range(2)]
                den = [kb.sb([128, 256], F32, ph) for _ in range(2)]
                obT = [kb.sb([128, NTOK], BF16, ph) for _ in range(2)]
                it = 0
                for h in range(8):
                    hb = h % 2
                    q_, k_, v_, kc_, vc_, ob_ = qTh[hb], kTh[hb], vh[hb], kcT[hb], vc[hb], obT[hb]
                    kb.dma("sync", q_[:], QNT[h], w=[q_])
                    kb.dma("sync", k_[:], KNT[h], w=[k_])
                    kb.dma("pool", v_[:], SC[:, 5120 + h * 128:5120 + (h + 1) * 128].rearrange("(t p) d -> p t d", p=128), w=[v_])
                    kb.dma("pool", kcf[:], A["ck"][l, h].rearrange("(t p) d -> p t d", p=128), w=[kcf])
                    kb.dma("pool", vc_[:], A["cv"][l, h].rearrange("(t p) d -> p t d", p=128), w=[vc_])
                    pb = PB[h % 2]
                    for t_ in range(2):
                        kb.tr(pb[:, t_ * 128:(t_ + 1) * 128], kcf[:, t_, :], IDb, [kcf, cpb], [pb])
                    kb.act(kc_[:], pb[:, 0:256], AF.Identity, [pb], [kc_])
                    for d0 in range(14):
                        for di in range(2):
                            src = bass.AP(tensor=I["rpbp"], offset=((l * 8 + h) * 15 + d0 + di) * 127, ap=[[1, 64], [1, 64]])
                            kb.dma("sync", BTr[di * 64:(di + 1) * 64, d0, :], src, w=[BTr])
                    for hf in range(2):
                        pj = P[hf]
                        kb.mm(pj[:, 0:448], [(JREV, BTr[:, hf * 7:(hf + 1) * 7, :].rearrange("p a b -> p (a b)"))], [cp32, BTr], [pj])
                        kb.tt(BT[:, hf * 7:(hf + 1) * 7, :], pj[:, 0:448].rearrange("p (a b) -> p a b", a=7),
                              cm[:].unsqueeze(1).to_broadcast([128, 7, 64]), ALU.add, [pj, cm], [BT])
                    for r in range(64):
                        rs_ = min(max(r - 4, 0), 56)
                        d0 = rs_ - r + 7
                        qc = slice(r * 64, (r + 1) * 64)
                        i2 = it % 2
                        it += 1
                        pl, pc = P[0 + i2], P[2 + i2]
                        for j in range(4):
                            kc0 = (rs_ + 2 * j) * 64
                            kb.mm(pl[:, j * 64:(j + 1) * 64], [(k_[:, kc0:kc0 + 128], q_[:, qc])], [k_, q_], [pl])
                        for j in range(2):
                            kb.mm(pc[:, j * 64:(j + 1) * 64], [(kc_[:, j * 128:(j + 1) * 128], q_[:, qc])], [kc_, q_], [pc])
                        kb.tt(sl[i2][:], pl[:, 0:256].rearrange("p (j c) -> p j c", j=4), BT[:, d0:d0 + 7:2, :], ALU.add, [pl, BT], [sl[i2]])
                        kb.act(El[i2][:], sl[i2][:], AF.Exp, [sl[i2]], [El[i2]])
                        kb.act(Ec[i2][:].rearrange("p j c -> p (j c)"), pc[:, 0:128], AF.Exp, [pc], [Ec[i2]])
                        po, pd = P[4], P[5]
                        oc = i2 * 64
                        prs = []
                        prd = []
                        for j in range(4):
                            kt_i = (rs_ + 2 * j) // 2
                            prs.append((v_[:, kt_i, :], El[i2][:, j, :]))
                            prd.append((ONESb, El[i2][:, j, :]))
                        for j in range(2):
                            prs.append((vc_[:, j, :], Ec[i2][:, j, :]))
                            prd.append((ONESb, Ec[i2][:, j, :]))
                        kb.mm(po[:, oc:oc + 64], prs, [v_, vc_, El[i2], Ec[i2]], [po])
                        kb.mm(pd[:, oc:oc + 64], prd, [cpb, El[i2], Ec[i2]], [pd])
                        kb.recip(den[i2][:, 0:64], pd[:, oc:oc + 64], [pd], [den[i2]])
                        kb.tt(ob_[:, qc], po[:, oc:oc + 64], den[i2][:, 0:64], ALU.mult, [po, den[i2]], [ob_])
                    for p in range(4):
                        t0 = LS + p * 256
                        i2 = it % 2
                        it += 1
                        pl = P[0 + i2]
                        for j in range(2):
                            kb.mm(pl[:, j * 256:(j + 1) * 256], [(k_[:, t0 + j * 128:t0 + (j + 1) * 128], q_[:, t0:t0 + 256])], [k_, q_], [pl])
                        kb.act(Ed[i2][:].rearrange("p j c -> p (j c)"), pl[:], AF.Exp, [pl], [Ed[i2]])
                        po, pd = P[2 + i2], P[4 + i2]
                        kb.mm(po[:, 0:256], [(v_[:, 32 + 2 * p + j, :], Ed[i2][:, j, :]) for j in range(2)], [v_, Ed[i2]], [po])
                        kb.mm(pd[:, 0:256], [(ONESb, Ed[i2][:, j, :]) for j in range(2)], [cpb, Ed[i2]], [pd])
                        kb.recip(den[i2][:], pd[:, 0:256], [pd], [den[i2]])
                        kb.tt(ob_[:, t0:t0 + 256], po[:, 0:256], den[i2][:], ALU.mult, [po, den[i2]], [ob_])
                    kb.dma("pool", OBT[h], ob_[:], r=[ob_])
                kb.barrier()

        def hyena_phase(l):
            TWO_PI = 2.0 * math.pi
            def fwd_dft(ph, zb, n_sc, n_ft, FC, FS, ftab, consume):
                for ft in range(n_ft):
                    fc_, fs_ = ftab[0][ft % 2], ftab[1][ft % 2]
                    kb.dma("sync", fc_[:, 0:n_sc * 128], FC[ft], w=[fc_])
                    kb.dma("sync", fs_[:, 0:n_sc * 128], FS[ft], w=[fs_])
                    pre, pim = P[(ft % 2) * 2], P[(ft % 2) * 2 + 1]
                    kb.mm(pre[:], [(fc_[:, s * 128:(s + 1) * 128], zb[:, s, :]) for s in range(n_sc)], [fc_, zb], [pre])
                    kb.mm(pim[:], [(fs_[:, s * 128:(s + 1) * 128], zb[:, s, :]) for s in range(n_sc)], [fs_, zb], [pim])
                    consume(ft, pre, pim)

            for (L, zpos, ndist) in ((LS, A["zposL"], A["ndistL"]), (LP, A["zposS"], A["ndistS"])):
                n_tt = L // 128
                FC, FS = (A["FCL"], A["FSL"]) if L == LS else (A["FCS"], A["FSS"])
                with ExitStack() as ph:
                    zp = kb.sb([33, L], F32, ph)
                    kb.dma("sync", zp[:], zpos[:, :], w=[zp])
                    f1w = kb.sb([33, 64], F32, ph)
                    f2w = kb.sb([64, 64], F32, ph)
                    f3w = kb.sb([64, 2048], F32, ph)
                    fb = kb.sb([64, 4], F32, ph)
                    kb.dma("sync", f1w[:], A["f1w"][l], w=[f1w])
                    kb.dma("sync", f2w[:], A["f2w"][l], w=[f2w])
                    kb.dma("sync", f3w[:], A["f3w"][l], w=[f3w])
                    kb.dma("sync", fb[:, 0:1], A["f1b"][l], w=[fb])
                    kb.dma("sync", fb[:, 1:2], A["f2b"][l], w=[fb])
                    kb.dma("sync", fb[:, 2:3], A["hfr"][l], w=[fb])
                    sc_ = kb.sb([64, 4], F32, ph)
                    kb.ts(sc_[:, 0:1], fb[:, 2:3], 1.0 / TWO_PI, None, ALU.mult, None, [fb], [sc_])
                    for q in range(2):
                        kb.ts(sc_[:, 1 + q:2 + q], fb[:, q:q + 1], sc_[:, 0:1], 8.0, ALU.mult, ALU.add, [fb, sc_], [sc_])
                    h2T = kb.sb([64, L], F32, ph)
                    h1c = kb.sb([64, 512], F32, ph)
                    u = kb.sb([64, 512], F32, ph)
                    ki = kb.sb([64, 512], I32, ph)
                    kf_ = kb.sb([64, 512], F32, ph)
                    gq_ = kb.sb([64, 512], F32, ph)

                    def sine_layer(out_ap, ps_ap, q, CW):
                        kb.ts(u[:, 0:CW], ps_ap, sc_[:, 0:1], sc_[:, 1 + q:2 + q], ALU.mult, ALU.add, [P[0], sc_], [u])
                        kb.cp(ki[:, 0:CW], u[:, 0:CW], [u], [ki])
                        kb.cp(kf_[:, 0:CW], ki[:, 0:CW], [ki], [kf_])
                        kb.tt(u[:, 0:CW], u[:, 0:CW], kf_[:, 0:CW], ALU.subtract, [u, kf_], [u])
                        kb.ts(gq_[:, 0:CW], u[:, 0:CW], 0.5, None, ALU.is_gt, None, [u], [gq_])
                        kb.tt(u[:, 0:CW], u[:, 0:CW], gq_[:, 0:CW], ALU.subtract, [u, gq_], [u])
                        kb.ts(gq_[:, 0:CW], u[:, 0:CW], -0.5, None, ALU.is_lt, None, [u], [gq_])
                        kb.tt(u[:, 0:CW], u[:, 0:CW], gq_[:, 0:CW], ALU.add, [u, gq_], [u])
                        kb.act(out_ap, u[:, 0:CW], AF.Sin, [u], [h1c, h2T], scale=TWO_PI)

                    CW = min(512, L)
                    for c in range(L // CW):
                        cs_ = slice(c * CW, (c + 1) * CW)
                        kb.mm(P[0][0:64, 0:CW], [(f1w[:], zp[:, cs_])], [f1w, zp], [P[0]])
                        sine_layer(h1c[:, 0:CW], P[0][0:64, 0:CW], 0, CW)
                        kb.mm(P[0][0:64, 0:CW], [(f2w[:], h1c[:, 0:CW])], [f2w, h1c], [P[0]])
                        sine_layer(h2T[:, cs_], P[0][0:64, 0:CW], 1, CW)
                    dl = kb.sb([128, 1024], F32, ph)
                    nd = kb.sb([128, n_tt], F32, ph)
                    kb.dma("sync", dl[:], A["deltas"].partition_broadcast(128), w=[dl])
                    kb.dma("sync", nd[:], ndist[:, :], w=[nd])
                    win = kb.sb([128, 1024], F32, ph)
                    hw = [kb.sb([128, 512], F32, ph) for _ in range(2)]
                    hsq = [kb.sb([128, 512], F32, ph) for _ in range(2)]
                    hbf = [kb.sb([128, 512], BF16, ph) for _ in range(2)]
                    rn = kb.sb([128, 2048], F32, ph)
                    ssb = kb.sb([128, 2048], F32, ph)
                    kb.memset(ssb[:], 0.0, [ssb])
                    k = 0
                    for tt_ in range(n_tt):
                        kb.act(win[:], dl[:], AF.Exp, [dl, nd], [win], scale=nd[:, tt_:tt_ + 1])
                        for cb in range(4):
                            i2 = k % 2
                            k += 1
                            pp = P[i2]
                            kb.mm(pp[:], [(h2T[:, tt_ * 128:(tt_ + 1) * 128], f3w[:, cb * 512:(cb + 1) * 512])], [h2T, f3w], [pp])
                            kb.tt(hw[i2][:], pp[:], win[:, (cb % 2) * 512:(cb % 2 + 1) * 512], ALU.mult, [pp, win], [hw[i2]])
                            kb.act(hsq[i2][:], hw[i2][:], AF.Square, [hw[i2]], [hsq[i2]])
                            kb.act(hbf[i2][:], hw[i2][:], AF.Copy, [hw[i2]], [hbf[i2]])
                            kb.dma("pool", HB[L][tt_ * 128:(tt_ + 1) * 128, cb * 512:(cb + 1) * 512], hbf[i2][:], r=[hbf[i2]])
                            p2 = P[2 + i2]
                            kb.mm(p2[:], [(ONESf, hsq[i2][:])], [cp32, hsq[i2]], [p2])
                            kb.tt(ssb[:, cb * 512:(cb + 1) * 512], ssb[:, cb * 512:(cb + 1) * 512], p2[:], ALU.add, [ssb, p2], [ssb])
                    kb.act(rn[:], ssb[:], AF.Sqrt, [ssb], [rn], bias=EPS)
                    kb.recip(rn[:], rn[:], [rn], [rn])
                    kb.barrier()
                    zbt = [kb.sb([128, n_tt, 512], BF16, ph) for _ in range(1)]
                    ftab = ([kb.sb([128, 4096], BF16, ph) for _ in range(2)], [kb.sb([128, 4096], BF16, ph) for _ in range(2)])
                    hst = [kb.sb([128, 2, 512], F32, ph) for _ in range(2)]
                    kk = [0]
                    for o in range(2):
                        for chalf in range(2):
                            c0 = o * 1024 + chalf * 512
                            zb = zbt[0]
                            kb.dma("sync", zb[:], HB[L][:, c0:c0 + 512].rearrange("(s p) c -> p s c", p=128), w=[zb])

                            def consume(ft, pre, pim, o=o, chalf=chalf, c0=c0):
                                hs_ = hst[kk[0] % 2]
                                kk[0] += 1
                                kb.tt(hs_[:, 0, :], pre[:], rn[:, c0:c0 + 512], ALU.mult, [pre, rn], [hs_])
                                kb.tt(hs_[:, 1, :], pim[:], rn[:, c0:c0 + 512], ALU.mult, [pim, rn], [hs_])
                                kb.dma("pool", HF[L][o, ft, :, :, chalf * 512:(chalf + 1) * 512], hs_[:], r=[hs_])
                            fwd_dft(ph, zb, n_tt, NFT[L], FC, FS, ftab, consume)
                    kb.barrier()

            with ExitStack() as ph:
                cw = kb.sb([128, 3, 3072], F32, ph)
                cbias = kb.sb([128, 3072], F32, ph)
                for j in range(3):
                    kb.dma("sync", cw[:, j, :], A["hcw"][l, j].partition_broadcast(128), w=[cw])
                kb.dma("sync", cbias[:], A["hcb"][l].partition_broadcast(128), w=[cbias])
                um = [kb.sb([128, 3072], F32, ph) for _ in range(2)]
                uc = [kb.sb([128, 3072], F32, ph) for _ in range(2)]
                up = [kb.sb([128, 3072], F32, ph) for _ in range(2)]
                acc = [kb.sb([128, 3072], F32, ph) for _ in range(1)]
                tm = [kb.sb([128, 3072], F32, ph) for _ in range(1)]
                zbf = [kb.sb([128, 1024], BF16, ph) for _ in range(2)]
                for tile_i in range(NT):
                    i2 = tile_i % 2
                    r0 = tile_i * 128
                    if tile_i < 32:
                        s_lo, s_hi = 0, LS
                    else:
                        s_lo = LS + ((tile_i - 32) // 2) * 256
                        s_hi = s_lo + 256
                    a_, b_, c_ = um[i2], uc[i2], up[i2]
                    kb.dma("sync", b_[:], SC[r0:r0 + 128, 6144:9216], w=[b_])
                    if r0 == s_lo:
                        kb.memset(a_[0:1, :], 0.0, [a_])
                        kb.dma("sync", a_[1:128, :], SC[r0:r0 + 127, 6144:9216], w=[a_])
                    else:
                        kb.dma("sync", a_[:], SC[r0 - 1:r0 + 127, 6144:9216], w=[a_])
                    if r0 + 128 == s_hi:
                        kb.memset(c_[:], 0.0, [c_])
                        kb.dma("sync", c_[0:127, :], SC[r0 + 1:r0 + 128, 6144:9216], w=[c_])
                    else:
                        kb.dma("sync", c_[:], SC[r0 + 1:r0 + 129, 6144:9216], w=[c_])
                    ac, t_ = acc[0], tm[0]
                    kb.tt(ac[:], a_[:], cw[:, 0, :], ALU.mult, [a_, cw], [ac])
                    kb.tt(t_[:], b_[:], cw[:, 1, :], ALU.mult, [b_, cw], [t_], eng="pool")
                    kb.tt(ac[:], ac[:], cbias[:], ALU.add, [ac, cbias], [ac])
                    kb.tt(ac[:], ac[:], t_[:], ALU.add, [ac, t_], [ac])
                    kb.tt(t_[:], c_[:], cw[:, 2, :], ALU.mult, [c_, cw], [t_], eng="pool")
                    kb.tt(ac[:], ac[:], t_[:], ALU.add, [ac, t_], [ac])
                    kb.act(zbf[i2][:], ac[:, 2048:3072], AF.Copy, [ac], [zbf[i2]])
                    kb.dma("pool", HG[r0:r0 + 128, :], ac[:, 0:2048], r=[ac])
                    kb.dma("pool", ZF[0][r0:r0 + 128, :], ac[:, 2048:3072], r=[ac])
                    kb.dma("pool", ZB[0][r0:r0 + 128, :], zbf[i2][:], r=[zbf[i2]])
                kb.barrier()

            with ExitStack() as ph:
                hbs = kb.sb([128, 2, 1024], F32, ph)
                for o in range(2):
                    kb.dma("sync", hbs[:, o, :], A["hbias"][l, o].partition_broadcast(128), w=[hbs])
                zbt = [kb.sb([128, 32, 512], BF16, ph) for _ in range(1)]
                ftab = ([kb.sb([128, 4096], BF16, ph) for _ in range(2)], [kb.sb([128, 4096], BF16, ph) for _ in range(2)])
                Yf = kb.sb([128, 24, 2, 512], BF16, ph)
                hft = [kb.sb([128, 2, 512], F32, ph) for _ in range(2)]
                m1 = kb.sb([128, 512], F32, ph)
                m2 = kb.sb([128, 512], F32, ph)
                gt = [kb.sb([128, 512], F32, ph) for _ in range(2)]
                zo = [kb.sb([128, 512], F32, ph) for _ in range(2)]
                zn = [kb.sb([128, 512], F32, ph) for _ in range(2)]
                znb = [kb.sb([128, 512], BF16, ph) for _ in range(2)]
                ocT = [kb.sb([128, 4, 128], BF16, ph) for _ in range(2)]
                cc = [0, 0, 0]
                seqs = [(0, LS)] + [(LS + p * 256, LP) for p in range(4)]
                for o in range(2):
                    for (t0, L) in seqs:
                        n_sc = L // 128
                        FC, FS, GC, GS = (A["FCL"], A["FSL"], A["GCL"], A["GSL"]) if L == LS else (A["FCS"], A["FSS"], A["GCS"], A["GSS"])
                        for chalf in range(2):
                            c0 = chalf * 512
                            zb = zbt[0]
                            cc[0] += 1
                            kb.dma("sync", zb[:, 0:n_sc, :], ZB[o][t0:t0 + L, c0:c0 + 512].rearrange("(s p) c -> p s c", p=128), w=[zb])

                            def consume(ft, pre, pim, o=o, L=L, c0=c0):
                                hf_ = hft[cc[1] % 2]
                                cc[1] += 1
                                kb.dma("sync", hf_[:], HF[L][o, ft, :, :, c0:c0 + 512], w=[hf_])
                                kb.tt(m1[:], pre[:], hf_[:, 0, :], ALU.mult, [pre, hf_], [m1])
                                kb.tt(m2[:], pim[:], hf_[:, 1, :], ALU.mult, [pim, hf_], [m2])
                                kb.tt(Yf[:, ft, 0, :], m1[:], m2[:], ALU.subtract, [m1, m2], [Yf])
                                kb.tt(m1[:], pre[:], hf_[:, 1, :], ALU.mult, [pre, hf_], [m1])
                                kb.tt(m2[:], pim[:], hf_[:, 0, :], ALU.mult, [pim, hf_], [m2])
                                kb.tt(Yf[:, ft, 1, :], m1[:], m2[:], ALU.add, [m1, m2], [Yf])
                                if ft == 0:
                                    kb.tt(Yf[0:1, 0, 0, :], pre[0:1, :], hf_[0:1, 0, :], ALU.mult, [pre, hf_], [Yf])
                                    kb.tt(Yf[0:1, 0, 1, :], pim[0:1, :], hf_[0:1, 1, :], ALU.mult, [pim, hf_], [Yf])
                            fwd_dft(ph, zb, n_sc, NFT[L], FC, FS, ftab, consume)
                            for tt_ in range(n_sc):
                                i2 = cc[2] % 2
                                cc[2] += 1
                                r0 = t0 + tt_ * 128
                                gc_, gs_ = ftab[0][tt_ % 2], ftab[1][tt_ % 2]
                                n_ft = NFT[L]
                                kb.dma("sync", gc_[:, 0:n_ft * 128], GC[tt_], w=[gc_])
                                kb.dma("sync", gs_[:, 0:n_ft * 128], GS[tt_], w=[gs_])
                                kb.dma("sync", gt[i2][:], HG[r0:r0 + 128, o * 1024 + c0:o * 1024 + c0 + 512], w=[gt[i2]])
                                kb.dma("sync", zo[i2][:], ZF[o][r0:r0 + 128, c0:c0 + 512], w=[zo[i2]])
                                py = P[4 + i2]
                                prs = []
                                for f in range(n_ft):
                                    prs.append((gc_[:, f * 128:(f + 1) * 128], Yf[:, f, 0, :]))
                                    prs.append((gs_[:, f * 128:(f + 1) * 128], Yf[:, f, 1, :]))
                                kb.mm(py[:], prs, [gc_, gs_, Yf], [py])
                                kb.tt(zo[i2][:], zo[i2][:], hbs[:, o, c0:c0 + 512], ALU.mult, [zo[i2], hbs], [zo[i2]], eng="pool")
                                kb.tt(zn[i2][:], py[:], zo[i2][:], ALU.add, [py, zo[i2]], [zn[i2]])
                                if o == 0:
                                    kb.tt(zn[i2][:], zn[i2][:], gt[i2][:], ALU.mult, [zn[i2], gt[i2]], [zn[i2]])
                                    kb.act(znb[i2][:], zn[i2][:], AF.Copy, [zn[i2]], [znb[i2]])
                                    kb.dma("pool", ZF[1][r0:r0 + 128, c0:c0 + 512], zn[i2][:], r=[zn[i2]])
                                    kb.dma("pool", ZB[1][r0:r0 + 128, c0:c0 + 512], znb[i2][:], r=[znb[i2]])
                                else:
                                    kb.tt(znb[i2][:], zn[i2][:], gt[i2][:], ALU.mult, [zn[i2], gt[i2]], [znb[i2]])
                                    pb = PB[i2]
                                    for q in range(4):
                                        kb.tr(pb[:, q * 128:(q + 1) * 128], znb[i2][:, q * 128:(q + 1) * 128], IDb, [znb[i2], cpb], [pb])
                                    kb.act(ocT[i2][:].rearrange("p a b -> p (a b)"), pb[:, 0:512], AF.Identity, [pb], [ocT[i2]])
                                    kb.dma("pool", OCT[chalf * 4:(chalf + 1) * 4, :, r0:r0 + 128].rearrange("kc p t -> p kc t"), ocT[i2][:], r=[ocT[i2]])
                    kb.barrier()

        def phase_c(l, A2, B2):
            G = 512
            with ExitStack() as ph:
                arena = kb.sb([128, 32768], BF16, ph)
                h1T = arena[:, 0:8192].rearrange("p (k t) -> p k t", k=16)
                oT = [arena[:, 8192 + b * 4096: 8192 + (b + 1) * 4096].rearrange("p (k t) -> p k t", k=8) for b in range(3)]
                hidT = arena[:, :].rearrange("p (k t) -> p k t", k=64)
                mgT = kb.sb([128, 16, G], BF16, ph)
                h2T = kb.sb([128, 16, G], BF16, ph)
                gch = [kb.sb([128, 512], F32, ph) for _ in range(2)]
                xch = [kb.sb([128, 512], F32, ph) for _ in range(2)]
                xt = kb.sb([128, D], F32, ph)
                tmp = (None, kb.sb([128, 1], F32, ph), kb.sb([128, 1], F32, ph), kb.sb([128, D], BF16, ph))
                wg = [kb.sb([128, 16, 128], BF16, ph) for _ in range(3)]
                wbr = [kb.sb([128, 8, 128], BF16, ph) for _ in range(3)]
                wo = kb.sb([128, 16, 512], BF16, ph)
                w2 = [kb.sb([128, 512], BF16, ph) for _ in range(6)]
                sg = [kb.sb([128, G], F32, ph) for _ in range(2)]
                pr = [kb.sb([128, G], F32, ph) for _ in range(2)]
                macc = kb.sb([128, G], F32, ph)
                ysb = [kb.sb([128, 512], F32, ph) for _ in range(2)]
                hs1 = [kb.sb([128, G], F32, ph) for _ in range(2)]
                x1b = T(None)
                win_src = WB["w_in"][l].rearrange("(kc p) c -> p kc c", p=128)
                brs = [WB["w_br_a"][l].rearrange("(kc p) c -> p kc c", p=128), WB["w_br_b"][l].rearrange("(kc p) c -> p kc c", p=128),
                       WB["w_br_c"][l].rearrange("(kc p) c -> p kc c", p=128)]
                wo_src = WB["w_out"][l].rearrange("(kc p) c -> p kc c", p=128)
                w1_src = WB["w_mlp1"][l].rearrange("(kc p) c -> p kc c", p=128)
                w2_src = WB["w_mlp2"][l]
                srcT = [OAT, OBT, OCT]
                c = {"wg": 0, "w2": 0, "k": 0, "y": 0}
                x_in = A["x"] if l == 0 else XS
                for g in range(NTOK // G):
                    j = 0 if g < 8 else 1
                    tk = slice(g * G, (g + 1) * G)
                    kb.dma("sync", h1T, H1T[:, :, tk].rearrange("kc p t -> p kc t"), w=[arena])
                    for b in range(3):
                        kb.dma("sync", oT[b], srcT[b][:, :, tk].rearrange("kc p t -> p kc t"), w=[arena])
                    for mf in range(16):
                        for b in range(3):
                            wg_ = wg[c["wg"] % 3]
                            wb_ = wbr[c["wg"] % 3]
                            c["wg"] += 1
                            gcol = 9248 + b * 2048 + mf * 128
                            kb.dma("sync", wg_[:], win_src[:, :, gcol:gcol + 128], w=[wg_])
                            kb.dma("sync", wb_[:], brs[b][:, :, mf * 128:(mf + 1) * 128], w=[wb_])
                            i2 = c["k"] % 2
                            c["k"] += 1
                            pg, pp = P[i2], P[2 + i2]
                            kb.mm(pg[:], [(wg_[:, kc, :], h1T[:, kc, :]) for kc in range(16)], [wg_, arena], [pg])
                            kb.mm(pp[:], [(wb_[:, kc, :], oT[b][:, kc, :]) for kc in range(8)], [wb_, arena], [pp])
                            kb.act(sg[i2][:], pg[:], AF.Sigmoid, [pg], [sg[i2]])
                            if b == 0:
                                kb.tt(macc[:], pp[:], sg[i2][:], ALU.mult, [pp, sg[i2]], [macc])
                            else:
                                kb.tt(pr[i2][:], pp[:], sg[i2][:], ALU.mult, [pp, sg[i2]], [pr[i2]])
                                if b == 1:
                                    kb.tt(macc[:], macc[:], pr[i2][:], ALU.add, [macc, pr[i2]], [macc], eng="pool")
                                else:
                                    kb.tt(mgT[:, mf, :], macc[:], pr[i2][:], ALU.add, [macc, pr[i2]], [mgT], eng="pool")
                    for oc in range(4):
                        ocs = slice(oc * 512, (oc + 1) * 512)
                        kb.dma("sync", wo[:], wo_src[:, :, ocs], w=[wo])
                        for ti in range(4):
                            r0 = (g * 4 + ti) * 128
                            i2 = c["y"] % 2
                            c["y"] += 1
                            py, ys_, g_, x_ = P[4 + i2], ysb[i2], gch[i2], xch[i2]
                            kb.dma("sync", g_[:], GSC[j, 0, :, ocs], w=[g_])
                            kb.dma("sync", x_[:], x_in[r0:r0 + 128, ocs], w=[x_])
                            kb.mm(py[:], [(mgT[:, mf, ti * 128:(ti + 1) * 128], wo[:, mf, :]) for mf in range(16)], [mgT, wo], [py])
                            kb.tt(ys_[:], py[:], g_[:], ALU.mult, [py, g_], [ys_])
                            kb.tt(ys_[:], ys_[:], x_[:], ALU.add, [ys_, x_], [ys_], eng="pool")
                            kb.dma("pool", X1[r0:r0 + 128, ocs], ys_[:], r=[ys_], w=[x1b])
                    for ti in range(4):
                        r0 = (g * 4 + ti) * 128
                        kb.dma("sync", xt[:], X1[r0:r0 + 128, :], r=[x1b], w=[xt])
                        norm_to_T(xt, h2T, ti * 128, A2, B2, j, tmp)
                    for ff in range(64):
                        w1_ = wg[c["wg"] % 3]
                        c["wg"] += 1
                        kb.dma("sync", w1_[:], w1_src[:, :, ff * 128:(ff + 1) * 128], w=[w1_])
                        i2 = c["k"] % 2
                        c["k"] += 1
                        pg = P[i2]
                        kb.mm(pg[:], [(w1_[:, kc, :], h2T[:, kc, :]) for kc in range(16)], [w1_, h2T], [pg])
                        kb.act(hs1[i2][:], pg[:], AF.Relu, [pg], [hs1[i2]])
                        kb.tt(hidT[:, ff, :], hs1[i2][:], hs1[i2][:], ALU.mult, [hs1[i2]], [arena], eng=("dve" if ff % 2 == 0 else "pool"))
                    for oc in range(4):
                        ocs = slice(oc * 512, (oc + 1) * 512)
                        pys = [P[2], P[3], P[4], P[5]]
                        for ff in range(64):
                            w2_ = w2[c["w2"] % 6]
                            c["w2"] += 1
                            kb.dma("sync", w2_[:], w2_src[ff * 128:(ff + 1) * 128, ocs], w=[w2_])
                            for ti in range(4):
                                kb.mm(pys[ti][:], [(hidT[:, ff, ti * 128:(ti + 1) * 128], w2_[:])], [arena, w2_], [pys[ti]],
                                      start=(ff == 0), stop=(ff == 63))
                        for ti in range(4):
                            tile_i = g * 4 + ti
                            r0 = tile_i * 128
                            i2 = c["y"] % 2
                            c["y"] += 1
                            ys_, g_, x_ = ysb[i2], gch[i2], xch[i2]
                            kb.dma("sync", g_[:], GSC[j, 1, :, ocs], w=[g_])
                            kb.dma("sync", x_[:], X1[r0:r0 + 128, ocs], r=[x1b], w=[x_])
                            kb.tt(ys_[:], pys[ti][:], g_[:], ALU.mult, [pys[ti], g_], [ys_])
                            kb.tt(ys_[:], ys_[:], x_[:], ALU.add, [ys_, x_], [ys_], eng="pool")
                            if l == DEPTH - 1:
                                if tile_i < 32:
                                    dst = OA["ys"][r0:r0 + 128, ocs]
                                else:
                                    dst = OA["yp"][r0 - LS:r0 - LS + 128, ocs]
                            else:
                                dst = XS[r0:r0 + 128, ocs]
                            kb.dma("pool", dst, ys_[:], r=[ys_])
                kb.barrier()

        for l in range(DEPTH):
            layer(l)
        kb.finish()
        print("n_inst", kb.n_inst, {e: len(kb.ops[e]) for e in ENGS})
    return nc


def _constants():
    bf = ml_dtypes.bfloat16
    c = {}
    p = np.arange(128)
    ident = np.eye(128, dtype=np.float32)
    tri_le = (p[:, None] <= p[None, :]).astype(np.float32)
    tri_ge = (p[:, None] >= p[None, :]).astype(np.float32)
    jrev = ((p[:, None] // 64 == p[None, :] // 64) & (p[:, None] % 64 == 63 - p[None, :] % 64)).astype(np.float32)
    c["cpack"] = np.concatenate([ident, tri_le, tri_ge, np.ones((128, 128), np.float32), jrev], axis=1)
    pos = np.arange(LS)
    row = (pos // 64).astype(np.float32)
    col = (pos % 64).astype(np.float32)
    inv = (np.float32(10000.0) ** (-np.arange(32, dtype=np.float32) / np.float32(32))).astype(np.float32)
    ang = np.concatenate([row[:, None] * inv, col[:, None] * inv], axis=-1).astype(np.float32)
    c["cos4"] = np.tile(np.cos(ang).astype(np.float32), (1, 4))
    c["sin4"] = np.tile(np.sin(ang).astype(np.float32), (1, 4))
    cc = np.arange(64)
    cstart = np.clip(cc - 8, 0, 48)
    col_in = (cc[None, :] >= cstart[:, None]) & (cc[None, :] < cstart[:, None] + 16)
    cmask = np.where(col_in.T, 0.0, NEG).astype(np.float32)
    c["colmask"] = np.concatenate([cmask, cmask], axis=0)
    for L, nm in ((LS, "L"), (LP, "S")):
        t = np.arange(L, dtype=np.float32)
        tn = (t / np.float32(L)).astype(np.float32)
        bands = np.arange(1, 17, dtype=np.float32)
        a = (np.float32(2.0 * np.pi) * tn[:, None] * bands[None, :]).astype(np.float32)
        z = np.concatenate([tn[:, None], np.cos(a), np.sin(a)], axis=-1).astype(np.float32)
        c["zpos" + nm] = np.ascontiguousarray(z.T)
        dist = (np.abs(t - L // 2) * np.float32(2.0 / L)).astype(np.float32)
        c["ndist" + nm] = np.ascontiguousarray((-dist).reshape(L // 128, 128).T)
        nf = NFT[L] * 128
        N = 2 * nf
        s = np.arange(L, dtype=np.float64)
        f = np.arange(nf, dtype=np.float64)
        th = 2.0 * np.pi * np.outer(s, f) / N
        FC = np.cos(th)
        FS = -np.sin(th)
        FS[:, 0] = np.cos(np.pi * s)
        n = np.arange(L, dtype=np.float64) + L // 2
        th2 = 2.0 * np.pi * np.outer(f, n) / N
        GC = (2.0 / N) * np.cos(th2)
        GS = -(2.0 / N) * np.sin(th2)
        GC[0, :] = 1.0 / N
        GS[0, :] = (1.0 / N) * np.cos(np.pi * n)

        def lay(M):
            R, C = M.shape
            return np.ascontiguousarray(M.reshape(R // 128, 128, C // 128, 128).transpose(2, 1, 0, 3).reshape(C // 128, 128, R)).astype(bf)
        c["FC" + nm], c["FS" + nm], c["GC" + nm], c["GS" + nm] = lay(FC), lay(FS), lay(GC), lay(GS)
    c["deltas"] = np.abs(np.linspace(math.log(1e-2) / 1.5, math.log(1e-2) / 0.3, 1024, dtype=np.float32)).astype(np.float32)
    return c


_CACHE = {}


def kernel(**inp):
    f32 = np.float32
    g = {k: np.asarray(v) for k, v in inp.items()}
    if "nc" not in _CACHE:
        _CACHE["nc"] = build_program()
        _CACHE["const"] = _constants()
    nc = _CACHE["nc"]
    const = _CACHE["const"]

    def colsT(v, n):
        return np.ascontiguousarray(v.reshape(2, n, 128).transpose(0, 2, 1))
    rp = g["na_rpb"][..., ::-1]
    rpbp = np.zeros((2, 8, 15, 127), f32)
    rpbp[..., 48:79] = rp
    shared = {
        "w_mod": g["w_mod"], "b_modT": colsT(g["b_mod"], 96), "b_mod": g["b_mod"],
        "n1T": colsT(g["norm1_g"], 16), "n2T": colsT(g["norm2_g"], 16),
        "w_in": g["w_in"], "wa2": g["gla_wa2"], "ba2": g["gla_ba2"],
        "gng4": np.tile(g["gla_norm_g"], (1, 4)), "qg8": np.tile(g["na_qnorm_g"], (1, 8)), "kg8": np.tile(g["na_knorm_g"], (1, 8)),
        "rpbp": rpbp, "hcw": g["hy_conv_w"], "hcb": g["hy_conv_b"], "f1w": g["hy_f1_w"], "f1b": g["hy_f1_b"][..., None],
        "f2w": g["hy_f2_w"], "f2b": g["hy_f2_b"][..., None], "f3w": g["hy_f3_w"], "hfr": g["hy_freq"][..., None],
        "hbias": g["hy_bias"], "w_br_a": g["w_br_a"], "w_br_b": g["w_br_b"], "w_br_c": g["w_br_c"],
        "w_out": g["w_out"], "w_mlp1": g["w_mlp1"], "w_mlp2": g["w_mlp2"],
    }
    shared.update(const)
    shared = {k: np.ascontiguousarray(v) for k, v in shared.items()}
    in_maps = []
    for i in range(NCORES):
        m = dict(shared)
        m["x"] = np.ascontiguousarray(np.concatenate([g["x_sample"][i], g["x_prompt"][4 * i:4 * i + 4].reshape(4 * LP, D)], axis=0))
        cond = np.stack([g["c"][i], g["c_ctx"]], axis=0)
        m["condT"] = np.ascontiguousarray(cond.reshape(2, 16, 128).transpose(2, 1, 0))
        m["state0"] = np.ascontiguousarray(g["state_gla"][i])
        m["ck"] = np.ascontiguousarray(g["cache_na_k"][i])
        m["cv"] = np.ascontiguousarray(g["cache_na_v"][i])
        in_maps.append(m)
    res = run_bass_kernel_spmd(nc, in_maps, core_ids=list(range(NCORES)), trace=bool(os.environ.get('KTRACE')))
    if NCORES != 8:
        print('exec_time_ns', res.exec_time_ns)
        return res.results
    R = res.results
    y_sample = np.stack([R[i]["ys"] for i in range(8)], axis=0)
    y_prompt = np.concatenate([R[i]["yp"].reshape(4, LP, D) for i in range(8)], axis=0)
    nst = np.concatenate([R[i]["nst"] for i in range(8)], axis=0)
    nk = np.concatenate([R[i]["nk"] for i in range(8)], axis=0)
    nv = np.concatenate([R[i]["nv"] for i in range(8)], axis=0)
    return (y_prompt.astype(f32), y_sample.astype(f32), nst.astype(f32), nk.astype(f32), nv.astype(f32))
```

```python
import math
from contextlib import ExitStack
import numpy as np
import ml_dtypes
import concourse.bass as bass
import concourse.mybir as mybir
from concourse.bass_utils import run_bass_kernel_spmd

F32 = mybir.dt.float32
BF16 = mybir.dt.bfloat16
I32 = mybir.dt.int32
AF = mybir.ActivationFunctionType
ALU = mybir.AluOpType
AX = mybir.AxisListType

ENGS = ("sync", "act", "pool", "dve", "pe")
N_DMA_SEMS = 64
SAME_ENGINE_SYNC = True

D = 2048
NTOK = 5120
NT = 40
LS = 4096
LP = 256
DEPTH = 2
EPS = 1e-6
W_IN = 15392
NEG = -30000.0
NFT = {4096: 24, 256: 2}
import os
STOP = os.environ.get('KSTOP', '')
KGLA = int(os.environ.get('KGLA', '9'))
NOPOOL = int(os.environ.get('NOPOOL', '0'))
NCORES = int(os.environ.get('KCORES', '8'))


class Buf:
    __slots__ = ("w", "r", "const")

    def __init__(self, const=False):
        self.w = None
        self.r = {}
        self.const = const


class T:
    __slots__ = ("t", "b")

    def __init__(self, t, const=False):
        self.t = t
        self.b = Buf(const)

    def __getitem__(self, k):
        return self.t[k]


class KB:
    def __init__(self, nc, stack):
        self.nc = nc
        self.stack = stack
        self.ops = {e: [] for e in ENGS}
        self.sems = []
        for e in ENGS:
            self.sems.append(stack.enter_context(nc.semaphore("s_" + e)))
        self.own = {e: i for i, e in enumerate(ENGS)}
        self.cnt = {e: 0 for e in ENGS}
        self.seen = {e: {} for e in ENGS}
        self.dma_pools = {}
        for e, n in (("sync", 48), ("pool", 40), ("act", 8)):
            base = len(self.sems)
            for i in range(n):
                self.sems.append(stack.enter_context(nc.semaphore("s_dma_%s%d" % (e, i))))
            self.dma_pools[e] = {"base": base, "n": n, "tgt": [0] * n, "k": 0}
        self.n_inst = 0
        self.uid = 0

    def sb(self, shape, dtype, stack=None, const=False, name=None):
        st = stack or self.stack
        self.uid += 1
        return T(st.enter_context(self.nc.sbuf_tensor("%s_%d" % (name or "sb", self.uid), list(shape), dtype)), const)

    def ps(self, shape, dtype=F32, stack=None):
        st = stack or self.stack
        self.uid += 1
        return T(st.enter_context(self.nc.psum_tensor("ps_%d" % self.uid, list(shape), dtype)))

    def _wait(self, eng, ev):
        sk, v = ev
        if self.seen[eng].get(sk, 0) >= v:
            return
        self.seen[eng][sk] = v
        sem = self.sems[sk]
        self.ops[eng].append(lambda e, sem=sem, v=v: e.wait_ge(sem, v))

    def issue(self, eng, fns, reads=(), writes=(), dma=False):
        if callable(fns):
            fns = [fns]
        deps = []
        for t in reads:
            b = t.b
            if b.w is not None:
                deps.append(b.w)
        for t in writes:
            b = t.b
            if b.w is not None:
                deps.append(b.w)
            for sk, v in b.r.items():
                deps.append((sk, v))
        own = self.own[eng]
        for ev in deps:
            if ev[0] == own and not dma:
                if eng == "pe" or not SAME_ENGINE_SYNC:
                    continue
            self._wait(eng, ev)
        if dma:
            dp = self.dma_pools[eng]
            slot = dp["k"] % dp["n"]
            dp["k"] += 1
            sk = dp["base"] + slot
            if dp["tgt"][slot] > 0:
                self._wait(eng, (sk, dp["tgt"][slot]))
            dp["tgt"][slot] += 16
            ev = (sk, dp["tgt"][slot])
            inc = 16
        else:
            self.cnt[eng] += 1
            sk = own
            ev = (sk, self.cnt[eng])
            inc = 1
        sem = self.sems[sk]
        for f in fns[:-1]:
            self.ops[eng].append(f)
        last = fns[-1]
        self.ops[eng].append(lambda e, last=last, sem=sem, inc=inc: last(e).then_inc(sem, inc))
        self.n_inst += len(fns)
        for t in reads:
            b = t.b
            if not b.const:
                if b.r.get(ev[0], 0) < ev[1]:
                    b.r[ev[0]] = ev[1]
        for t in writes:
            t.b.w = ev
            t.b.r = {}
        return ev

    def dma(self, eng, out, in_, r=(), w=(), **kw):
        return self.issue(eng, lambda e: e.dma_start(out=out, in_=in_, **kw), r, w, dma=True)

    def barrier(self):
        evs = [(self.own[e], self.cnt[e]) for e in ENGS if self.cnt[e] > 0]
        for dp in self.dma_pools.values():
            for i in range(dp["n"]):
                if dp["tgt"][i] > 0:
                    evs.append((dp["base"] + i, dp["tgt"][i]))
        for e in ENGS:
            for ev in evs:
                if ev[0] == self.own[e]:
                    continue
                self._wait(e, ev)

    def mm(self, out, pairs, r, w, start=True, stop=True):
        n = len(pairs)
        fns = []
        for i, (l, rh) in enumerate(pairs):
            fns.append(lambda e, l=l, rh=rh, s0=(start and i == 0), s1=(stop and i == n - 1):
                       e.matmul(out, lhsT=l, rhs=rh, start=s0, stop=s1))
        return self.issue("pe", fns, r, w)

    def tr(self, out, in_, ident, r, w):
        return self.issue("pe", lambda e: e.transpose(out, in_, ident), r, w)

    def act(self, out, in_, func, r, w, bias=0.0, scale=1.0, accum_out=None):
        if accum_out is None:
            return self.issue("act", lambda e: e.activation(out=out, in_=in_, func=func, bias=bias, scale=scale), r, w)
        return self.issue("act", lambda e: e.activation(out=out, in_=in_, func=func, bias=bias, scale=scale,
                                                        accum_out=accum_out), r, w)

    def tt(self, out, in0, in1, op, r, w, eng="dve"):
        if NOPOOL:
            eng = "dve"
        return self.issue(eng, lambda e: e.tensor_tensor(out=out, in0=in0, in1=in1, op=op), r, w)

    def ts(self, out, in0, s1, s2, op0, op1, r, w, eng="dve"):
        if s2 is None:
            return self.issue(eng, lambda e: e.tensor_scalar(out=out, in0=in0, scalar1=s1, scalar2=None, op0=op0), r, w)
        return self.issue(eng, lambda e: e.tensor_scalar(out=out, in0=in0, scalar1=s1, scalar2=s2, op0=op0, op1=op1), r, w)

    def stt(self, out, in0, scalar, in1, op0, op1, r, w, eng="dve"):
        return self.issue(eng, lambda e: e.scalar_tensor_tensor(out=out, in0=in0, scalar=scalar, in1=in1, op0=op0, op1=op1), r, w)

    def cp(self, out, in_, r, w, eng="dve"):
        if NOPOOL:
            eng = "dve"
        return self.issue(eng, lambda e: e.tensor_copy(out=out, in_=in_), r, w)

    def recip(self, out, in_, r, w):
        return self.issue("dve", lambda e: e.reciprocal(out=out, in_=in_), r, w)

    def memset(self, ap, val, w, eng="pool"):
        return self.issue(eng, lambda e: e.memset(ap, val), (), w)

    def finish(self):
        self.barrier()
        nc = self.nc
        ops = self.ops
        with nc.Block() as block:
            @block.sync
            def _(e):
                for f in ops["sync"]:
                    f(e)

            @block.scalar
            def _(e):
                for f in ops["act"]:
                    f(e)

            @block.gpsimd
            def _(e):
                for f in ops["pool"]:
                    f(e)

            @block.vector
            def _(e):
                for f in ops["dve"]:
                    f(e)

            @block.tensor
            def _(e):
                for f in ops["pe"]:
                    f(e)


def _div_le(n, cap):
    for d in range(min(n, cap), 0, -1):
        if n % d == 0:
            return d
    return 1


IN_SPECS = [
    ("x", [NTOK, D], F32), ("condT", [128, 16, 2], F32),
    ("state0", [2, 2, 4, 128, 256], F32), ("ck", [2, 8, 256, 128], F32), ("cv", [2, 8, 256, 128], F32),
    ("w_mod", [2, D, 6 * D], F32), ("b_modT", [2, 128, 96], F32), ("b_mod", [2, 6 * D], F32),
    ("n1T", [2, 128, 16], F32), ("n2T", [2, 128, 16], F32),
    ("w_in", [2, D, W_IN], F32), ("wa2", [2, 2, 16, 512], F32), ("ba2", [2, 2, 512], F32),
    ("gng4", [2, 1024], F32), ("qg8", [2, 1024], F32), ("kg8", [2, 1024], F32),
    ("rpbp", [2, 8, 15, 127], F32),
    ("hcw", [2, 3, 3072], F32), ("hcb", [2, 3072], F32), ("f1w", [2, 33, 64], F32), ("f1b", [2, 64, 1], F32),
    ("f2w", [2, 64, 64], F32), ("f2b", [2, 64, 1], F32), ("f3w", [2, 64, 2048], F32), ("hfr", [2, 64, 1], F32),
    ("hbias", [2, 2, 1024], F32),
    ("w_br_a", [2, 1024, D], F32), ("w_br_b", [2, 1024, D], F32), ("w_br_c", [2, 1024, D], F32),
    ("w_out", [2, D, D], F32), ("w_mlp1", [2, D, 4 * D], F32), ("w_mlp2", [2, 4 * D, D], F32),
    ("cpack", [128, 640], F32), ("cos4", [LS, 256], F32), ("sin4", [LS, 256], F32), ("colmask", [128, 64], F32),
    ("zposL", [33, LS], F32), ("zposS", [33, LP], F32), ("ndistL", [128, 32], F32), ("ndistS", [128, 2], F32),
    ("deltas", [1024], F32),
    ("FCL", [24, 128, 4096], BF16), ("FSL", [24, 128, 4096], BF16), ("GCL", [32, 128, 3072], BF16), ("GSL", [32, 128, 3072], BF16),
    ("FCS", [2, 128, 256], BF16), ("FSS", [2, 128, 256], BF16), ("GCS", [2, 128, 256], BF16), ("GSS", [2, 128, 256], BF16),
]
OUT_SPECS = [
    ("ys", [LS, D], F32), ("yp", [4 * LP, D], F32), ("nst", [4, 2, 2, 4, 128, 256], F32),
    ("nk", [4, 2, 8, 256, 128], F32), ("nv", [4, 2, 8, 256, 128], F32),
]


def build_program():
    nc = bass.Bass("TRN2", target_bir_lowering=False)
    I = {n: nc.dram_tensor(n, s, d, kind="ExternalInput") for n, s, d in IN_SPECS}
    O = {n: nc.dram_tensor(n, s, d, kind="ExternalOutput") for n, s, d in OUT_SPECS}
    A = {n: t.ap() for n, t in I.items()}
    OA = {n: t.ap() for n, t in O.items()}

    def dram(name, shape, dt):
        return nc.dram_tensor(name, shape, dt, kind="Internal").ap()

    WB = {
        "w_in": dram("wb_in", [2, D, W_IN], BF16), "w_br_a": dram("wb_a", [2, 1024, D], BF16),
        "w_br_b": dram("wb_b", [2, 1024, D], BF16), "w_br_c": dram("wb_c", [2, 1024, D], BF16),
        "w_out": dram("wb_o", [2, D, D], BF16), "w_mlp1": dram("wb_1", [2, D, 4 * D], BF16),
        "w_mlp2": dram("wb_2", [2, 4 * D, D], BF16),
    }
    WBG = dram("wbg", [2, 48, 128, 16, 128], BF16)
    WB1 = dram("wb1b", [2, 64, 128, 16, 128], BF16)
    WBR = [dram("wbr%d" % b, [2, 16, 128, 8, 128], BF16) for b in range(3)]
    SC = dram("sc", [NTOK, 9216], F32)
    SGA = dram("sga", [2, 16, NTOK], F32)
    H1T = dram("h1t", [16, 128, NTOK], BF16)
    OF = dram("of", [NTOK, 1024], F32)
    OAT = dram("oat", [8, 128, NTOK], BF16)
    OBT = dram("obt", [8, 128, NTOK], BF16)
    OCT = dram("oct", [8, 128, NTOK], BF16)
    QNT = dram("qnt", [8, 128, NTOK], BF16)
    KNT = dram("knt", [8, 128, NTOK], BF16)
    XS = dram("xs", [NTOK, D], F32)
    X1 = dram("x1", [NTOK, D], F32)
    GSC = dram("gsc", [2, 2, 128, D], F32)
    HG = dram("hg", [NTOK, 2048], F32)
    ZF = [dram("zf0", [NTOK, 1024], F32), dram("zf1", [NTOK, 1024], F32)]
    ZB = [dram("zb0", [NTOK, 1024], BF16), dram("zb1", [NTOK, 1024], BF16)]
    HB = {LS: dram("hbL", [LS, 2048], BF16), LP: dram("hbS", [LP, 2048], BF16)}
    HF = {LS: dram("hfL", [2, 24, 128, 2, 1024], F32), LP: dram("hfS", [2, 2, 128, 2, 1024], F32)}

    with ExitStack() as st:
        kb = KB(nc, st)
        cp32 = kb.sb([128, 640], F32, const=True)
        cpb = kb.sb([128, 640], BF16, const=True)
        P = [kb.ps([128, 512], F32) for _ in range(6)]
        PB = [kb.ps([128, 1024], BF16) for _ in range(2)]
        kb.dma("sync", cp32[:], A["cpack"][:, :], w=[cp32])
        kb.cp(cpb[:], cp32[:], [cp32], [cpb])
        IDb = cpb[:, 0:128]
        TRIf = {0: cp32[:, 128:256], 1: cp32[:, 256:384]}
        TRIb = {0: cpb[:, 128:256], 1: cpb[:, 256:384]}
        ONESf = cp32[:, 384:512]
        ONESb = cpb[:, 384:512]
        JREV = cp32[:, 512:640]
        CB = [cp32, cpb]
        rr = {"pb": 0}
        A1 = kb.sb([128, 16, 2], F32, st, name="A1")
        A2 = kb.sb([128, 16, 2], F32, st, name="A2")
        B1 = kb.sb([128, 16, 2], F32, st, name="B1")
        B2 = kb.sb([128, 16, 2], F32, st, name="B2")

        def evac_engine(i):
            return "act" if i % 2 == 0 else "dve"

        def copy_any(i, out, in_, r, w):
            if i % 2 == 0:
                kb.act(out, in_, AF.Copy, r, w)
            else:
                kb.cp(out, in_, r, w)

        for l in range(DEPTH):
            for name in ("w_in", "w_out", "w_mlp2"):
                wb = WB[name]
                src = A[name]
                rows, cols = src.shape[1], src.shape[2]
                if name == "w_in":
                    cols = 9248
                cw = _div_le(cols, 2048)
                for r0 in range(0, rows, 128):
                    s_ap = src[l, r0:r0 + 128, 0:cols].rearrange("p (a b) -> p a b", b=cw)
                    d_ap = wb[l, r0:r0 + 128, 0:cols].rearrange("p (a b) -> p a b", b=cw)
                    kb.dma("pool", d_ap, s_ap)
            for kc in range(16):
                r0 = kc * 128
                kb.dma("pool", WBG[l, :, :, kc, :].rearrange("b p c -> p b c"),
                       A["w_in"][l, r0:r0 + 128, 9248:15392].rearrange("p (b c) -> p b c", c=128))
                kb.dma("pool", WB1[l, :, :, kc, :].rearrange("b p c -> p b c"),
                       A["w_mlp1"][l, r0:r0 + 128, :].rearrange("p (b c) -> p b c", c=128))
            for b, nm in enumerate(("w_br_a", "w_br_b", "w_br_c")):
                for kc in range(8):
                    r0 = kc * 128
                    kb.dma("pool", WBR[b][l, :, :, kc, :].rearrange("b p c -> p b c"),
                           A[nm][l, r0:r0 + 128, :].rearrange("p (b c) -> p b c", c=128))
        kb.barrier()
        stop_flag = {'s': STOP == 'cast'}

        def norm_to_T(xt, hT, col0, Am, Bm, j, tmp):
            _, ssq, rstd, xn = tmp
            kb.act(xn[:], xt[:], AF.Square, [xt], [xn, ssq], accum_out=ssq[:, 0:1])
            kb.act(rstd[:], ssq[:, 0:1], AF.Sqrt, [ssq], [rstd], bias=EPS, scale=1.0 / D)
            kb.recip(rstd[:], rstd[:], [rstd], [rstd])
            kb.ts(xn[:], xt[:], rstd[:, 0:1], None, ALU.mult, None, [xt, rstd], [xn])
            for half in range(2):
                pb = PB[rr["pb"] % 2]
                rr["pb"] += 1
                for q in range(8):
                    kc = half * 8 + q
                    kb.tr(pb[:, q * 128:(q + 1) * 128], xn[:, kc * 128:(kc + 1) * 128], IDb, [xn, cpb], [pb])
                for q in range(8):
                    kc = half * 8 + q
                    kb.act(hT[:, kc, col0:col0 + 128], pb[:, q * 128:(q + 1) * 128], AF.Identity, [pb, Am, Bm], [hT],
                           bias=Bm[:, kc, j:j + 1], scale=Am[:, kc, j:j + 1])

        def layer(l):
            if stop_flag['s']:
                return
            x_in = A["x"] if l == 0 else XS
            with ExitStack() as ph:
                mT = kb.sb([128, 96, 2], F32, ph)
                ct = kb.sb([128, 16, 2], F32, ph)
                sc = kb.sb([128, 16, 2], BF16, ph)
                sB = [kb.sb([128, 16, 128], BF16, ph) for _ in range(2)]
                bmT = kb.sb([128, 96], F32, ph)
                kb.dma("sync", ct[:], A["condT"][:, :, :], w=[ct])
                kb.dma("sync", bmT[:], A["b_modT"][l], w=[bmT])
                ctf = kb.sb([128, 16, 2], F32, ph)
                kb.act(ctf[:], ct[:], AF.Silu, [ct], [ctf])
                kb.cp(sc[:], ctf[:], [ctf], [sc])
                for j in range(2):
                    for kc in range(16):
                        kb.cp(sB[j][:, kc, :], ctf[:, kc, j:j + 1].to_broadcast([128, 128]), [ctf], [sB[j]])
                wm = [kb.sb([128, 16, 512], BF16, ph) for _ in range(2)]
                bb = [kb.sb([128, 512], F32, ph) for _ in range(2)]
                gst = [kb.sb([128, 512], F32, ph) for _ in range(2)]
                wsrc = A["w_mod"][l].rearrange("(kc p) c -> p kc c", p=128)
                gi = 0
                for cb in range(24):
                    w_ = wm[cb % 2]
                    kb.dma("pool", w_[:], wsrc[:, :, cb * 512:(cb + 1) * 512], w=[w_])
                    for sub in range(4):
                        ch = cb * 4 + sub
                        pp = P[ch % 2]
                        kb.mm(pp[:, 0:2], [(w_[:, kc, sub * 128:(sub + 1) * 128], sc[:, kc, :]) for kc in range(16)], [w_, sc], [pp])
                        kb.ts(mT[:, ch, :], pp[:, 0:2], bmT[:, ch:ch + 1], None, ALU.add, None, [pp, bmT], [mT])
                    gsel = {8: 0, 9: 0, 10: 0, 11: 0, 20: 1, 21: 1, 22: 1, 23: 1}.get(cb)
                    if gsel is not None:
                        b_ = bb[cb % 2]
                        kb.dma("sync", b_[:], A["b_mod"][l, cb * 512:(cb + 1) * 512].partition_broadcast(128), w=[b_])
                        for j in range(2):
                            pp = P[2 + j]
                            g_ = gst[gi % 2]
                            gi += 1
                            kb.mm(pp[:], [(sB[j][:, kc, :], w_[:, kc, :]) for kc in range(16)], [w_, sB[j]], [pp])
                            kb.tt(g_[:], pp[:], b_[:], ALU.add, [pp, b_], [g_])
                            c0 = (cb % 4) * 512
                            kb.dma("pool", GSC[j, gsel, :, c0:c0 + 512], g_[:], r=[g_])
                g1 = kb.sb([128, 16], F32, ph)
                g2 = kb.sb([128, 16], F32, ph)
                kb.dma("sync", g1[:], A["n1T"][l], w=[g1])
                kb.dma("sync", g2[:], A["n2T"][l], w=[g2])
                for j in range(2):
                    kb.stt(A1[:, :, j], mT[:, 16:32, j], 1.0, g1[:], ALU.add, ALU.mult, [mT, g1], [A1])
                    kb.stt(A2[:, :, j], mT[:, 64:80, j], 1.0, g2[:], ALU.add, ALU.mult, [mT, g2], [A2])
                    kb.cp(B1[:, :, j], mT[:, 0:16, j], [mT], [B1])
                    kb.cp(B2[:, :, j], mT[:, 48:64, j], [mT], [B2])
                kb.barrier()

            if STOP == "M" + str(l):
                stop_flag['s'] = True
                return
            with ExitStack() as ph:
                G = 1024
                hT = kb.sb([128, 16, G], BF16, ph)
                xt2 = [kb.sb([128, D], F32, ph) for _ in range(2)]
                tmp = (None, kb.sb([128, 1], F32, ph), kb.sb([128, 1], F32, ph), kb.sb([128, D], BF16, ph))
                wbuf = [kb.sb([128, 16, 512], BF16, ph) for _ in range(3)]
                stg = [kb.sb([128, 512], F32, ph) for _ in range(4)]
                wga = kb.sb([128, 16, 32], BF16, ph)
                gst_ = [kb.sb([16, 512], F32, ph) for _ in range(2)]
                wsrc = WB["w_in"][l].rearrange("(kc p) c -> p kc c", p=128)
                kb.dma("sync", wga[:], wsrc[:, :, 3072:3104], w=[wga])
                si = 0
                wi = 0
                for g in range(NTOK // G):
                    j = 0 if g < 4 else 1
                    for ti in range(8):
                        tile_i = g * 8 + ti
                        xt = xt2[tile_i % 2]
                        kb.dma("sync", xt[:], x_in[tile_i * 128:(tile_i + 1) * 128, :], w=[xt])
                        norm_to_T(xt, hT, ti * 128, A1, B1, j, tmp)
                    kb.dma("pool", H1T[:, :, g * G:(g + 1) * G].rearrange("kc p t -> p kc t"), hT[:], r=[hT])
                    for blk in range(18):
                        wcol = blk * 512 if blk < 6 else blk * 512 + 32
                        w_ = wbuf[wi % 3]
                        wi += 1
                        kb.dma("sync", w_[:], wsrc[:, :, wcol:wcol + 512], w=[w_])
                        for ti in range(8):
                            tile_i = g * 8 + ti
                            pp = P[si % 4]
                            s_ = stg[si % 4]
                            kb.mm(pp[:], [(hT[:, kc, ti * 128:(ti + 1) * 128], w_[:, kc, :]) for kc in range(16)], [hT, w_], [pp])
                            copy_any(si, s_[:], pp[:], [pp], [s_])
                            si += 1
                            kb.dma("pool", SC[tile_i * 128:(tile_i + 1) * 128, blk * 512:(blk + 1) * 512], s_[:], r=[s_])
                            if j == 1 and blk in (10, 11):
                                pi = (tile_i - 32) // 2
                                s0 = ((tile_i - 32) % 2) * 128
                                for hh in range(4):
                                    h = (blk - 10) * 4 + hh
                                    kb.dma("pool", OA["nv"][pi, l, h, s0:s0 + 128, :], s_[:, hh * 128:(hh + 1) * 128], r=[s_])
                    for dr in range(2):
                        for tq in range(G // 512):
                            pp = P[4 + (tq % 2)]
                            g_ = gst_[tq % 2]
                            kb.mm(pp[0:16, :], [(wga[:, kc, dr * 16:(dr + 1) * 16], hT[:, kc, tq * 512:(tq + 1) * 512]) for kc in range(16)],
                                  [hT, wga], [pp])
                            kb.cp(g_[:], pp[0:16, :], [pp], [g_])
                            kb.dma("pool", SGA[dr, :, g * G + tq * 512: g * G + (tq + 1) * 512], g_[:], r=[g_])
                kb.barrier()

            if STOP == "A" + str(l):
                stop_flag['s'] = True
                return
            for nm, fn in (("gla", gla_phase), ("na", na_phase), ("hy", hyena_phase), ("C", lambda l_: phase_c(l_, A2, B2))):
                if stop_flag['s']:
                    return
                fn(l)
                if STOP == nm + str(l):
                    stop_flag['s'] = True

        def gla_phase(l):
            with ExitStack() as ph:
                wa = kb.sb([17, 2, 512], F32, ph)
                kb.dma("sync", wa[0:16, :, :], A["wa2"][l].rearrange("d r c -> r d c"), w=[wa])
                kb.dma("sync", wa[16:17, :, :], A["ba2"][l:l + 1, :, :], w=[wa])
                gn = kb.sb([128, 1024], F32, ph)
                kb.dma("sync", gn[:], A["gng4"][l].partition_broadcast(128), w=[gn])
                S = [kb.sb([128, 256], F32, ph) for _ in range(4)]
                Sb = [kb.sb([128, 256], BF16, ph) for _ in range(4)]
                NB = 2
                qf = [kb.sb([128, 512], F32, ph) for _ in range(NB)]
                kf = [kb.sb([128, 512], F32, ph) for _ in range(NB)]
                vf = [kb.sb([128, 1024], F32, ph) for _ in range(NB)]
                vb = [kb.sb([128, 1024], BF16, ph) for _ in range(NB)]
                gaT = [kb.sb([17, 128], F32, ph) for _ in range(NB)]
                for g_ in gaT:
                    kb.memset(g_[:], 1.0, [g_])
                cs4 = [kb.sb([128, 256], F32, ph) for _ in range(NB)]
                sn4 = [kb.sb([128, 256], F32, ph) for _ in range(NB)]
                e1 = kb.sb([128, 512], F32, ph)
                sp = kb.sb([128, 512], F32, ph)
                tots = kb.sb([128, 512], F32, ph)
                dd = kb.sb([128, 512], F32, ph)
                EQ = kb.sb([128, 512], F32, ph)
                EK = kb.sb([128, 512], F32, ph)
                EH = kb.sb([128, 512], F32, ph)
                dec = kb.sb([128, 4], F32, ph)
                qr = kb.sb([128, 512], F32, ph)
                kr = kb.sb([128, 512], F32, ph)
                t1 = kb.sb([128, 256], F32, ph)
                t2 = kb.sb([128, 256], F32, ph)
                qt = kb.sb([128, 512], BF16, ph)
                kt = kb.sb([128, 512], BF16, ph)
                kh = kb.sb([128, 512], BF16, ph)
                qT = kb.sb([128, 512], BF16, ph)
                kT = kb.sb([128, 512], BF16, ph)
                AT = [kb.sb([128, 128], BF16, ph) for _ in range(2)]
                ost = kb.sb([128, 1024], F32, ph)
                ofl = kb.sb([128, 1024], F32, ph)
                gg = kb.sb([128, 1024], F32, ph)
                gs = kb.sb([128, 1024], F32, ph)
                junk = kb.sb([128, 256], F32, ph)
                ss4 = kb.sb([128, 4], F32, ph)
                rs4 = kb.sb([128, 4], F32, ph)
                oa = kb.sb([128, 1024], BF16, ph)
                oaT = kb.sb([128, 8, 128], BF16, ph)

                def rope(dst, src, c_, s_):
                    s3 = src[:].rearrange("p (h d) -> p h d", h=4)
                    d3 = dst[:].rearrange("p (h d) -> p h d", h=4)
                    c3 = c_[:].rearrange("p (h d) -> p h d", h=4)
                    n3 = s_[:].rearrange("p (h d) -> p h d", h=4)
                    a3 = t1[:].rearrange("p (h d) -> p h d", h=4)
                    b3 = t2[:].rearrange("p (h d) -> p h d", h=4)
                    kb.tt(a3, s3[:, :, 0:64], c3, ALU.mult, [src, c_], [t1])
                    kb.tt(b3, s3[:, :, 64:128], n3, ALU.mult, [src, s_], [t2], eng="pool")
                    kb.tt(d3[:, :, 0:64], a3, b3, ALU.subtract, [t1, t2], [dst])
                    kb.tt(a3, s3[:, :, 0:64], n3, ALU.mult, [src, s_], [t1])
                    kb.tt(b3, s3[:, :, 64:128], c3, ALU.mult, [src, c_], [t2], eng="pool")
                    kb.tt(d3[:, :, 64:128], a3, b3, ALU.add, [t1, t2], [dst])

                seqs = [(0, 32, True, None)] + [(32 + 2 * p, 2, False, p) for p in range(4)]
                if os.environ.get('KSEQ') == 'p':
                    seqs = seqs[1:]
                cnt = [0]
                kh2 = [kh, kb.sb([128, 512], BF16, ph)]
                qT2 = [qT, kb.sb([128, 512], BF16, ph)]
                kT2 = [kT, kb.sb([128, 512], BF16, ph)]
                dec2 = [dec, kb.sb([128, 4], F32, ph)]

                def prep(tile_i, dr, latent):
                    r0 = tile_i * 128
                    bi = cnt[0] % NB
                    cnt[0] += 1
                    q_, k_, v_, vb_, ga_ = qf[bi], kf[bi], vf[bi], vb[bi], gaT[bi]
                    kh_, qT_, kT_, dec_ = kh2[bi], qT2[bi], kT2[bi], dec2[bi]
                    kb.dma("sync", q_[:], SC[r0:r0 + 128, 0:512], w=[q_])
                    kb.dma("sync", k_[:], SC[r0:r0 + 128, 512:1024], w=[k_])
                    kb.dma("sync", v_[:], SC[r0:r0 + 128, 1024:2048], w=[v_])
                    kb.dma("sync", ga_[0:16, :], SGA[dr, :, r0:r0 + 128], w=[ga_])
                    if latent:
                        c_, s_ = cs4[bi], sn4[bi]
                        kb.dma("sync", c_[:], A["cos4"][r0:r0 + 128, :], w=[c_])
                        kb.dma("sync", s_[:], A["sin4"][r0:r0 + 128, :], w=[s_])
                    kb.act(vb_[:], v_[:], AF.Copy, [v_], [vb_])
                    kb.mm(P[0][:], [(ga_[:], wa[:, dr, :])], [ga_, wa], [P[0]])
                    kb.act(e1[:], P[0][:], AF.Exp, [P[0]], [e1], scale=-1.0)
                    kb.act(sp[:], e1[:], AF.Ln, [e1], [sp], bias=1.0)
                    kb.mm(P[1][:], [(TRIf[dr], sp[:])], [sp, cp32], [P[1]])
                    kb.mm(P[2][:], [(ONESf, sp[:])], [sp, cp32], [P[2]])
                    for h in range(4):
                        kb.mm(P[0][:, 2 * h:2 * h + 2], [(sp[:, h * 128:(h + 1) * 128], ONESf[:, 0:2])], [sp, cp32], [P[0]])
                    kb.act(dec_[:], P[0][:, 0:8:2], AF.Exp, [P[0]], [dec_], scale=-1.0 / 16)
                    kb.act(EQ[:], P[1][:], AF.Exp, [P[1]], [EQ], scale=-1.0 / 16)
                    kb.act(EK[:], P[1][:], AF.Exp, [P[1]], [EK], scale=1.0 / 16)
                    kb.act(tots[:], P[2][:], AF.Copy, [P[2]], [tots])
                    kb.tt(dd[:], P[1][:], tots[:], ALU.subtract, [P[1], tots], [dd])
                    kb.act(EH[:], dd[:], AF.Exp, [dd], [EH], scale=1.0 / 16)
                    if latent:
                        rope(qr, q_, c_, s_)
                        rope(kr, k_, c_, s_)
                        qs, ks = qr, kr
                    else:
                        qs, ks = q_, k_
                    kb.stt(qt[:], qs[:], 128.0 ** -0.5, EQ[:], ALU.mult, ALU.mult, [qs, EQ], [qt])
                    kb.tt(kt[:], ks[:], EK[:], ALU.mult, [ks, EK], [kt])
                    kb.tt(kh_[:], ks[:], EH[:], ALU.mult, [ks, EH], [kh_], eng="pool")
                    pb = PB[0]
                    for h in range(4):
                        kb.tr(pb[:, h * 128:(h + 1) * 128], qt[:, h * 128:(h + 1) * 128], IDb, [qt, cpb], [pb])
                        kb.tr(pb[:, 512 + h * 128:512 + (h + 1) * 128], kt[:, h * 128:(h + 1) * 128], IDb, [kt, cpb], [pb])
                    kb.act(qT_[:], pb[:, 0:512], AF.Identity, [pb], [qT_])
                    kb.act(kT_[:], pb[:, 512:1024], AF.Identity, [pb], [kT_])
                    return (r0, vb_, kh_, qT_, kT_, dec_)

                def heads(ctx, dr):
                    r0, vb_, kh_, qT_, kT_, dec_ = ctx
                    if dr == 1:
                        kb.dma("sync", ofl[:], OF[r0:r0 + 128, :], w=[ofl])
                        kb.dma("sync", gg[:], SC[r0:r0 + 128, 2048:3072], w=[gg])
                    for h in range(4):
                        hs = slice(h * 128, (h + 1) * 128)
                        at = AT[h % 2]
                        kb.mm(P[3][:, 0:128], [(kT_[:, hs], qT_[:, hs])], [kT_, qT_], [P[3]])
                        kb.tt(at[:], P[3][:, 0:128], TRIf[dr], ALU.mult, [P[3], cp32], [at])
                        po = P[4 + h // 2]
                        oc = (h % 2) * 256
                        kb.mm(po[:, oc:oc + 256], [(at[:], vb_[:, h * 256:(h + 1) * 256]), (qT_[:, hs], Sb[h][:])],
                              [at, vb_, qT_, Sb[h]], [po])
                        kb.mm(P[3][:, 256:512], [(kh_[:, hs], vb_[:, h * 256:(h + 1) * 256])], [kh_, vb_], [P[3]])
                        kb.stt(S[h][:], S[h][:], dec_[:, h:h + 1], P[3][:, 256:512], ALU.mult, ALU.add, [S[h], dec_, P[3]], [S[h]])
                        kb.act(Sb[h][:], S[h][:], AF.Copy, [S[h]], [Sb[h]])
                    if dr == 0:
                        kb.act(ost[:, 0:512], P[4][:], AF.Copy, [P[4]], [ost])
                        kb.cp(ost[:, 512:1024], P[5][:], [P[5]], [ost])
                        kb.dma("pool", OF[r0:r0 + 128, :], ost[:], r=[ost])
                    else:
                        kb.tt(ost[:, 0:512], P[4][:], ofl[:, 0:512], ALU.add, [P[4], ofl], [ost])
                        kb.tt(ost[:, 512:1024], P[5][:], ofl[:, 512:1024], ALU.add, [P[5], ofl], [ost])
                        for h in range(4):
                            kb.act(junk[:], ost[:, h * 256:(h + 1) * 256], AF.Square, [ost], [junk, ss4], accum_out=ss4[:, h:h + 1])
                        kb.act(rs4[:], ss4[:], AF.Sqrt, [ss4], [rs4], bias=EPS, scale=1.0 / 256)
                        kb.recip(rs4[:], rs4[:], [rs4], [rs4])
                        kb.act(gs[:], gg[:], AF.Silu, [gg], [gs])
                        kb.tt(gs[:], gs[:], gn[:], ALU.mult, [gs, gn], [gs], eng="pool")
                        for h in range(4):
                            hs2 = slice(h * 256, (h + 1) * 256)
                            kb.stt(oa[:, hs2], ost[:, hs2], rs4[:, h:h + 1], gs[:, hs2], ALU.mult, ALU.mult, [ost, rs4, gs], [oa])
                        pb = PB[1]
                        for q in range(8):
                            kb.tr(pb[:, q * 128:(q + 1) * 128], oa[:, q * 128:(q + 1) * 128], IDb, [oa, cpb], [pb])
                        kb.act(oaT[:].rearrange("p a b -> p (a b)"), pb[:], AF.Identity, [pb], [oaT])
                        kb.dma("pool", OAT[:, :, r0:r0 + 128].rearrange("kc p t -> p kc t"), oaT[:], r=[oaT])

                for (tile0, ntile, latent, pidx) in seqs:
                    for dr in range(2):
                        for h in range(4):
                            if latent:
                                kb.dma("sync", S[h][:], A["state0"][l, dr, h], w=[S[h]])
                            else:
                                kb.memset(S[h][:], 0.0, [S[h]])
                            kb.cp(Sb[h][:], S[h][:], [S[h]], [Sb[h]], eng="pool")
                        order = list(range(ntile)) if dr == 0 else list(range(ntile - 1, -1, -1))
                        cur = prep(tile0 + order[0], dr, latent)
                        for idx in range(len(order)):
                            nxt = prep(tile0 + order[idx + 1], dr, latent) if idx + 1 < len(order) else None
                            heads(cur, dr)
                            cur = nxt
                        if not latent:
                            for h in range(4):
                                kb.dma("pool", OA["nst"][pidx, l, dr, h], S[h][:], r=[S[h]])
                        kb.barrier()

        def na_phase(l):
            with ExitStack() as ph:
                qg = kb.sb([128, 1024], F32, ph)
                kg = kb.sb([128, 1024], F32, ph)
                kb.dma("sync", qg[:], A["qg8"][l].partition_broadcast(128), w=[qg])
                kb.dma("sync", kg[:], A["kg8"][l].partition_broadcast(128), w=[kg])
                xin = [[kb.sb([128, 1024], F32, ph) for _ in range(2)] for _ in range(2)]
                sq = [kb.sb([128, 1024], F32, ph) for _ in range(2)]
                ss8 = [kb.sb([128, 8], F32, ph) for _ in range(2)]
                rs8 = [kb.sb([128, 8], F32, ph) for _ in range(2)]
                nf = [kb.sb([128, 1024], F32, ph) for _ in range(2)]
                nb = [kb.sb([128, 1024], BF16, ph) for _ in range(2)]
                nT = [kb.sb([128, 8, 128], BF16, ph) for _ in range(2)]

                def s1(k):
                    tile_i, which = k // 2, k % 2
                    r0 = tile_i * 128
                    b = k % 2
                    xi = xin[which][tile_i % 2]
                    c0 = 3072 + which * 1024
                    kb.dma("sync", xi[:], SC[r0:r0 + 128, c0:c0 + 1024], w=[xi])
                    kb.act(sq[b][:], xi[:], AF.Square, [xi], [sq[b]])
                    kb.issue("dve", lambda e: e.tensor_reduce(out=ss8[b][:], in_=sq[b][:].rearrange("p (h d) -> p h d", h=8),
                                                              axis=AX.X, op=ALU.add), [sq[b]], [ss8[b]])
                    kb.act(rs8[b][:], ss8[b][:], AF.Sqrt, [ss8[b]], [rs8[b]], bias=EPS, scale=1.0 / 128)
                    kb.recip(rs8[b][:], rs8[b][:], [rs8[b]], [rs8[b]])
                    n_ = nf[b]
                    gsel = qg if which == 0 else kg
                    for h in range(8):
                        hs = slice(h * 128, (h + 1) * 128)
                        kb.stt(n_[:, hs], xi[:, hs], rs8[b][:, h:h + 1], gsel[:, hs], ALU.mult, ALU.mult, [xi, rs8[b], gsel], [n_])
                    if which == 1 and tile_i >= 32:
                        pi = (tile_i - 32) // 2
                        s0 = ((tile_i - 32) % 2) * 128
                        kb.dma("pool", OA["nk"][pi, l, :, s0:s0 + 128, :].rearrange("h s d -> s h d"),
                               n_[:].rearrange("p (h d) -> p h d", h=8), r=[n_])
                    if which == 0:
                        kb.act(nb[b][:], n_[:], AF.Copy, [n_], [nb[b]], scale=128.0 ** -0.5)
                    else:
                        kb.act(nb[b][:], n_[:], AF.Copy, [n_], [nb[b]])

                def s2(k):
                    tile_i, which = k // 2, k % 2
                    r0 = tile_i * 128
                    b = k % 2
                    pb = PB[b]
                    t_ = nT[b]
                    for h in range(8):
                        kb.tr(pb[:, h * 128:(h + 1) * 128], nb[b][:, h * 128:(h + 1) * 128], IDb, [nb[b], cpb], [pb])
                    kb.act(t_[:].rearrange("p a b -> p (a b)"), pb[:], AF.Identity, [pb], [t_])
                    dst = QNT if which == 0 else KNT
                    kb.dma("pool", dst[:, :, r0:r0 + 128].rearrange("h p t -> p h t"), t_[:], r=[t_])

                s1(0)
                for k in range(2 * NT):
                    if k + 1 < 2 * NT:
                        s1(k + 1)
                    s2(k)
                kb.barrier()
            with ExitStack() as ph:
                BT = kb.sb([128, 14, 64], F32, ph)
                BTr = kb.sb([128, 14, 64], F32, ph)
                cm = kb.sb([128, 64], F32, ph)
                kb.dma("sync", cm[:], A["colmask"][:, :], w=[cm])
                qTh = [kb.sb([128, NTOK], BF16, ph) for _ in range(2)]
                kTh = [kb.sb([128, NTOK], BF16, ph) for _ in range(2)]
                vh = [kb.sb([128, NT, 128], BF16, ph) for _ in range(2)]
                kcf = kb.sb([128, 2, 128], BF16, ph)
                kcT = [kb.sb([128, 256], BF16, ph) for _ in range(2)]
                vc = [kb.sb([128, 2, 128], BF16, ph) for _ in range(2)]
                sl = [kb.sb([128, 4, 64], F32, ph) for _ in range(2)]
                El = [kb.sb([128, 4, 64], BF16, ph) for _ in range(2)]
                Ec = [kb.sb([128, 2, 64], BF16, ph) for _ in range(2)]
                Ed = [kb.sb([128, 2, 256], BF16, ph) for _ in range(2)]
                den = [kb.sb([128, 256], F32, ph) for _ in range(2)]
                obT = [kb.sb([128, NTOK], BF16, ph) for _ in range(2)]
                it = 0
                for h in range(8):
                    hb = h % 2
                    q_, k_, v_, kc_, vc_, ob_ = qTh[hb], kTh[hb], vh[hb], kcT[hb], vc[hb], obT[hb]
                    kb.dma("sync", q_[:], QNT[h], w=[q_])
                    kb.dma("sync", k_[:], KNT[h], w=[k_])
                    kb.dma("pool", v_[:], SC[:, 5120 + h * 128:5120 + (h + 1) * 128].rearrange("(t p) d -> p t d", p=128), w=[v_])
                    kb.dma("pool", kcf[:], A["ck"][l, h].rearrange("(t p) d -> p t d", p=128), w=[kcf])
                    kb.dma("pool", vc_[:], A["cv"][l, h].rearrange("(t p) d -> p t d", p=128), w=[vc_])
                    pb = PB[h % 2]
                    for t_ in range(2):
                        kb.tr(pb[:, t_ * 128:(t_ + 1) * 128], kcf[:, t_, :], IDb, [kcf, cpb], [pb])
                    kb.act(kc_[:], pb[:, 0:256], AF.Identity, [pb], [kc_])
                    for d0 in range(14):
                        for di in range(2):
                            src = bass.AP(tensor=I["rpbp"], offset=((l * 8 + h) * 15 + d0 + di) * 127, ap=[[1, 64], [1, 64]])
                            kb.dma("sync", BTr[di * 64:(di + 1) * 64, d0, :], src, w=[BTr])
                    for hf in range(2):
                        pj = P[hf]
                        kb.mm(pj[:, 0:448], [(JREV, BTr[:, hf * 7:(hf + 1) * 7, :].rearrange("p a b -> p (a b)"))], [cp32, BTr], [pj])
                        kb.tt(BT[:, hf * 7:(hf + 1) * 7, :], pj[:, 0:448].rearrange("p (a b) -> p a b", a=7),
                              cm[:].unsqueeze(1).to_broadcast([128, 7, 64]), ALU.add, [pj, cm], [BT])
                    for r in range(64):
                        rs_ = min(max(r - 4, 0), 56)
                        d0 = rs_ - r + 7
                        qc = slice(r * 64, (r + 1) * 64)
                        i2 = it % 2
                        it += 1
                        pl, pc = P[0 + i2], P[2 + i2]
                        for j in range(4):
                            kc0 = (rs_ + 2 * j) * 64
                            kb.mm(pl[:, j * 64:(j + 1) * 64], [(k_[:, kc0:kc0 + 128], q_[:, qc])], [k_, q_], [pl])
                        for j in range(2):
                            kb.mm(pc[:, j * 64:(j + 1) * 64], [(kc_[:, j * 128:(j + 1) * 128], q_[:, qc])], [kc_, q_], [pc])
                        kb.tt(sl[i2][:], pl[:, 0:256].rearrange("p (j c) -> p j c", j=4), BT[:, d0:d0 + 7:2, :], ALU.add, [pl, BT], [sl[i2]])
                        kb.act(El[i2][:], sl[i2][:], AF.Exp, [sl[i2]], [El[i2]])
                        kb.act(Ec[i2][:].rearrange("p j c -> p (j c)"), pc[:, 0:128], AF.Exp, [pc], [Ec[i2]])
                        po, pd = P[4], P[5]
                        oc = i2 * 64
                        prs = []
                        prd = []
                        for j in range(4):
                            kt_i = (rs_ + 2 * j) // 2
                            prs.append((v_[:, kt_i, :], El[i2][:, j, :]))
                            prd.append((ONESb, El[i2][:, j, :]))
                        for j in range(2):
                            prs.append((vc_[:, j, :], Ec[i2][:, j, :]))
                            prd.append((ONESb, Ec[i2][:, j, :]))
                        kb.mm(po[:, oc:oc + 64], prs, [v_, vc_, El[i2], Ec[i2]], [po])
                        kb.mm(pd[:, oc:oc + 64], prd, [cpb, El[i2], Ec[i2]], [pd])
                        kb.recip(den[i2][:, 0:64], pd[:, oc:oc + 64], [pd], [den[i2]])
                        kb.tt(ob_[:, qc], po[:, oc:oc + 64], den[i2][:, 0:64], ALU.mult, [po, den[i2]], [ob_])
                    for p in range(4):
                        t0 = LS + p * 256
                        i2 = it % 2
                        it += 1
                        pl = P[0 + i2]
                        for j in range(2):
                            kb.mm(pl[:, j * 256:(j + 1) * 256], [(k_[:, t0 + j * 128:t0 + (j + 1) * 128], q_[:, t0:t0 + 256])], [k_, q_], [pl])
                        kb.act(Ed[i2][:].rearrange("p j c -> p (j c)"), pl[:], AF.Exp, [pl], [Ed[i2]])
                        po, pd = P[2 + i2], P[4 + i2]
                        kb.mm(po[:, 0:256], [(v_[:, 32 + 2 * p + j, :], Ed[i2][:, j, :]) for j in range(2)], [v_, Ed[i2]], [po])
                        kb.mm(pd[:, 0:256], [(ONESb, Ed[i2][:, j, :]) for j in range(2)], [cpb, Ed[i2]], [pd])
                        kb.recip(den[i2][:], pd[:, 0:256], [pd], [den[i2]])
                        kb.tt(ob_[:, t0:t0 + 256], po[:, 0:256], den[i2][:], ALU.mult, [po, den[i2]], [ob_])
                    kb.dma("pool", OBT[h], ob_[:], r=[ob_])
                kb.barrier()

        def hyena_phase(l):
            TWO_PI = 2.0 * math.pi
            def fwd_dft(ph, zb, n_sc, n_ft, FC, FS, ftab, consume):
                for ft in range(n_ft):
                    fc_, fs_ = ftab[0][ft % 3], ftab[1][ft % 3]
                    kb.dma("sync", fc_[:, 0:n_sc * 128], FC[ft], w=[fc_])
                    kb.dma("sync", fs_[:, 0:n_sc * 128], FS[ft], w=[fs_])
                    pre, pim = P[(ft % 2) * 2], P[(ft % 2) * 2 + 1]
                    kb.mm(pre[:], [(fc_[:, s * 128:(s + 1) * 128], zb[:, s, :]) for s in range(n_sc)], [fc_, zb], [pre])
                    kb.mm(pim[:], [(fs_[:, s * 128:(s + 1) * 128], zb[:, s, :]) for s in range(n_sc)], [fs_, zb], [pim])
                    consume(ft, pre, pim)

            for (L, zpos, ndist) in ((LS, A["zposL"], A["ndistL"]), (LP, A["zposS"], A["ndistS"])):
                n_tt = L // 128
                FC, FS = (A["FCL"], A["FSL"]) if L == LS else (A["FCS"], A["FSS"])
                with ExitStack() as ph:
                    zp = kb.sb([33, L], F32, ph)
                    kb.dma("sync", zp[:], zpos[:, :], w=[zp])
                    f1w = kb.sb([33, 64], F32, ph)
                    f2w = kb.sb([64, 64], F32, ph)
                    f3w = kb.sb([64, 2048], F32, ph)
                    fb = kb.sb([64, 4], F32, ph)
                    kb.dma("sync", f1w[:], A["f1w"][l], w=[f1w])
                    kb.dma("sync", f2w[:], A["f2w"][l], w=[f2w])
                    kb.dma("sync", f3w[:], A["f3w"][l], w=[f3w])
                    kb.dma("sync", fb[:, 0:1], A["f1b"][l], w=[fb])
                    kb.dma("sync", fb[:, 1:2], A["f2b"][l], w=[fb])
                    kb.dma("sync", fb[:, 2:3], A["hfr"][l], w=[fb])
                    sc_ = kb.sb([64, 4], F32, ph)
                    kb.ts(sc_[:, 0:1], fb[:, 2:3], 1.0 / TWO_PI, None, ALU.mult, None, [fb], [sc_])
                    for q in range(2):
                        kb.ts(sc_[:, 1 + q:2 + q], fb[:, q:q + 1], sc_[:, 0:1], 8.0, ALU.mult, ALU.add, [fb, sc_], [sc_])
                    h2T = kb.sb([64, L], F32, ph)
                    h1c = kb.sb([64, 512], F32, ph)
                    u = kb.sb([64, 512], F32, ph)
                    ki = kb.sb([64, 512], I32, ph)
                    kf_ = kb.sb([64, 512], F32, ph)
                    gq_ = kb.sb([64, 512], F32, ph)

                    def sine_layer(out_ap, ps_ap, q, CW):
                        kb.ts(u[:, 0:CW], ps_ap, sc_[:, 0:1], sc_[:, 1 + q:2 + q], ALU.mult, ALU.add, [P[0], sc_], [u])
                        kb.cp(ki[:, 0:CW], u[:, 0:CW], [u], [ki])
                        kb.cp(kf_[:, 0:CW], ki[:, 0:CW], [ki], [kf_])
                        kb.tt(u[:, 0:CW], u[:, 0:CW], kf_[:, 0:CW], ALU.subtract, [u, kf_], [u])
                        kb.ts(gq_[:, 0:CW], u[:, 0:CW], 0.5, None, ALU.is_gt, None, [u], [gq_])
                        kb.tt(u[:, 0:CW], u[:, 0:CW], gq_[:, 0:CW], ALU.subtract, [u, gq_], [u])
                        kb.ts(gq_[:, 0:CW], u[:, 0:CW], -0.5, None, ALU.is_lt, None, [u], [gq_])
                        kb.tt(u[:, 0:CW], u[:, 0:CW], gq_[:, 0:CW], ALU.add, [u, gq_], [u])
                        kb.act(out_ap, u[:, 0:CW], AF.Sin, [u], [h1c, h2T], scale=TWO_PI)

                    CW = min(512, L)
                    for c in range(L // CW):
                        cs_ = slice(c * CW, (c + 1) * CW)
                        kb.mm(P[0][0:64, 0:CW], [(f1w[:], zp[:, cs_])], [f1w, zp], [P[0]])
                        sine_layer(h1c[:, 0:CW], P[0][0:64, 0:CW], 0, CW)
                        kb.mm(P[0][0:64, 0:CW], [(f2w[:], h1c[:, 0:CW])], [f2w, h1c], [P[0]])
                        sine_layer(h2T[:, cs_], P[0][0:64, 0:CW], 1, CW)
                    dl = kb.sb([128, 1024], F32, ph)
                    nd = kb.sb([128, n_tt], F32, ph)
                    kb.dma("sync", dl[:], A["deltas"].partition_broadcast(128), w=[dl])
                    kb.dma("sync", nd[:], ndist[:, :], w=[nd])
                    win = kb.sb([128, 1024], F32, ph)
                    hw = [kb.sb([128, 512], F32, ph) for _ in range(2)]
                    hsq = [kb.sb([128, 512], F32, ph) for _ in range(2)]
                    hbf = [kb.sb([128, 512], BF16, ph) for _ in range(2)]
                    rn = kb.sb([128, 2048], F32, ph)
                    ssb = kb.sb([128, 2048], F32, ph)
                    kb.memset(ssb[:], 0.0, [ssb])
                    k = 0
                    for tt_ in range(n_tt):
                        kb.act(win[:], dl[:], AF.Exp, [dl, nd], [win], scale=nd[:, tt_:tt_ + 1])
                        for cb in range(4):
                            i2 = k % 2
                            k += 1
                            pp = P[i2]
                            kb.mm(pp[:], [(h2T[:, tt_ * 128:(tt_ + 1) * 128], f3w[:, cb * 512:(cb + 1) * 512])], [h2T, f3w], [pp])
                            kb.tt(hw[i2][:], pp[:], win[:, (cb % 2) * 512:(cb % 2 + 1) * 512], ALU.mult, [pp, win], [hw[i2]])
                            kb.act(hsq[i2][:], hw[i2][:], AF.Square, [hw[i2]], [hsq[i2]])
                            kb.act(hbf[i2][:], hw[i2][:], AF.Copy, [hw[i2]], [hbf[i2]])
                            kb.dma("pool", HB[L][tt_ * 128:(tt_ + 1) * 128, cb * 512:(cb + 1) * 512], hbf[i2][:], r=[hbf[i2]])
                            p2 = P[2 + i2]
                            kb.mm(p2[:], [(ONESf, hsq[i2][:])], [cp32, hsq[i2]], [p2])
                            kb.tt(ssb[:, cb * 512:(cb + 1) * 512], ssb[:, cb * 512:(cb + 1) * 512], p2[:], ALU.add, [ssb, p2], [ssb])
                    kb.act(rn[:], ssb[:], AF.Sqrt, [ssb], [rn], bias=EPS)
                    kb.recip(rn[:], rn[:], [rn], [rn])
                    kb.barrier()
                    zbt = [kb.sb([128, n_tt, 512], BF16, ph) for _ in range(1)]
                    ftab = ([kb.sb([128, 4096], BF16, ph) for _ in range(3)], [kb.sb([128, 4096], BF16, ph) for _ in range(3)])
                    hst = [kb.sb([128, 2, 512], F32, ph) for _ in range(2)]
                    kk = [0]
                    for o in range(2):
                        for chalf in range(2):
                            c0 = o * 1024 + chalf * 512
                            zb = zbt[0]
                            kb.dma("sync", zb[:], HB[L][:, c0:c0 + 512].rearrange("(s p) c -> p s c", p=128), w=[zb])

                            def consume(ft, pre, pim, o=o, chalf=chalf, c0=c0):
                                hs_ = hst[kk[0] % 2]
                                kk[0] += 1
                                kb.tt(hs_[:, 0, :], pre[:], rn[:, c0:c0 + 512], ALU.mult, [pre, rn], [hs_])
                                kb.tt(hs_[:, 1, :], pim[:], rn[:, c0:c0 + 512], ALU.mult, [pim, rn], [hs_])
                                kb.dma("pool", HF[L][o, ft, :, :, chalf * 512:(chalf + 1) * 512], hs_[:], r=[hs_])
                            fwd_dft(ph, zb, n_tt, NFT[L], FC, FS, ftab, consume)
                    kb.barrier()

            with ExitStack() as ph:
                cw = kb.sb([128, 3, 3072], F32, ph)
                cbias = kb.sb([128, 3072], F32, ph)
                for j in range(3):
                    kb.dma("sync", cw[:, j, :], A["hcw"][l, j].partition_broadcast(128), w=[cw])
                kb.dma("sync", cbias[:], A["hcb"][l].partition_broadcast(128), w=[cbias])
                um = [kb.sb([128, 3072], F32, ph) for _ in range(2)]
                uc = [kb.sb([128, 3072], F32, ph) for _ in range(2)]
                up = [kb.sb([128, 3072], F32, ph) for _ in range(2)]
                acc = [kb.sb([128, 3072], F32, ph) for _ in range(1)]
                tm = [kb.sb([128, 3072], F32, ph) for _ in range(1)]
                zbf = [kb.sb([128, 1024], BF16, ph) for _ in range(2)]
                for tile_i in range(NT):
                    i2 = tile_i % 2
                    r0 = tile_i * 128
                    if tile_i < 32:
                        s_lo, s_hi = 0, LS
                    else:
                        s_lo = LS + ((tile_i - 32) // 2) * 256
                        s_hi = s_lo + 256
                    a_, b_, c_ = um[i2], uc[i2], up[i2]
                    kb.dma("sync", b_[:], SC[r0:r0 + 128, 6144:9216], w=[b_])
                    if r0 == s_lo:
                        kb.memset(a_[0:1, :], 0.0, [a_])
                        kb.dma("sync", a_[1:128, :], SC[r0:r0 + 127, 6144:9216], w=[a_])
                    else:
                        kb.dma("sync", a_[:], SC[r0 - 1:r0 + 127, 6144:9216], w=[a_])
                    if r0 + 128 == s_hi:
                        kb.memset(c_[:], 0.0, [c_])
                        kb.dma("sync", c_[0:127, :], SC[r0 + 1:r0 + 128, 6144:9216], w=[c_])
                    else:
                        kb.dma("sync", c_[:], SC[r0 + 1:r0 + 129, 6144:9216], w=[c_])
                    ac, t_ = acc[0], tm[0]
                    kb.tt(ac[:], a_[:], cw[:, 0, :], ALU.mult, [a_, cw], [ac])
                    kb.tt(t_[:], b_[:], cw[:, 1, :], ALU.mult, [b_, cw], [t_], eng="pool")
                    kb.tt(ac[:], ac[:], cbias[:], ALU.add, [ac, cbias], [ac])
                    kb.tt(ac[:], ac[:], t_[:], ALU.add, [ac, t_], [ac])
                    kb.tt(t_[:], c_[:], cw[:, 2, :], ALU.mult, [c_, cw], [t_], eng="pool")
                    kb.tt(ac[:], ac[:], t_[:], ALU.add, [ac, t_], [ac])
                    kb.act(zbf[i2][:], ac[:, 2048:3072], AF.Copy, [ac], [zbf[i2]])
                    kb.dma("pool", HG[r0:r0 + 128, :], ac[:, 0:2048], r=[ac])
                    kb.dma("pool", ZF[0][r0:r0 + 128, :], ac[:, 2048:3072], r=[ac])
                    kb.dma("pool", ZB[0][r0:r0 + 128, :], zbf[i2][:], r=[zbf[i2]])
                kb.barrier()

            with ExitStack() as ph:
                hbs = kb.sb([128, 2, 1024], F32, ph)
                for o in range(2):
                    kb.dma("sync", hbs[:, o, :], A["hbias"][l, o].partition_broadcast(128), w=[hbs])
                zbt = [kb.sb([128, 32, 512], BF16, ph) for _ in range(1)]
                ftab = ([kb.sb([128, 4096], BF16, ph) for _ in range(3)], [kb.sb([128, 4096], BF16, ph) for _ in range(3)])
                Yf = kb.sb([128, 24, 2, 512], BF16, ph)
                hft = [kb.sb([128, 2, 512], F32, ph) for _ in range(2)]
                m1 = kb.sb([128, 512], F32, ph)
                m2 = kb.sb([128, 512], F32, ph)
                gt = [kb.sb([128, 512], F32, ph) for _ in range(2)]
                zo = [kb.sb([128, 512], F32, ph) for _ in range(2)]
                zn = [kb.sb([128, 512], F32, ph) for _ in range(2)]
                znb = [kb.sb([128, 512], BF16, ph) for _ in range(2)]
                ocT = [kb.sb([128, 4, 128], BF16, ph) for _ in range(2)]
                cc = [0, 0, 0]
                seqs = [(0, LS)] + [(LS + p * 256, LP) for p in range(4)]
                for o in range(2):
                    for (t0, L) in seqs:
                        n_sc = L // 128
                        FC, FS, GC, GS = (A["FCL"], A["FSL"], A["GCL"], A["GSL"]) if L == LS else (A["FCS"], A["FSS"], A["GCS"], A["GSS"])
                        for chalf in range(2):
                            c0 = chalf * 512
                            zb = zbt[0]
                            cc[0] += 1
                            kb.dma("sync", zb[:, 0:n_sc, :], ZB[o][t0:t0 + L, c0:c0 + 512].rearrange("(s p) c -> p s c", p=128), w=[zb])

                            def consume(ft, pre, pim, o=o, L=L, c0=c0):
                                hf_ = hft[cc[1] % 2]
                                cc[1] += 1
                                kb.dma("sync", hf_[:], HF[L][o, ft, :, :, c0:c0 + 512], w=[hf_])
                                kb.tt(m1[:], pre[:], hf_[:, 0, :], ALU.mult, [pre, hf_], [m1])
                                kb.tt(m2[:], pim[:], hf_[:, 1, :], ALU.mult, [pim, hf_], [m2])
                                kb.tt(Yf[:, ft, 0, :], m1[:], m2[:], ALU.subtract, [m1, m2], [Yf])
                                kb.tt(m1[:], pre[:], hf_[:, 1, :], ALU.mult, [pre, hf_], [m1])
                                kb.tt(m2[:], pim[:], hf_[:, 0, :], ALU.mult, [pim, hf_], [m2])
                                kb.tt(Yf[:, ft, 1, :], m1[:], m2[:], ALU.add, [m1, m2], [Yf])
                                if ft == 0:
                                    kb.tt(Yf[0:1, 0, 0, :], pre[0:1, :], hf_[0:1, 0, :], ALU.mult, [pre, hf_], [Yf])
                                    kb.tt(Yf[0:1, 0, 1, :], pim[0:1, :], hf_[0:1, 1, :], ALU.mult, [pim, hf_], [Yf])
                            fwd_dft(ph, zb, n_sc, NFT[L], FC, FS, ftab, consume)
                            for tt_ in range(n_sc):
                                i2 = cc[2] % 2
                                cc[2] += 1
                                r0 = t0 + tt_ * 128
                                gc_, gs_ = ftab[0][tt_ % 3], ftab[1][tt_ % 3]
                                n_ft = NFT[L]
                                kb.dma("sync", gc_[:, 0:n_ft * 128], GC[tt_], w=[gc_])
                                kb.dma("sync", gs_[:, 0:n_ft * 128], GS[tt_], w=[gs_])
                                kb.dma("sync", gt[i2][:], HG[r0:r0 + 128, o * 1024 + c0:o * 1024 + c0 + 512], w=[gt[i2]])
                                kb.dma("sync", zo[i2][:], ZF[o][r0:r0 + 128, c0:c0 + 512], w=[zo[i2]])
                                py = P[4 + i2]
                                prs = []
                                for f in range(n_ft):
                                    prs.append((gc_[:, f * 128:(f + 1) * 128], Yf[:, f, 0, :]))
                                    prs.append((gs_[:, f * 128:(f + 1) * 128], Yf[:, f, 1, :]))
                                kb.mm(py[:], prs, [gc_, gs_, Yf], [py])
                                kb.tt(zo[i2][:], zo[i2][:], hbs[:, o, c0:c0 + 512], ALU.mult, [zo[i2], hbs], [zo[i2]], eng="pool")
                                kb.tt(zn[i2][:], py[:], zo[i2][:], ALU.add, [py, zo[i2]], [zn[i2]])
                                if o == 0:
                                    kb.tt(zn[i2][:], zn[i2][:], gt[i2][:], ALU.mult, [zn[i2], gt[i2]], [zn[i2]])
                                    kb.act(znb[i2][:], zn[i2][:], AF.Copy, [zn[i2]], [znb[i2]])
                                    kb.dma("pool", ZF[1][r0:r0 + 128, c0:c0 + 512], zn[i2][:], r=[zn[i2]])
                                    kb.dma("pool", ZB[1][r0:r0 + 128, c0:c0 + 512], znb[i2][:], r=[znb[i2]])
                                else:
                                    kb.tt(znb[i2][:], zn[i2][:], gt[i2][:], ALU.mult, [zn[i2], gt[i2]], [znb[i2]])
                                    pb = PB[i2]
                                    for q in range(4):
                                        kb.tr(pb[:, q * 128:(q + 1) * 128], znb[i2][:, q * 128:(q + 1) * 128], IDb, [znb[i2], cpb], [pb])
                                    kb.act(ocT[i2][:].rearrange("p a b -> p (a b)"), pb[:, 0:512], AF.Identity, [pb], [ocT[i2]])
                                    kb.dma("pool", OCT[chalf * 4:(chalf + 1) * 4, :, r0:r0 + 128].rearrange("kc p t -> p kc t"), ocT[i2][:], r=[ocT[i2]])
                    kb.barrier()

        def phase_c(l, A2, B2):
            G = 512
            with ExitStack() as ph:
                arena = kb.sb([128, 32768], BF16, ph)
                h1T = arena[:, 0:8192].rearrange("p (k t) -> p k t", k=16)
                oT = [arena[:, 8192 + b * 4096: 8192 + (b + 1) * 4096].rearrange("p (k t) -> p k t", k=8) for b in range(3)]
                hidT = arena[:, :].rearrange("p (k t) -> p k t", k=64)
                mgT = kb.sb([128, 16, G], BF16, ph)
                h2T = kb.sb([128, 16, G], BF16, ph)
                gch = [kb.sb([128, 512], F32, ph) for _ in range(2)]
                xch = [kb.sb([128, 512], F32, ph) for _ in range(2)]
                xt = kb.sb([128, D], F32, ph)
                tmp = (None, kb.sb([128, 1], F32, ph), kb.sb([128, 1], F32, ph), kb.sb([128, D], BF16, ph))
                wg = [kb.sb([128, 16, 128], BF16, ph) for _ in range(3)]
                wbr = [kb.sb([128, 8, 128], BF16, ph) for _ in range(3)]
                wo2 = [kb.sb([128, 16, 512], BF16, ph) for _ in range(2)]
                w2 = [kb.sb([128, 512], BF16, ph) for _ in range(6)]
                sg = [kb.sb([128, G], F32, ph) for _ in range(2)]
                pr = [kb.sb([128, G], F32, ph) for _ in range(2)]
                macc = kb.sb([128, G], F32, ph)
                ysb = [kb.sb([128, 512], F32, ph) for _ in range(2)]
                hs1 = [kb.sb([128, G], F32, ph) for _ in range(2)]
                x1b = T(None)
                win_src = WB["w_in"][l].rearrange("(kc p) c -> p kc c", p=128)
                brs = [WB["w_br_a"][l].rearrange("(kc p) c -> p kc c", p=128), WB["w_br_b"][l].rearrange("(kc p) c -> p kc c", p=128),
                       WB["w_br_c"][l].rearrange("(kc p) c -> p kc c", p=128)]
                wo_src = WB["w_out"][l].rearrange("(kc p) c -> p kc c", p=128)
                w1_src = WB["w_mlp1"][l].rearrange("(kc p) c -> p kc c", p=128)
                w2_src = WB["w_mlp2"][l]
                srcT = [OAT, OBT, OCT]
                c = {"wg": 0, "w2": 0, "k": 0, "y": 0}
                x_in = A["x"] if l == 0 else XS
                for g in range(NTOK // G):
                    j = 0 if g < 8 else 1
                    tk = slice(g * G, (g + 1) * G)
                    kb.dma("sync", h1T, H1T[:, :, tk].rearrange("kc p t -> p kc t"), w=[arena])
                    for b in range(3):
                        kb.dma("sync", oT[b], srcT[b][:, :, tk].rearrange("kc p t -> p kc t"), w=[arena])
                    for mf in range(16):
                        for b in range(3):
                            wg_ = wg[c["wg"] % 3]
                            wb_ = wbr[c["wg"] % 3]
                            c["wg"] += 1
                            kb.dma("sync", wg_[:], WBG[l, b * 16 + mf], w=[wg_])
                            kb.dma("sync", wb_[:], WBR[b][l, mf], w=[wb_])
                            i2 = c["k"] % 2
                            c["k"] += 1
                            pg, pp = P[i2], P[2 + i2]
                            kb.mm(pg[:], [(wg_[:, kc, :], h1T[:, kc, :]) for kc in range(16)], [wg_, arena], [pg])
                            kb.mm(pp[:], [(wb_[:, kc, :], oT[b][:, kc, :]) for kc in range(8)], [wb_, arena], [pp])
                            kb.act(sg[i2][:], pg[:], AF.Sigmoid, [pg], [sg[i2]])
                            if b == 0:
                                kb.tt(macc[:], pp[:], sg[i2][:], ALU.mult, [pp, sg[i2]], [macc])
                            else:
                                kb.tt(pr[i2][:], pp[:], sg[i2][:], ALU.mult, [pp, sg[i2]], [pr[i2]])
                                if b == 1:
                                    kb.tt(macc[:], macc[:], pr[i2][:], ALU.add, [macc, pr[i2]], [macc], eng="pool")
                                else:
                                    kb.tt(mgT[:, mf, :], macc[:], pr[i2][:], ALU.add, [macc, pr[i2]], [mgT], eng="pool")
                    for oc in range(4):
                        ocs = slice(oc * 512, (oc + 1) * 512)
                        wo = wo2[oc % 2]
                        kb.dma("sync", wo[:], wo_src[:, :, ocs], w=[wo])
                        for ti in range(4):
                            r0 = (g * 4 + ti) * 128
                            i2 = c["y"] % 2
                            c["y"] += 1
                            py, ys_, g_, x_ = P[4 + i2], ysb[i2], gch[i2], xch[i2]
                            kb.dma("sync", g_[:], GSC[j, 0, :, ocs], w=[g_])
                            kb.dma("sync", x_[:], x_in[r0:r0 + 128, ocs], w=[x_])
                            kb.mm(py[:], [(mgT[:, mf, ti * 128:(ti + 1) * 128], wo[:, mf, :]) for mf in range(16)], [mgT, wo], [py])
                            kb.tt(ys_[:], py[:], g_[:], ALU.mult, [py, g_], [ys_])
                            kb.tt(ys_[:], ys_[:], x_[:], ALU.add, [ys_, x_], [ys_], eng="pool")
                            kb.dma("pool", X1[r0:r0 + 128, ocs], ys_[:], r=[ys_], w=[x1b])
                    for ti in range(4):
                        r0 = (g * 4 + ti) * 128
                        kb.dma("sync", xt[:], X1[r0:r0 + 128, :], r=[x1b], w=[xt])
                        norm_to_T(xt, h2T, ti * 128, A2, B2, j, tmp)
                    for ff in range(64):
                        w1_ = wg[c["wg"] % 3]
                        c["wg"] += 1
                        kb.dma("sync", w1_[:], WB1[l, ff], w=[w1_])
                        i2 = c["k"] % 2
                        c["k"] += 1
                        pg = P[i2]
                        kb.mm(pg[:], [(w1_[:, kc, :], h2T[:, kc, :]) for kc in range(16)], [w1_, h2T], [pg])
                        kb.act(hs1[i2][:], pg[:], AF.Relu, [pg], [hs1[i2]])
                        kb.tt(hidT[:, ff, :], hs1[i2][:], hs1[i2][:], ALU.mult, [hs1[i2]], [arena], eng=("dve" if ff % 2 == 0 else "pool"))
                    for oc in range(4):
                        ocs = slice(oc * 512, (oc + 1) * 512)
                        pys = [P[2], P[3], P[4], P[5]]
                        for ff in range(64):
                            w2_ = w2[c["w2"] % 6]
                            c["w2"] += 1
                            kb.dma("sync", w2_[:], w2_src[ff * 128:(ff + 1) * 128, ocs], w=[w2_])
                            for ti in range(4):
                                kb.mm(pys[ti][:], [(hidT[:, ff, ti * 128:(ti + 1) * 128], w2_[:])], [arena, w2_], [pys[ti]],
                                      start=(ff == 0), stop=(ff == 63))
                        for ti in range(4):
                            tile_i = g * 4 + ti
                            r0 = tile_i * 128
                            i2 = c["y"] % 2
                            c["y"] += 1
                            ys_, g_, x_ = ysb[i2], gch[i2], xch[i2]
                            kb.dma("sync", g_[:], GSC[j, 1, :, ocs], w=[g_])
                            kb.dma("sync", x_[:], X1[r0:r0 + 128, ocs], r=[x1b], w=[x_])
                            kb.tt(ys_[:], pys[ti][:], g_[:], ALU.mult, [pys[ti], g_], [ys_])
                            kb.tt(ys_[:], ys_[:], x_[:], ALU.add, [ys_, x_], [ys_], eng="pool")
                            if l == DEPTH - 1:
                                if tile_i < 32:
                                    dst = OA["ys"][r0:r0 + 128, ocs]
                                else:
                                    dst = OA["yp"][r0 - LS:r0 - LS + 128, ocs]
                            else:
                                dst = XS[r0:r0 + 128, ocs]
                            kb.dma("pool", dst, ys_[:], r=[ys_])
                kb.barrier()

        for l in range(DEPTH):
            layer(l)
        kb.finish()
        print("n_inst", kb.n_inst, {e: len(kb.ops[e]) for e in ENGS})
    return nc


def _constants():
    bf = ml_dtypes.bfloat16
    c = {}
    p = np.arange(128)
    ident = np.eye(128, dtype=np.float32)
    tri_le = (p[:, None] <= p[None, :]).astype(np.float32)
    tri_ge = (p[:, None] >= p[None, :]).astype(np.float32)
    jrev = ((p[:, None] // 64 == p[None, :] // 64) & (p[:, None] % 64 == 63 - p[None, :] % 64)).astype(np.float32)
    c["cpack"] = np.concatenate([ident, tri_le, tri_ge, np.ones((128, 128), np.float32), jrev], axis=1)
    pos = np.arange(LS)
    row = (pos // 64).astype(np.float32)
    col = (pos % 64).astype(np.float32)
    inv = (np.float32(10000.0) ** (-np.arange(32, dtype=np.float32) / np.float32(32))).astype(np.float32)
    ang = np.concatenate([row[:, None] * inv, col[:, None] * inv], axis=-1).astype(np.float32)
    c["cos4"] = np.tile(np.cos(ang).astype(np.float32), (1, 4))
    c["sin4"] = np.tile(np.sin(ang).astype(np.float32), (1, 4))
    cc = np.arange(64)
    cstart = np.clip(cc - 8, 0, 48)
    col_in = (cc[None, :] >= cstart[:, None]) & (cc[None, :] < cstart[:, None] + 16)
    cmask = np.where(col_in.T, 0.0, NEG).astype(np.float32)
    c["colmask"] = np.concatenate([cmask, cmask], axis=0)
    for L, nm in ((LS, "L"), (LP, "S")):
        t = np.arange(L, dtype=np.float32)
        tn = (t / np.float32(L)).astype(np.float32)
        bands = np.arange(1, 17, dtype=np.float32)
        a = (np.float32(2.0 * np.pi) * tn[:, None] * bands[None, :]).astype(np.float32)
        z = np.concatenate([tn[:, None], np.cos(a), np.sin(a)], axis=-1).astype(np.float32)
        c["zpos" + nm] = np.ascontiguousarray(z.T)
        dist = (np.abs(t - L // 2) * np.float32(2.0 / L)).astype(np.float32)
        c["ndist" + nm] = np.ascontiguousarray((-dist).reshape(L // 128, 128).T)
        nf = NFT[L] * 128
        N = 2 * nf
        s = np.arange(L, dtype=np.float64)
        f = np.arange(nf, dtype=np.float64)
        th = 2.0 * np.pi * np.outer(s, f) / N
        FC = np.cos(th)
        FS = -np.sin(th)
        FS[:, 0] = np.cos(np.pi * s)
        n = np.arange(L, dtype=np.float64) + L // 2
        th2 = 2.0 * np.pi * np.outer(f, n) / N
        GC = (2.0 / N) * np.cos(th2)
        GS = -(2.0 / N) * np.sin(th2)
        GC[0, :] = 1.0 / N
        GS[0, :] = (1.0 / N) * np.cos(np.pi * n)

        def lay(M):
            R, C = M.shape
            return np.ascontiguousarray(M.reshape(R // 128, 128, C // 128, 128).transpose(2, 1, 0, 3).reshape(C // 128, 128, R)).astype(bf)
        c["FC" + nm], c["FS" + nm], c["GC" + nm], c["GS" + nm] = lay(FC), lay(FS), lay(GC), lay(GS)
    c["deltas"] = np.abs(np.linspace(math.log(1e-2) / 1.5, math.log(1e-2) / 0.3, 1024, dtype=np.float32)).astype(np.float32)
    return c


_CACHE = {}


def kernel(**inp):
    f32 = np.float32
    g = {k: np.asarray(v) for k, v in inp.items()}
    if "nc" not in _CACHE:
        _CACHE["nc"] = build_program()
        _CACHE["const"] = _constants()
    nc = _CACHE["nc"]
    const = _CACHE["const"]

    def colsT(v, n):
        return np.ascontiguousarray(v.reshape(2, n, 128).transpose(0, 2, 1))
    rp = g["na_rpb"][..., ::-1]
    rpbp = np.zeros((2, 8, 15, 127), f32)
    rpbp[..., 48:79] = rp
    shared = {
        "w_mod": g["w_mod"], "b_modT": colsT(g["b_mod"], 96), "b_mod": g["b_mod"],
        "n1T": colsT(g["norm1_g"], 16), "n2T": colsT(g["norm2_g"], 16),
        "w_in": g["w_in"], "wa2": g["gla_wa2"], "ba2": g["gla_ba2"],
        "gng4": np.tile(g["gla_norm_g"], (1, 4)), "qg8": np.tile(g["na_qnorm_g"], (1, 8)), "kg8": np.tile(g["na_knorm_g"], (1, 8)),
        "rpbp": rpbp, "hcw": g["hy_conv_w"], "hcb": g["hy_conv_b"], "f1w": g["hy_f1_w"], "f1b": g["hy_f1_b"][..., None],
        "f2w": g["hy_f2_w"], "f2b": g["hy_f2_b"][..., None], "f3w": g["hy_f3_w"], "hfr": g["hy_freq"][..., None],
        "hbias": g["hy_bias"], "w_br_a": g["w_br_a"], "w_br_b": g["w_br_b"], "w_br_c": g["w_br_c"],
        "w_out": g["w_out"], "w_mlp1": g["w_mlp1"], "w_mlp2": g["w_mlp2"],
    }
    shared.update(const)
    shared = {k: np.ascontiguousarray(v) for k, v in shared.items()}
    in_maps = []
    for i in range(NCORES):
        m = dict(shared)
        m["x"] = np.ascontiguousarray(np.concatenate([g["x_sample"][i], g["x_prompt"][4 * i:4 * i + 4].reshape(4 * LP, D)], axis=0))
        cond = np.stack([g["c"][i], g["c_ctx"]], axis=0)
        m["condT"] = np.ascontiguousarray(cond.reshape(2, 16, 128).transpose(2, 1, 0))
        m["state0"] = np.ascontiguousarray(g["state_gla"][i])
        m["ck"] = np.ascontiguousarray(g["cache_na_k"][i])
        m["cv"] = np.ascontiguousarray(g["cache_na_v"][i])
        in_maps.append(m)
    res = run_bass_kernel_spmd(nc, in_maps, core_ids=list(range(NCORES)), trace=bool(os.environ.get('KTRACE')))
    if NCORES != 8:
        print('exec_time_ns', res.exec_time_ns)
        return res.results
    R = res.results
    y_sample = np.stack([R[i]["ys"] for i in range(8)], axis=0)
    y_prompt = np.concatenate([R[i]["yp"].reshape(4, LP, D) for i in range(8)], axis=0)
    nst = np.concatenate([R[i]["nst"] for i in range(8)], axis=0)
    nk = np.concatenate([R[i]["nk"] for i in range(8)], axis=0)
    nv = np.concatenate([R[i]["nv"] for i in range(8)], axis=0)
    return (y_prompt.astype(f32), y_sample.astype(f32), nst.astype(f32), nk.astype(f32), nv.astype(f32))
```

```python
import math
from contextlib import ExitStack
import numpy as np
import ml_dtypes
import concourse.bass as bass
import concourse.mybir as mybir
from concourse.bass_utils import run_bass_kernel_spmd

F32 = mybir.dt.float32
BF16 = mybir.dt.bfloat16
I32 = mybir.dt.int32
AF = mybir.ActivationFunctionType
ALU = mybir.AluOpType
AX = mybir.AxisListType

ENGS = ("sync", "act", "pool", "dve", "pe")
N_DMA_SEMS = 64
SAME_ENGINE_SYNC = True

D = 2048
NTOK = 5120
NT = 40
LS = 4096
LP = 256
DEPTH = 2
EPS = 1e-6
W_IN = 15392
NEG = -30000.0
NFT = {4096: 24, 256: 2}
import os
STOP = os.environ.get('KSTOP', '')
KGLA = int(os.environ.get('KGLA', '9'))
NOPOOL = int(os.environ.get('NOPOOL', '0'))
NCORES = int(os.environ.get('KCORES', '8'))


class Buf:
    __slots__ = ("w", "r", "const")

    def __init__(self, const=False):
        self.w = None
        self.r = {}
        self.const = const


class T:
    __slots__ = ("t", "b")

    def __init__(self, t, const=False):
        self.t = t
        self.b = Buf(const)

    def __getitem__(self, k):
        return self.t[k]


class KB:
    def __init__(self, nc, stack):
        self.nc = nc
        self.stack = stack
        self.ops = {e: [] for e in ENGS}
        self.sems = []
        for e in ENGS:
            self.sems.append(stack.enter_context(nc.semaphore("s_" + e)))
        self.own = {e: i for i, e in enumerate(ENGS)}
        self.cnt = {e: 0 for e in ENGS}
        self.seen = {e: {} for e in ENGS}
        self.dma_pools = {}
        for e, n in (("sync", 48), ("pool", 40), ("act", 8)):
            base = len(self.sems)
            for i in range(n):
                self.sems.append(stack.enter_context(nc.semaphore("s_dma_%s%d" % (e, i))))
            self.dma_pools[e] = {"base": base, "n": n, "tgt": [0] * n, "k": 0}
        self.n_inst = 0
        self.uid = 0

    def sb(self, shape, dtype, stack=None, const=False, name=None):
        st = stack or self.stack
        self.uid += 1
        return T(st.enter_context(self.nc.sbuf_tensor("%s_%d" % (name or "sb", self.uid), list(shape), dtype)), const)

    def ps(self, shape, dtype=F32, stack=None):
        st = stack or self.stack
        self.uid += 1
        return T(st.enter_context(self.nc.psum_tensor("ps_%d" % self.uid, list(shape), dtype)))

    def _wait(self, eng, ev):
        sk, v = ev
        if self.seen[eng].get(sk, 0) >= v:
            return
        self.seen[eng][sk] = v
        sem = self.sems[sk]
        self.ops[eng].append(lambda e, sem=sem, v=v: e.wait_ge(sem, v))

    def issue(self, eng, fns, reads=(), writes=(), dma=False):
        if callable(fns):
            fns = [fns]
        deps = []
        for t in reads:
            b = t.b
            if b.w is not None:
                deps.append(b.w)
        for t in writes:
            b = t.b
            if b.w is not None:
                deps.append(b.w)
            for sk, v in b.r.items():
                deps.append((sk, v))
        own = self.own[eng]
        for ev in deps:
            if ev[0] == own and not dma:
                if eng == "pe" or not SAME_ENGINE_SYNC:
                    continue
            self._wait(eng, ev)
        if dma:
            dp = self.dma_pools[eng]
            slot = dp["k"] % dp["n"]
            dp["k"] += 1
            sk = dp["base"] + slot
            if dp["tgt"][slot] > 0:
                self._wait(eng, (sk, dp["tgt"][slot]))
            dp["tgt"][slot] += 16
            ev = (sk, dp["tgt"][slot])
            inc = 16
        else:
            self.cnt[eng] += 1
            sk = own
            ev = (sk, self.cnt[eng])
            inc = 1
        sem = self.sems[sk]
        for f in fns[:-1]:
            self.ops[eng].append(f)
        last = fns[-1]
        self.ops[eng].append(lambda e, last=last, sem=sem, inc=inc: last(e).then_inc(sem, inc))
        self.n_inst += len(fns)
        for t in reads:
            b = t.b
            if not b.const:
                if b.r.get(ev[0], 0) < ev[1]:
                    b.r[ev[0]] = ev[1]
        for t in writes:
            t.b.w = ev
            t.b.r = {}
        return ev

    def dma(self, eng, out, in_, r=(), w=(), **kw):
        return self.issue(eng, lambda e: e.dma_start(out=out, in_=in_, **kw), r, w, dma=True)

    def barrier(self):
        evs = [(self.own[e], self.cnt[e]) for e in ENGS if self.cnt[e] > 0]
        for dp in self.dma_pools.values():
            for i in range(dp["n"]):
                if dp["tgt"][i] > 0:
                    evs.append((dp["base"] + i, dp["tgt"][i]))
        for e in ENGS:
            for ev in evs:
                if ev[0] == self.own[e]:
                    continue
                self._wait(e, ev)

    def mm(self, out, pairs, r, w, start=True, stop=True):
        n = len(pairs)
        fns = []
        for i, (l, rh) in enumerate(pairs):
            fns.append(lambda e, l=l, rh=rh, s0=(start and i == 0), s1=(stop and i == n - 1):
                       e.matmul(out, lhsT=l, rhs=rh, start=s0, stop=s1))
        return self.issue("pe", fns, r, w)

    def tr(self, out, in_, ident, r, w):
        return self.issue("pe", lambda e: e.transpose(out, in_, ident), r, w)

    def act(self, out, in_, func, r, w, bias=0.0, scale=1.0, accum_out=None):
        if accum_out is None:
            return self.issue("act", lambda e: e.activation(out=out, in_=in_, func=func, bias=bias, scale=scale), r, w)
        return self.issue("act", lambda e: e.activation(out=out, in_=in_, func=func, bias=bias, scale=scale,
                                                        accum_out=accum_out), r, w)

    def tt(self, out, in0, in1, op, r, w, eng="dve"):
        if NOPOOL:
            eng = "dve"
        return self.issue(eng, lambda e: e.tensor_tensor(out=out, in0=in0, in1=in1, op=op), r, w)

    def ts(self, out, in0, s1, s2, op0, op1, r, w, eng="dve"):
        if s2 is None:
            return self.issue(eng, lambda e: e.tensor_scalar(out=out, in0=in0, scalar1=s1, scalar2=None, op0=op0), r, w)
        return self.issue(eng, lambda e: e.tensor_scalar(out=out, in0=in0, scalar1=s1, scalar2=s2, op0=op0, op1=op1), r, w)

    def stt(self, out, in0, scalar, in1, op0, op1, r, w, eng="dve"):
        return self.issue(eng, lambda e: e.scalar_tensor_tensor(out=out, in0=in0, scalar=scalar, in1=in1, op0=op0, op1=op1), r, w)

    def cp(self, out, in_, r, w, eng="dve"):
        if NOPOOL:
            eng = "dve"
        return self.issue(eng, lambda e: e.tensor_copy(out=out, in_=in_), r, w)

    def recip(self, out, in_, r, w):
        return self.issue("dve", lambda e: e.reciprocal(out=out, in_=in_), r, w)

    def memset(self, ap, val, w, eng="pool"):
        return self.issue(eng, lambda e: e.memset(ap, val), (), w)

    def finish(self):
        self.barrier()
        nc = self.nc
        ops = self.ops
        with nc.Block() as block:
            @block.sync
            def _(e):
                for f in ops["sync"]:
                    f(e)

            @block.scalar
            def _(e):
                for f in ops["act"]:
                    f(e)

            @block.gpsimd
            def _(e):
                for f in ops["pool"]:
                    f(e)

            @block.vector
            def _(e):
                for f in ops["dve"]:
                    f(e)

            @block.tensor
            def _(e):
                for f in ops["pe"]:
                    f(e)


def _div_le(n, cap):
    for d in range(min(n, cap), 0, -1):
        if n % d == 0:
            return d
    return 1


IN_SPECS = [
    ("x", [NTOK, D], F32), ("condT", [128, 16, 2], F32),
    ("state0", [2, 2, 4, 128, 256], F32), ("ck", [2, 8, 256, 128], F32), ("cv", [2, 8, 256, 128], F32),
    ("w_mod", [2, D, 6 * D], F32), ("b_modT", [2, 128, 96], F32), ("b_mod", [2, 6 * D], F32),
    ("n1T", [2, 128, 16], F32), ("n2T", [2, 128, 16], F32),
    ("w_in", [2, D, W_IN], F32), ("wa2", [2, 2, 16, 512], F32), ("ba2", [2, 2, 512], F32),
    ("gng4", [2, 1024], F32), ("qg8", [2, 1024], F32), ("kg8", [2, 1024], F32),
    ("rpbp", [2, 8, 15, 127], F32),
    ("hcw", [2, 3, 3072], F32), ("hcb", [2, 3072], F32), ("f1w", [2, 33, 64], F32), ("f1b", [2, 64, 1], F32),
    ("f2w", [2, 64, 64], F32), ("f2b", [2, 64, 1], F32), ("f3w", [2, 64, 2048], F32), ("hfr", [2, 64, 1], F32),
    ("hbias", [2, 2, 1024], F32),
    ("w_br_a", [2, 1024, D], F32), ("w_br_b", [2, 1024, D], F32), ("w_br_c", [2, 1024, D], F32),
    ("w_out", [2, D, D], F32), ("w_mlp1", [2, D, 4 * D], F32), ("w_mlp2", [2, 4 * D, D], F32),
    ("cpack", [128, 640], F32), ("cos4", [LS, 256], F32), ("sin4", [LS, 256], F32), ("colmask", [128, 64], F32),
    ("zposL", [33, LS], F32), ("zposS", [33, LP], F32), ("ndistL", [128, 32], F32), ("ndistS", [128, 2], F32),
    ("deltas", [1024], F32),
    ("FCL", [24, 128, 4096], BF16), ("FSL", [24, 128, 4096], BF16), ("GCL", [32, 128, 3072], BF16), ("GSL", [32, 128, 3072], BF16),
    ("FCS", [2, 128, 256], BF16), ("FSS", [2, 128, 256], BF16), ("GCS", [2, 128, 256], BF16), ("GSS", [2, 128, 256], BF16),
]
OUT_SPECS = [
    ("ys", [LS, D], F32), ("yp", [4 * LP, D], F32), ("nst", [4, 2, 2, 4, 128, 256], F32),
    ("nk", [4, 2, 8, 256, 128], F32), ("nv", [4, 2, 8, 256, 128], F32),
]


def build_program():
    nc = bass.Bass("TRN2", target_bir_lowering=False)
    I = {n: nc.dram_tensor(n, s, d, kind="ExternalInput") for n, s, d in IN_SPECS}
    O = {n: nc.dram_tensor(n, s, d, kind="ExternalOutput") for n, s, d in OUT_SPECS}
    A = {n: t.ap() for n, t in I.items()}
    OA = {n: t.ap() for n, t in O.items()}

    def dram(name, shape, dt):
        return nc.dram_tensor(name, shape, dt, kind="Internal").ap()

    WB = {
        "w_in": dram("wb_in", [2, D, W_IN], BF16), "w_br_a": dram("wb_a", [2, 1024, D], BF16),
        "w_br_b": dram("wb_b", [2, 1024, D], BF16), "w_br_c": dram("wb_c", [2, 1024, D], BF16),
        "w_out": dram("wb_o", [2, D, D], BF16), "w_mlp1": dram("wb_1", [2, D, 4 * D], BF16),
        "w_mlp2": dram("wb_2", [2, 4 * D, D], BF16),
    }
    WBG = dram("wbg", [2, 48, 128, 16, 128], BF16)
    WB1 = dram("wb1b", [2, 64, 128, 16, 128], BF16)
    WBR = [dram("wbr%d" % b, [2, 16, 128, 8, 128], BF16) for b in range(3)]
    SC = dram("sc", [NTOK, 9216], F32)
    SGA = dram("sga", [2, 16, NTOK], F32)
    H1T = dram("h1t", [16, 128, NTOK], BF16)
    OF = dram("of", [NTOK, 1024], F32)
    OAT = dram("oat", [8, 128, NTOK], BF16)
    OBT = dram("obt", [8, 128, NTOK], BF16)
    OCT = dram("oct", [8, 128, NTOK], BF16)
    QNT = dram("qnt", [8, 128, NTOK], BF16)
    KNT = dram("knt", [8, 128, NTOK], BF16)
    XS = dram("xs", [NTOK, D], F32)
    X1 = dram("x1", [NTOK, D], F32)
    GSC = dram("gsc", [2, 2, 128, D], F32)
    HG = dram("hg", [NTOK, 2048], F32)
    ZF = [dram("zf0", [NTOK, 1024], F32), dram("zf1", [NTOK, 1024], F32)]
    ZB = [dram("zb0", [NTOK, 1024], BF16), dram("zb1", [NTOK, 1024], BF16)]
    HB = {LS: dram("hbL", [LS, 2048], BF16), LP: dram("hbS", [LP, 2048], BF16)}
    HF = {LS: dram("hfL", [2, 24, 128, 2, 1024], F32), LP: dram("hfS", [2, 2, 128, 2, 1024], F32)}

    with ExitStack() as st:
        kb = KB(nc, st)
        cp32 = kb.sb([128, 640], F32, const=True)
        cpb = kb.sb([128, 640], BF16, const=True)
        P = [kb.ps([128, 512], F32) for _ in range(6)]
        PB = [kb.ps([128, 1024], BF16) for _ in range(2)]
        kb.dma("sync", cp32[:], A["cpack"][:, :], w=[cp32])
        kb.cp(cpb[:], cp32[:], [cp32], [cpb])
        IDb = cpb[:, 0:128]
        TRIf = {0: cp32[:, 128:256], 1: cp32[:, 256:384]}
        TRIb = {0: cpb[:, 128:256], 1: cpb[:, 256:384]}
        ONESf = cp32[:, 384:512]
        ONESb = cpb[:, 384:512]
        JREV = cp32[:, 512:640]
        CB = [cp32, cpb]
        rr = {"pb": 0}
        A1 = kb.sb([128, 16, 2], F32, st, name="A1")
        A2 = kb.sb([128, 16, 2], F32, st, name="A2")
        B1 = kb.sb([128, 16, 2], F32, st, name="B1")
        B2 = kb.sb([128, 16, 2], F32, st, name="B2")

        def evac_engine(i):
            return "act" if i % 2 == 0 else "dve"

        def copy_any(i, out, in_, r, w):
            if i % 2 == 0:
                kb.act(out, in_, AF.Copy, r, w)
            else:
                kb.cp(out, in_, r, w)

        for l in range(DEPTH):
            for name in ("w_in", "w_out", "w_mlp2"):
                wb = WB[name]
                src = A[name]
                rows, cols = src.shape[1], src.shape[2]
                if name == "w_in":
                    cols = 9248
                cw = _div_le(cols, 2048)
                for r0 in range(0, rows, 128):
                    s_ap = src[l, r0:r0 + 128, 0:cols].rearrange("p (a b) -> p a b", b=cw)
                    d_ap = wb[l, r0:r0 + 128, 0:cols].rearrange("p (a b) -> p a b", b=cw)
                    kb.dma("pool", d_ap, s_ap)
            for kc in range(16):
                r0 = kc * 128
                kb.dma("pool", WBG[l, :, :, kc, :].rearrange("b p c -> p b c"),
                       A["w_in"][l, r0:r0 + 128, 9248:15392].rearrange("p (b c) -> p b c", c=128))
                kb.dma("pool", WB1[l, :, :, kc, :].rearrange("b p c -> p b c"),
                       A["w_mlp1"][l, r0:r0 + 128, :].rearrange("p (b c) -> p b c", c=128))
            for b, nm in enumerate(("w_br_a", "w_br_b", "w_br_c")):
                for kc in range(8):
                    r0 = kc * 128
                    kb.dma("pool", WBR[b][l, :, :, kc, :].rearrange("b p c -> p b c"),
                           A[nm][l, r0:r0 + 128, :].rearrange("p (b c) -> p b c", c=128))
        kb.barrier()
        stop_flag = {'s': STOP == 'cast'}

        def norm_to_T(xt, hT, col0, Am, Bm, j, tmp):
            _, ssq, rstd, xn = tmp
            kb.act(xn[:], xt[:], AF.Square, [xt], [xn, ssq], accum_out=ssq[:, 0:1])
            kb.act(rstd[:], ssq[:, 0:1], AF.Sqrt, [ssq], [rstd], bias=EPS, scale=1.0 / D)
            kb.recip(rstd[:], rstd[:], [rstd], [rstd])
            kb.ts(xn[:], xt[:], rstd[:, 0:1], None, ALU.mult, None, [xt, rstd], [xn])
            for half in range(2):
                pb = PB[rr["pb"] % 2]
                rr["pb"] += 1
                for q in range(8):
                    kc = half * 8 + q
                    kb.tr(pb[:, q * 128:(q + 1) * 128], xn[:, kc * 128:(kc + 1) * 128], IDb, [xn, cpb], [pb])
                for q in range(8):
                    kc = half * 8 + q
                    kb.act(hT[:, kc, col0:col0 + 128], pb[:, q * 128:(q + 1) * 128], AF.Identity, [pb, Am, Bm], [hT],
                           bias=Bm[:, kc, j:j + 1], scale=Am[:, kc, j:j + 1])

        def layer(l):
            if stop_flag['s']:
                return
            x_in = A["x"] if l == 0 else XS
            with ExitStack() as ph:
                mT = kb.sb([128, 96, 2], F32, ph)
                ct = kb.sb([128, 16, 2], F32, ph)
                sc = kb.sb([128, 16, 2], BF16, ph)
                sB = [kb.sb([128, 16, 128], BF16, ph) for _ in range(2)]
                bmT = kb.sb([128, 96], F32, ph)
                kb.dma("sync", ct[:], A["condT"][:, :, :], w=[ct])
                kb.dma("sync", bmT[:], A["b_modT"][l], w=[bmT])
                ctf = kb.sb([128, 16, 2], F32, ph)
                kb.act(ctf[:], ct[:], AF.Silu, [ct], [ctf])
                kb.cp(sc[:], ctf[:], [ctf], [sc])
                for j in range(2):
                    for kc in range(16):
                        kb.cp(sB[j][:, kc, :], ctf[:, kc, j:j + 1].to_broadcast([128, 128]), [ctf], [sB[j]])
                wm = [kb.sb([128, 16, 512], BF16, ph) for _ in range(2)]
                bb = [kb.sb([128, 512], F32, ph) for _ in range(2)]
                gst = [kb.sb([128, 512], F32, ph) for _ in range(2)]
                wsrc = A["w_mod"][l].rearrange("(kc p) c -> p kc c", p=128)
                gi = 0
                for cb in range(24):
                    w_ = wm[cb % 2]
                    kb.dma("pool", w_[:], wsrc[:, :, cb * 512:(cb + 1) * 512], w=[w_])
                    for sub in range(4):
                        ch = cb * 4 + sub
                        pp = P[ch % 2]
                        kb.mm(pp[:, 0:2], [(w_[:, kc, sub * 128:(sub + 1) * 128], sc[:, kc, :]) for kc in range(16)], [w_, sc], [pp])
                        kb.ts(mT[:, ch, :], pp[:, 0:2], bmT[:, ch:ch + 1], None, ALU.add, None, [pp, bmT], [mT])
                    gsel = {8: 0, 9: 0, 10: 0, 11: 0, 20: 1, 21: 1, 22: 1, 23: 1}.get(cb)
                    if gsel is not None:
                        b_ = bb[cb % 2]
                        kb.dma("sync", b_[:], A["b_mod"][l, cb * 512:(cb + 1) * 512].partition_broadcast(128), w=[b_])
                        for j in range(2):
                            pp = P[2 + j]
                            g_ = gst[gi % 2]
                            gi += 1
                            kb.mm(pp[:], [(sB[j][:, kc, :], w_[:, kc, :]) for kc in range(16)], [w_, sB[j]], [pp])
                            kb.tt(g_[:], pp[:], b_[:], ALU.add, [pp, b_], [g_])
                            c0 = (cb % 4) * 512
                            kb.dma("pool", GSC[j, gsel, :, c0:c0 + 512], g_[:], r=[g_])
                g1 = kb.sb([128, 16], F32, ph)
                g2 = kb.sb([128, 16], F32, ph)
                kb.dma("sync", g1[:], A["n1T"][l], w=[g1])
                kb.dma("sync", g2[:], A["n2T"][l], w=[g2])
                for j in range(2):
                    kb.stt(A1[:, :, j], mT[:, 16:32, j], 1.0, g1[:], ALU.add, ALU.mult, [mT, g1], [A1])
                    kb.stt(A2[:, :, j], mT[:, 64:80, j], 1.0, g2[:], ALU.add, ALU.mult, [mT, g2], [A2])
                    kb.cp(B1[:, :, j], mT[:, 0:16, j], [mT], [B1])
                    kb.cp(B2[:, :, j], mT[:, 48:64, j], [mT], [B2])
                kb.barrier()

            if STOP == "M" + str(l):
                stop_flag['s'] = True
                return
            with ExitStack() as ph:
                G = 1024
                hT = kb.sb([128, 16, G], BF16, ph)
                xt2 = [kb.sb([128, D], F32, ph) for _ in range(2)]
                tmp = (None, kb.sb([128, 1], F32, ph), kb.sb([128, 1], F32, ph), kb.sb([128, D], BF16, ph))
                wbuf = [kb.sb([128, 16, 512], BF16, ph) for _ in range(3)]
                stg = [kb.sb([128, 512], F32, ph) for _ in range(4)]
                wga = kb.sb([128, 16, 32], BF16, ph)
                gst_ = [kb.sb([16, 512], F32, ph) for _ in range(2)]
                wsrc = WB["w_in"][l].rearrange("(kc p) c -> p kc c", p=128)
                kb.dma("sync", wga[:], wsrc[:, :, 3072:3104], w=[wga])
                si = 0
                wi = 0
                for g in range(NTOK // G):
                    j = 0 if g < 4 else 1
                    for ti in range(8):
                        tile_i = g * 8 + ti
                        xt = xt2[tile_i % 2]
                        kb.dma("sync", xt[:], x_in[tile_i * 128:(tile_i + 1) * 128, :], w=[xt])
                        norm_to_T(xt, hT, ti * 128, A1, B1, j, tmp)
                    kb.dma("pool", H1T[:, :, g * G:(g + 1) * G].rearrange("kc p t -> p kc t"), hT[:], r=[hT])
                    for blk in range(18):
                        wcol = blk * 512 if blk < 6 else blk * 512 + 32
                        w_ = wbuf[wi % 3]
                        wi += 1
                        kb.dma("sync", w_[:], wsrc[:, :, wcol:wcol + 512], w=[w_])
                        for ti in range(8):
                            tile_i = g * 8 + ti
                            pp = P[si % 4]
                            s_ = stg[si % 4]
                            kb.mm(pp[:], [(hT[:, kc, ti * 128:(ti + 1) * 128], w_[:, kc, :]) for kc in range(16)], [hT, w_], [pp])
                            copy_any(si, s_[:], pp[:], [pp], [s_])
                            si += 1
                            kb.dma("pool", SC[tile_i * 128:(tile_i + 1) * 128, blk * 512:(blk + 1) * 512], s_[:], r=[s_])
                            if j == 1 and blk in (10, 11):
                                pi = (tile_i - 32) // 2
                                s0 = ((tile_i - 32) % 2) * 128
                                for hh in range(4):
                                    h = (blk - 10) * 4 + hh
                                    kb.dma("pool", OA["nv"][pi, l, h, s0:s0 + 128, :], s_[:, hh * 128:(hh + 1) * 128], r=[s_])
                    for dr in range(2):
                        for tq in range(G // 512):
                            pp = P[4 + (tq % 2)]
                            g_ = gst_[tq % 2]
                            kb.mm(pp[0:16, :], [(wga[:, kc, dr * 16:(dr + 1) * 16], hT[:, kc, tq * 512:(tq + 1) * 512]) for kc in range(16)],
                                  [hT, wga], [pp])
                            kb.cp(g_[:], pp[0:16, :], [pp], [g_])
                            kb.dma("pool", SGA[dr, :, g * G + tq * 512: g * G + (tq + 1) * 512], g_[:], r=[g_])
                kb.barrier()

            if STOP == "A" + str(l):
                stop_flag['s'] = True
                return
            for nm, fn in (("gla", gla_phase), ("na", na_phase), ("hy", hyena_phase), ("C", lambda l_: phase_c(l_, A2, B2))):
                if stop_flag['s']:
                    return
                fn(l)
                if STOP == nm + str(l):
                    stop_flag['s'] = True

        def gla_phase(l):
            with ExitStack() as ph:
                wa = kb.sb([17, 2, 512], F32, ph)
                kb.dma("sync", wa[0:16, :, :], A["wa2"][l].rearrange("d r c -> r d c"), w=[wa])
                kb.dma("sync", wa[16:17, :, :], A["ba2"][l:l + 1, :, :], w=[wa])
                gn = kb.sb([128, 1024], F32, ph)
                kb.dma("sync", gn[:], A["gng4"][l].partition_broadcast(128), w=[gn])
                S = [kb.sb([128, 256], F32, ph) for _ in range(4)]
                Sb = [kb.sb([128, 256], BF16, ph) for _ in range(4)]
                NB = 2
                qf = [kb.sb([128, 512], F32, ph) for _ in range(NB)]
                kf = [kb.sb([128, 512], F32, ph) for _ in range(NB)]
                vf = [kb.sb([128, 1024], F32, ph) for _ in range(NB)]
                vb = [kb.sb([128, 1024], BF16, ph) for _ in range(NB)]
                gaT = [kb.sb([17, 128], F32, ph) for _ in range(NB)]
                for g_ in gaT:
                    kb.memset(g_[:], 1.0, [g_])
                cs4 = [kb.sb([128, 256], F32, ph) for _ in range(NB)]
                sn4 = [kb.sb([128, 256], F32, ph) for _ in range(NB)]
                e1 = kb.sb([128, 512], F32, ph)
                sp = kb.sb([128, 512], F32, ph)
                tots = kb.sb([128, 512], F32, ph)
                dd = kb.sb([128, 512], F32, ph)
                EQ = kb.sb([128, 512], F32, ph)
                EK = kb.sb([128, 512], F32, ph)
                EH = kb.sb([128, 512], F32, ph)
                dec = kb.sb([128, 4], F32, ph)
                qr = kb.sb([128, 512], F32, ph)
                kr = kb.sb([128, 512], F32, ph)
                t1 = kb.sb([128, 256], F32, ph)
                t2 = kb.sb([128, 256], F32, ph)
                qt = kb.sb([128, 512], BF16, ph)
                kt = kb.sb([128, 512], BF16, ph)
                kh = kb.sb([128, 512], BF16, ph)
                qT = kb.sb([128, 512], BF16, ph)
                kT = kb.sb([128, 512], BF16, ph)
                AT = [kb.sb([128, 128], BF16, ph) for _ in range(2)]
                ost = kb.sb([128, 1024], F32, ph)
                ofl = kb.sb([128, 1024], F32, ph)
                gg = kb.sb([128, 1024], F32, ph)
                gs = kb.sb([128, 1024], F32, ph)
                junk = kb.sb([128, 256], F32, ph)
                ss4 = kb.sb([128, 4], F32, ph)
                rs4 = kb.sb([128, 4], F32, ph)
                oa = kb.sb([128, 1024], BF16, ph)
                oaT = kb.sb([128, 8, 128], BF16, ph)

                def rope(dst, src, c_, s_):
                    s3 = src[:].rearrange("p (h d) -> p h d", h=4)
                    d3 = dst[:].rearrange("p (h d) -> p h d", h=4)
                    c3 = c_[:].rearrange("p (h d) -> p h d", h=4)
                    n3 = s_[:].rearrange("p (h d) -> p h d", h=4)
                    a3 = t1[:].rearrange("p (h d) -> p h d", h=4)
                    b3 = t2[:].rearrange("p (h d) -> p h d", h=4)
                    kb.tt(a3, s3[:, :, 0:64], c3, ALU.mult, [src, c_], [t1])
                    kb.tt(b3, s3[:, :, 64:128], n3, ALU.mult, [src, s_], [t2], eng="pool")
                    kb.tt(d3[:, :, 0:64], a3, b3, ALU.subtract, [t1, t2], [dst])
                    kb.tt(a3, s3[:, :, 0:64], n3, ALU.mult, [src, s_], [t1])
                    kb.tt(b3, s3[:, :, 64:128], c3, ALU.mult, [src, c_], [t2], eng="pool")
                    kb.tt(d3[:, :, 64:128], a3, b3, ALU.add, [t1, t2], [dst])

                seqs = [(0, 32, True, None)] + [(32 + 2 * p, 2, False, p) for p in range(4)]
                if os.environ.get('KSEQ') == 'p':
                    seqs = seqs[1:]
                cnt = [0]
                kh2 = [kh, kb.sb([128, 512], BF16, ph)]
                qT2 = [qT, kb.sb([128, 512], BF16, ph)]
                kT2 = [kT, kb.sb([128, 512], BF16, ph)]
                dec2 = [dec, kb.sb([128, 4], F32, ph)]

                def prep(tile_i, dr, latent):
                    r0 = tile_i * 128
                    bi = cnt[0] % NB
                    cnt[0] += 1
                    q_, k_, v_, vb_, ga_ = qf[bi], kf[bi], vf[bi], vb[bi], gaT[bi]
                    kh_, qT_, kT_, dec_ = kh2[bi], qT2[bi], kT2[bi], dec2[bi]
                    kb.dma("sync", q_[:], SC[r0:r0 + 128, 0:512], w=[q_])
                    kb.dma("sync", k_[:], SC[r0:r0 + 128, 512:1024], w=[k_])
                    kb.dma("sync", v_[:], SC[r0:r0 + 128, 1024:2048], w=[v_])
                    kb.dma("sync", ga_[0:16, :], SGA[dr, :, r0:r0 + 128], w=[ga_])
                    if latent:
                        c_, s_ = cs4[bi], sn4[bi]
                        kb.dma("sync", c_[:], A["cos4"][r0:r0 + 128, :], w=[c_])
                        kb.dma("sync", s_[:], A["sin4"][r0:r0 + 128, :], w=[s_])
                    kb.act(vb_[:], v_[:], AF.Copy, [v_], [vb_])
                    kb.mm(P[0][:], [(ga_[:], wa[:, dr, :])], [ga_, wa], [P[0]])
                    kb.act(e1[:], P[0][:], AF.Exp, [P[0]], [e1], scale=-1.0)
                    kb.act(sp[:], e1[:], AF.Ln, [e1], [sp], bias=1.0)
                    kb.mm(P[1][:], [(TRIf[dr], sp[:])], [sp, cp32], [P[1]])
                    kb.mm(P[2][:], [(ONESf, sp[:])], [sp, cp32], [P[2]])
                    for h in range(4):
                        kb.mm(P[0][:, 2 * h:2 * h + 2], [(sp[:, h * 128:(h + 1) * 128], ONESf[:, 0:2])], [sp, cp32], [P[0]])
                    kb.act(dec_[:], P[0][:, 0:8:2], AF.Exp, [P[0]], [dec_], scale=-1.0 / 16)
                    kb.act(EQ[:], P[1][:], AF.Exp, [P[1]], [EQ], scale=-1.0 / 16)
                    kb.act(EK[:], P[1][:], AF.Exp, [P[1]], [EK], scale=1.0 / 16)
                    kb.act(tots[:], P[2][:], AF.Copy, [P[2]], [tots])
                    kb.tt(dd[:], P[1][:], tots[:], ALU.subtract, [P[1], tots], [dd])
                    kb.act(EH[:], dd[:], AF.Exp, [dd], [EH], scale=1.0 / 16)
                    if latent:
                        rope(qr, q_, c_, s_)
                        rope(kr, k_, c_, s_)
                        qs, ks = qr, kr
                    else:
                        qs, ks = q_, k_
                    kb.stt(qt[:], qs[:], 128.0 ** -0.5, EQ[:], ALU.mult, ALU.mult, [qs, EQ], [qt])
                    kb.tt(kt[:], ks[:], EK[:], ALU.mult, [ks, EK], [kt])
                    kb.tt(kh_[:], ks[:], EH[:], ALU.mult, [ks, EH], [kh_], eng="pool")
                    pb = PB[0]
                    for h in range(4):
                        kb.tr(pb[:, h * 128:(h + 1) * 128], qt[:, h * 128:(h + 1) * 128], IDb, [qt, cpb], [pb])
                        kb.tr(pb[:, 512 + h * 128:512 + (h + 1) * 128], kt[:, h * 128:(h + 1) * 128], IDb, [kt, cpb], [pb])
                    kb.act(qT_[:], pb[:, 0:512], AF.Identity, [pb], [qT_])
                    kb.act(kT_[:], pb[:, 512:1024], AF.Identity, [pb], [kT_])
                    return (r0, vb_, kh_, qT_, kT_, dec_)

                S_all = kb.sb([128, 4, 256], F32, ph)
                Sb_all = kb.sb([128, 4, 256], BF16, ph)
                AT4 = [kb.sb([128, 4, 128], BF16, ph) for _ in range(2)]
                U01 = PB[1][:].bitcast(F32)
                hc = [0]

                def heads(ctx, dr):
                    r0, vb_, kh_, qT_, kT_, dec_ = ctx
                    if dr == 1:
                        kb.dma("sync", ofl[:], OF[r0:r0 + 128, :], w=[ofl])
                        kb.dma("sync", gg[:], SC[r0:r0 + 128, 2048:3072], w=[gg])
                    at = AT4[hc[0] % 2]
                    hc[0] += 1
                    H = [slice(h * 128, (h + 1) * 128) for h in range(4)]
                    kb.issue("pe", [(lambda e, h=h: e.matmul(P[3][:, H[h]], lhsT=kT_[:, H[h]], rhs=qT_[:, H[h]], start=True, stop=True))
                                    for h in range(4)], [kT_, qT_], [P[3]])
                    kb.tt(at[:], P[3][:].rearrange("p (h t) -> p h t", h=4), TRIf[dr].unsqueeze(1).to_broadcast([128, 4, 128]),
                          ALU.mult, [P[3], cp32], [at])
                    fns = []
                    for h in range(4):
                        po = P[4 + h // 2]
                        oc = (h % 2) * 256
                        fns.append(lambda e, h=h, po=po, oc=oc: e.matmul(po[:, oc:oc + 256], lhsT=at[:, h, :], rhs=vb_[:, h * 256:(h + 1) * 256],
                                                                         start=True, stop=False))
                        fns.append(lambda e, h=h, po=po, oc=oc: e.matmul(po[:, oc:oc + 256], lhsT=qT_[:, H[h]], rhs=Sb_all[:, h, :],
                                                                         start=False, stop=True))
                    kb.issue("pe", fns, [at, vb_, qT_, Sb_all], [P[4], P[5]])
                    usrc = []
                    fns = []
                    for h in range(4):
                        dst = (U01 if h < 2 else P[3][:, :])[:, (h % 2) * 256:(h % 2 + 1) * 256]
                        usrc.append(dst)
                        fns.append(lambda e, h=h, dst=dst: e.matmul(dst, lhsT=kh_[:, H[h]], rhs=vb_[:, h * 256:(h + 1) * 256], start=True, stop=True))
                    kb.issue("pe", fns, [kh_, vb_], [PB[1], P[3]])
                    for h in range(4):
                        kb.stt(S_all[:, h, :], S_all[:, h, :], dec_[:, h:h + 1], usrc[h], ALU.mult, ALU.add,
                               [S_all, dec_, PB[1] if h < 2 else P[3]], [S_all])
                    kb.act(Sb_all[:], S_all[:], AF.Copy, [S_all], [Sb_all])
                    if dr == 0:
                        kb.act(ost[:, 0:512], P[4][:], AF.Copy, [P[4]], [ost])
                        kb.cp(ost[:, 512:1024], P[5][:], [P[5]], [ost])
                        kb.dma("pool", OF[r0:r0 + 128, :], ost[:], r=[ost])
                    else:
                        kb.tt(ost[:, 0:512], P[4][:], ofl[:, 0:512], ALU.add, [P[4], ofl], [ost])
                        kb.tt(ost[:, 512:1024], P[5][:], ofl[:, 512:1024], ALU.add, [P[5], ofl], [ost])
                        for h in range(4):
                            kb.act(junk[:], ost[:, h * 256:(h + 1) * 256], AF.Square, [ost], [junk, ss4], accum_out=ss4[:, h:h + 1])
                        kb.act(rs4[:], ss4[:], AF.Sqrt, [ss4], [rs4], bias=EPS, scale=1.0 / 256)
                        kb.recip(rs4[:], rs4[:], [rs4], [rs4])
                        kb.act(gs[:], gg[:], AF.Silu, [gg], [gs])
                        kb.tt(gs[:], gs[:], gn[:], ALU.mult, [gs, gn], [gs], eng="pool")
                        for h in range(4):
                            hs2 = slice(h * 256, (h + 1) * 256)
                            kb.stt(oa[:, hs2], ost[:, hs2], rs4[:, h:h + 1], gs[:, hs2], ALU.mult, ALU.mult, [ost, rs4, gs], [oa])
                        pb = PB[1]
                        for q in range(8):
                            kb.tr(pb[:, q * 128:(q + 1) * 128], oa[:, q * 128:(q + 1) * 128], IDb, [oa, cpb], [pb])
                        kb.act(oaT[:].rearrange("p a b -> p (a b)"), pb[:], AF.Identity, [pb], [oaT])
                        kb.dma("pool", OAT[:, :, r0:r0 + 128].rearrange("kc p t -> p kc t"), oaT[:], r=[oaT])

                for (tile0, ntile, latent, pidx) in seqs:
                    for dr in range(2):
                        if latent:
                            kb.dma("sync", S_all[:], A["state0"][l, dr].rearrange("h p e -> p h e"), w=[S_all])
                        else:
                            kb.memset(S_all[:], 0.0, [S_all])
                        kb.act(Sb_all[:], S_all[:], AF.Copy, [S_all], [Sb_all])
                        order = list(range(ntile)) if dr == 0 else list(range(ntile - 1, -1, -1))
                        cur = prep(tile0 + order[0], dr, latent)
                        for idx in range(len(order)):
                            nxt = prep(tile0 + order[idx + 1], dr, latent) if idx + 1 < len(order) else None
                            heads(cur, dr)
                            cur = nxt
                        if not latent:
                            kb.dma("pool", OA["nst"][pidx, l, dr].rearrange("h p e -> p h e"), S_all[:], r=[S_all])
                        kb.barrier()

        def na_phase(l):
            with ExitStack() as ph:
                qg = kb.sb([128, 1024], F32, ph)
                kg = kb.sb([128, 1024], F32, ph)
                kb.dma("sync", qg[:], A["qg8"][l].partition_broadcast(128), w=[qg])
                kb.dma("sync", kg[:], A["kg8"][l].partition_broadcast(128), w=[kg])
                xin = [[kb.sb([128, 1024], F32, ph) for _ in range(2)] for _ in range(2)]
                sq = [kb.sb([128, 1024], F32, ph) for _ in range(2)]
                ss8 = [kb.sb([128, 8], F32, ph) for _ in range(2)]
                rs8 = [kb.sb([128, 8], F32, ph) for _ in range(2)]
                nf = [kb.sb([128, 1024], F32, ph) for _ in range(2)]
                nb = [kb.sb([128, 1024], BF16, ph) for _ in range(2)]
                nT = [kb.sb([128, 8, 128], BF16, ph) for _ in range(2)]

                def s1(k):
                    tile_i, which = k // 2, k % 2
                    r0 = tile_i * 128
                    b = k % 2
                    xi = xin[which][tile_i % 2]
                    c0 = 3072 + which * 1024
                    kb.dma("sync", xi[:], SC[r0:r0 + 128, c0:c0 + 1024], w=[xi])
                    kb.act(sq[b][:], xi[:], AF.Square, [xi], [sq[b]])
                    kb.issue("dve", lambda e: e.tensor_reduce(out=ss8[b][:], in_=sq[b][:].rearrange("p (h d) -> p h d", h=8),
                                                              axis=AX.X, op=ALU.add), [sq[b]], [ss8[b]])
                    kb.act(rs8[b][:], ss8[b][:], AF.Sqrt, [ss8[b]], [rs8[b]], bias=EPS, scale=1.0 / 128)
                    kb.recip(rs8[b][:], rs8[b][:], [rs8[b]], [rs8[b]])
                    n_ = nf[b]
                    gsel = qg if which == 0 else kg
                    for h in range(8):
                        hs = slice(h * 128, (h + 1) * 128)
                        kb.stt(n_[:, hs], xi[:, hs], rs8[b][:, h:h + 1], gsel[:, hs], ALU.mult, ALU.mult, [xi, rs8[b], gsel], [n_])
                    if which == 1 and tile_i >= 32:
                        pi = (tile_i - 32) // 2
                        s0 = ((tile_i - 32) % 2) * 128
                        kb.dma("pool", OA["nk"][pi, l, :, s0:s0 + 128, :].rearrange("h s d -> s h d"),
                               n_[:].rearrange("p (h d) -> p h d", h=8), r=[n_])
                    if which == 0:
                        kb.act(nb[b][:], n_[:], AF.Copy, [n_], [nb[b]], scale=128.0 ** -0.5)
                    else:
                        kb.act(nb[b][:], n_[:], AF.Copy, [n_], [nb[b]])

                def s2(k):
                    tile_i, which = k // 2, k % 2
                    r0 = tile_i * 128
                    b = k % 2
                    pb = PB[b]
                    t_ = nT[b]
                    for h in range(8):
                        kb.tr(pb[:, h * 128:(h + 1) * 128], nb[b][:, h * 128:(h + 1) * 128], IDb, [nb[b], cpb], [pb])
                    kb.act(t_[:].rearrange("p a b -> p (a b)"), pb[:], AF.Identity, [pb], [t_])
                    dst = QNT if which == 0 else KNT
                    kb.dma("pool", dst[:, :, r0:r0 + 128].rearrange("h p t -> p h t"), t_[:], r=[t_])

                s1(0)
                for k in range(2 * NT):
                    if k + 1 < 2 * NT:
                        s1(k + 1)
                    s2(k)
                kb.barrier()
            with ExitStack() as ph:
                BT = kb.sb([128, 14, 64], F32, ph)
                BTr = kb.sb([128, 14, 64], F32, ph)
                cm = kb.sb([128, 64], F32, ph)
                kb.dma("sync", cm[:], A["colmask"][:, :], w=[cm])
                qTh = [kb.sb([128, NTOK], BF16, ph) for _ in range(2)]
                kTh = [kb.sb([128, NTOK], BF16, ph) for _ in range(2)]
                vh = [kb.sb([128, NT, 128], BF16, ph) for _ in range(2)]
                kcf = kb.sb([128, 2, 128], BF16, ph)
                kcT = [kb.sb([128, 256], BF16, ph) for _ in range(2)]
                vc = [kb.sb([128, 2, 128], BF16, ph) for _ in range(2)]
                sl = [kb.sb([128, 4, 64], F32, ph) for _ in range(4)]
                El = [kb.sb([128, 4, 64], BF16, ph) for _ in range(4)]
                Ec = [kb.sb([128, 2, 64], BF16, ph) for _ in range(4)]
                Ed = [kb.sb([128, 2, 256], BF16, ph) for _ in range(2)]
                den = [kb.sb([128, 256], F32, ph) for _ in range(4)]
                PO = [T(P[4].t) for _ in range(4)]
                PD = [T(P[5].t) for _ in range(4)]
                obT = [kb.sb([128, NTOK], BF16, ph) for _ in range(2)]
                it = 0
                for h in range(8):
                    hb = h % 2
                    q_, k_, v_, kc_, vc_, ob_ = qTh[hb], kTh[hb], vh[hb], kcT[hb], vc[hb], obT[hb]
                    kb.dma("sync", q_[:], QNT[h], w=[q_])
                    kb.dma("sync", k_[:], KNT[h], w=[k_])
                    kb.dma("pool", v_[:], SC[:, 5120 + h * 128:5120 + (h + 1) * 128].rearrange("(t p) d -> p t d", p=128), w=[v_])
                    kb.dma("pool", kcf[:], A["ck"][l, h].rearrange("(t p) d -> p t d", p=128), w=[kcf])
                    kb.dma("pool", vc_[:], A["cv"][l, h].rearrange("(t p) d -> p t d", p=128), w=[vc_])
                    pb = PB[h % 2]
                    for t_ in range(2):
                        kb.tr(pb[:, t_ * 128:(t_ + 1) * 128], kcf[:, t_, :], IDb, [kcf, cpb], [pb])
                    kb.act(kc_[:], pb[:, 0:256], AF.Identity, [pb], [kc_])
                    for d0 in range(14):
                        for di in range(2):
                            src = bass.AP(tensor=I["rpbp"], offset=((l * 8 + h) * 15 + d0 + di) * 127, ap=[[1, 64], [1, 64]])
                            kb.dma("sync", BTr[di * 64:(di + 1) * 64, d0, :], src, w=[BTr])
                    for hf in range(2):
                        pj = P[hf]
                        kb.mm(pj[:, 0:448], [(JREV, BTr[:, hf * 7:(hf + 1) * 7, :].rearrange("p a b -> p (a b)"))], [cp32, BTr], [pj])
                        kb.tt(BT[:, hf * 7:(hf + 1) * 7, :], pj[:, 0:448].rearrange("p (a b) -> p a b", a=7),
                              cm[:].unsqueeze(1).to_broadcast([128, 7, 64]), ALU.add, [pj, cm], [BT])
                    for r in range(64):
                        rs_ = min(max(r - 4, 0), 56)
                        d0 = rs_ - r + 7
                        qc = slice(r * 64, (r + 1) * 64)
                        sl_i = it % 4
                        it += 1
                        pS = P[sl_i]
                        for j in range(4):
                            kc0 = (rs_ + 2 * j) * 64
                            kb.mm(pS[:, j * 64:(j + 1) * 64], [(k_[:, kc0:kc0 + 128], q_[:, qc])], [k_, q_], [pS])
                        for j in range(2):
                            kb.mm(pS[:, 256 + j * 64:256 + (j + 1) * 64], [(kc_[:, j * 128:(j + 1) * 128], q_[:, qc])], [kc_, q_], [pS])
                        kb.tt(sl[sl_i][:], pS[:, 0:256].rearrange("p (j c) -> p j c", j=4), BT[:, d0:d0 + 7:2, :], ALU.add, [pS, BT], [sl[sl_i]])
                        kb.act(El[sl_i][:], sl[sl_i][:], AF.Exp, [sl[sl_i]], [El[sl_i]])
                        kb.act(Ec[sl_i][:].rearrange("p j c -> p (j c)"), pS[:, 256:384], AF.Exp, [pS], [Ec[sl_i]])
                        po, pd = PO[sl_i], PD[sl_i]
                        oc = sl_i * 64
                        prs = []
                        prd = []
                        for j in range(4):
                            kt_i = (rs_ + 2 * j) // 2
                            prs.append((v_[:, kt_i, :], El[sl_i][:, j, :]))
                            prd.append((ONESb, El[sl_i][:, j, :]))
                        for j in range(2):
                            prs.append((vc_[:, j, :], Ec[sl_i][:, j, :]))
                            prd.append((ONESb, Ec[sl_i][:, j, :]))
                        kb.mm(po[:, oc:oc + 64], prs, [v_, vc_, El[sl_i], Ec[sl_i]], [po])
                        kb.mm(pd[:, oc:oc + 64], prd, [cpb, El[sl_i], Ec[sl_i]], [pd])
                        kb.recip(den[sl_i][:, 0:64], pd[:, oc:oc + 64], [pd], [den[sl_i]])
                        kb.tt(ob_[:, qc], po[:, oc:oc + 64], den[sl_i][:, 0:64], ALU.mult, [po, den[sl_i]], [ob_])
                    for p in range(4):
                        t0 = LS + p * 256
                        sl_i = it % 4
                        it += 1
                        pS = P[sl_i]
                        for j in range(2):
                            kb.mm(pS[:, j * 256:(j + 1) * 256], [(k_[:, t0 + j * 128:t0 + (j + 1) * 128], q_[:, t0:t0 + 256])], [k_, q_], [pS])
                        kb.act(Ed[sl_i % 2][:].rearrange("p j c -> p (j c)"), pS[:], AF.Exp, [pS], [Ed[sl_i % 2]])
                        kb.mm(P[4][:, 0:256], [(v_[:, 32 + 2 * p + j, :], Ed[sl_i % 2][:, j, :]) for j in range(2)], [v_, Ed[sl_i % 2]], PO)
                        kb.mm(P[5][:, 0:256], [(ONESb, Ed[sl_i % 2][:, j, :]) for j in range(2)], [cpb, Ed[sl_i % 2]], PD)
                        kb.recip(den[sl_i][:], P[5][:, 0:256], PD, [den[sl_i]])
                        kb.tt(ob_[:, t0:t0 + 256], P[4][:, 0:256], den[sl_i][:], ALU.mult, PO + [den[sl_i]], [ob_])
                    kb.dma("pool", OBT[h], ob_[:], r=[ob_])
                kb.barrier()

        def hyena_phase(l):
            TWO_PI = 2.0 * math.pi
            def fwd_dft(ph, zb, n_sc, n_ft, FC, FS, ftab, consume):
                for ft in range(n_ft):
                    fc_, fs_ = ftab[0][ft % 3], ftab[1][ft % 3]
                    kb.dma("sync", fc_[:, 0:n_sc * 128], FC[ft], w=[fc_])
                    kb.dma("sync", fs_[:, 0:n_sc * 128], FS[ft], w=[fs_])
                    pre, pim = P[(ft % 2) * 2], P[(ft % 2) * 2 + 1]
                    kb.mm(pre[:], [(fc_[:, s * 128:(s + 1) * 128], zb[:, s, :]) for s in range(n_sc)], [fc_, zb], [pre])
                    kb.mm(pim[:], [(fs_[:, s * 128:(s + 1) * 128], zb[:, s, :]) for s in range(n_sc)], [fs_, zb], [pim])
                    consume(ft, pre, pim)

            for (L, zpos, ndist) in ((LS, A["zposL"], A["ndistL"]), (LP, A["zposS"], A["ndistS"])):
                n_tt = L // 128
                FC, FS = (A["FCL"], A["FSL"]) if L == LS else (A["FCS"], A["FSS"])
                with ExitStack() as ph:
                    zp = kb.sb([33, L], F32, ph)
                    kb.dma("sync", zp[:], zpos[:, :], w=[zp])
                    f1w = kb.sb([33, 64], F32, ph)
                    f2w = kb.sb([64, 64], F32, ph)
                    f3w = kb.sb([64, 2048], F32, ph)
                    fb = kb.sb([64, 4], F32, ph)
                    kb.dma("sync", f1w[:], A["f1w"][l], w=[f1w])
                    kb.dma("sync", f2w[:], A["f2w"][l], w=[f2w])
                    kb.dma("sync", f3w[:], A["f3w"][l], w=[f3w])
                    kb.dma("sync", fb[:, 0:1], A["f1b"][l], w=[fb])
                    kb.dma("sync", fb[:, 1:2], A["f2b"][l], w=[fb])
                    kb.dma("sync", fb[:, 2:3], A["hfr"][l], w=[fb])
                    sc_ = kb.sb([64, 4], F32, ph)
                    kb.ts(sc_[:, 0:1], fb[:, 2:3], 1.0 / TWO_PI, None, ALU.mult, None, [fb], [sc_])
                    for q in range(2):
                        kb.ts(sc_[:, 1 + q:2 + q], fb[:, q:q + 1], sc_[:, 0:1], 8.0, ALU.mult, ALU.add, [fb, sc_], [sc_])
                    h2T = kb.sb([64, L], F32, ph)
                    h1c = kb.sb([64, 512], F32, ph)
                    u = kb.sb([64, 512], F32, ph)
                    ki = kb.sb([64, 512], I32, ph)
                    kf_ = kb.sb([64, 512], F32, ph)
                    gq_ = kb.sb([64, 512], F32, ph)

                    def sine_layer(out_ap, ps_ap, q, CW):
                        kb.ts(u[:, 0:CW], ps_ap, sc_[:, 0:1], sc_[:, 1 + q:2 + q], ALU.mult, ALU.add, [P[0], sc_], [u])
                        kb.cp(ki[:, 0:CW], u[:, 0:CW], [u], [ki])
                        kb.cp(kf_[:, 0:CW], ki[:, 0:CW], [ki], [kf_])
                        kb.tt(u[:, 0:CW], u[:, 0:CW], kf_[:, 0:CW], ALU.subtract, [u, kf_], [u])
                        kb.ts(gq_[:, 0:CW], u[:, 0:CW], 0.5, None, ALU.is_gt, None, [u], [gq_])
                        kb.tt(u[:, 0:CW], u[:, 0:CW], gq_[:, 0:CW], ALU.subtract, [u, gq_], [u])
                        kb.ts(gq_[:, 0:CW], u[:, 0:CW], -0.5, None, ALU.is_lt, None, [u], [gq_])
                        kb.tt(u[:, 0:CW], u[:, 0:CW], gq_[:, 0:CW], ALU.add, [u, gq_], [u])
                        kb.act(out_ap, u[:, 0:CW], AF.Sin, [u], [h1c, h2T], scale=TWO_PI)

                    CW = min(512, L)
                    for c in range(L // CW):
                        cs_ = slice(c * CW, (c + 1) * CW)
                        kb.mm(P[0][0:64, 0:CW], [(f1w[:], zp[:, cs_])], [f1w, zp], [P[0]])
                        sine_layer(h1c[:, 0:CW], P[0][0:64, 0:CW], 0, CW)
                        kb.mm(P[0][0:64, 0:CW], [(f2w[:], h1c[:, 0:CW])], [f2w, h1c], [P[0]])
                        sine_layer(h2T[:, cs_], P[0][0:64, 0:CW], 1, CW)
                    dl = kb.sb([128, 1024], F32, ph)
                    nd = kb.sb([128, n_tt], F32, ph)
                    kb.dma("sync", dl[:], A["deltas"].partition_broadcast(128), w=[dl])
                    kb.dma("sync", nd[:], ndist[:, :], w=[nd])
                    win = kb.sb([128, 1024], F32, ph)
                    hw = [kb.sb([128, 512], F32, ph) for _ in range(2)]
                    hsq = [kb.sb([128, 512], F32, ph) for _ in range(2)]
                    hbf = [kb.sb([128, 512], BF16, ph) for _ in range(2)]
                    rn = kb.sb([128, 2048], F32, ph)
                    ssb = kb.sb([128, 2048], F32, ph)
                    kb.memset(ssb[:], 0.0, [ssb])
                    k = 0
                    for tt_ in range(n_tt):
                        kb.act(win[:], dl[:], AF.Exp, [dl, nd], [win], scale=nd[:, tt_:tt_ + 1])
                        for cb in range(4):
                            i2 = k % 2
                            k += 1
                            pp = P[i2]
                            kb.mm(pp[:], [(h2T[:, tt_ * 128:(tt_ + 1) * 128], f3w[:, cb * 512:(cb + 1) * 512])], [h2T, f3w], [pp])
                            kb.tt(hw[i2][:], pp[:], win[:, (cb % 2) * 512:(cb % 2 + 1) * 512], ALU.mult, [pp, win], [hw[i2]])
                            kb.act(hsq[i2][:], hw[i2][:], AF.Square, [hw[i2]], [hsq[i2]])
                            kb.act(hbf[i2][:], hw[i2][:], AF.Copy, [hw[i2]], [hbf[i2]])
                            kb.dma("pool", HB[L][tt_ * 128:(tt_ + 1) * 128, cb * 512:(cb + 1) * 512], hbf[i2][:], r=[hbf[i2]])
                            p2 = P[2 + i2]
                            kb.mm(p2[:], [(ONESf, hsq[i2][:])], [cp32, hsq[i2]], [p2])
                            kb.tt(ssb[:, cb * 512:(cb + 1) * 512], ssb[:, cb * 512:(cb + 1) * 512], p2[:], ALU.add, [ssb, p2], [ssb])
                    kb.act(rn[:], ssb[:], AF.Sqrt, [ssb], [rn], bias=EPS)
                    kb.recip(rn[:], rn[:], [rn], [rn])
                    kb.barrier()
                    zbt = [kb.sb([128, n_tt, 512], BF16, ph) for _ in range(1)]
                    ftab = ([kb.sb([128, 4096], BF16, ph) for _ in range(3)], [kb.sb([128, 4096], BF16, ph) for _ in range(3)])
                    hst = [kb.sb([128, 2, 512], F32, ph) for _ in range(2)]
                    kk = [0]
                    for o in range(2):
                        for chalf in range(2):
                            c0 = o * 1024 + chalf * 512
                            zb = zbt[0]
                            kb.dma("sync", zb[:], HB[L][:, c0:c0 + 512].rearrange("(s p) c -> p s c", p=128), w=[zb])

                            def consume(ft, pre, pim, o=o, chalf=chalf, c0=c0):
                                hs_ = hst[kk[0] % 2]
                                kk[0] += 1
                                kb.tt(hs_[:, 0, :], pre[:], rn[:, c0:c0 + 512], ALU.mult, [pre, rn], [hs_])
                                kb.tt(hs_[:, 1, :], pim[:], rn[:, c0:c0 + 512], ALU.mult, [pim, rn], [hs_])
                                kb.dma("pool", HF[L][o, ft, :, :, chalf * 512:(chalf + 1) * 512], hs_[:], r=[hs_])
                            fwd_dft(ph, zb, n_tt, NFT[L], FC, FS, ftab, consume)
                    kb.barrier()

            with ExitStack() as ph:
                cw = kb.sb([128, 3, 3072], F32, ph)
                cbias = kb.sb([128, 3072], F32, ph)
                for j in range(3):
                    kb.dma("sync", cw[:, j, :], A["hcw"][l, j].partition_broadcast(128), w=[cw])
                kb.dma("sync", cbias[:], A["hcb"][l].partition_broadcast(128), w=[cbias])
                um = [kb.sb([128, 3072], F32, ph) for _ in range(2)]
                uc = [kb.sb([128, 3072], F32, ph) for _ in range(2)]
                up = [kb.sb([128, 3072], F32, ph) for _ in range(2)]
                acc = [kb.sb([128, 3072], F32, ph) for _ in range(1)]
                tm = [kb.sb([128, 3072], F32, ph) for _ in range(1)]
                zbf = [kb.sb([128, 1024], BF16, ph) for _ in range(2)]
                for tile_i in range(NT):
                    i2 = tile_i % 2
                    r0 = tile_i * 128
                    if tile_i < 32:
                        s_lo, s_hi = 0, LS
                    else:
                        s_lo = LS + ((tile_i - 32) // 2) * 256
                        s_hi = s_lo + 256
                    a_, b_, c_ = um[i2], uc[i2], up[i2]
                    kb.dma("sync", b_[:], SC[r0:r0 + 128, 6144:9216], w=[b_])
                    if r0 == s_lo:
                        kb.memset(a_[0:1, :], 0.0, [a_])
                        kb.dma("sync", a_[1:128, :], SC[r0:r0 + 127, 6144:9216], w=[a_])
                    else:
                        kb.dma("sync", a_[:], SC[r0 - 1:r0 + 127, 6144:9216], w=[a_])
                    if r0 + 128 == s_hi:
                        kb.memset(c_[:], 0.0, [c_])
                        kb.dma("sync", c_[0:127, :], SC[r0 + 1:r0 + 128, 6144:9216], w=[c_])
                    else:
                        kb.dma("sync", c_[:], SC[r0 + 1:r0 + 129, 6144:9216], w=[c_])
                    ac, t_ = acc[0], tm[0]
                    kb.tt(ac[:], a_[:], cw[:, 0, :], ALU.mult, [a_, cw], [ac])
                    kb.tt(t_[:], b_[:], cw[:, 1, :], ALU.mult, [b_, cw], [t_], eng="pool")
                    kb.tt(ac[:], ac[:], cbias[:], ALU.add, [ac, cbias], [ac])
                    kb.tt(ac[:], ac[:], t_[:], ALU.add, [ac, t_], [ac])
                    kb.tt(t_[:], c_[:], cw[:, 2, :], ALU.mult, [c_, cw], [t_], eng="pool")
                    kb.tt(ac[:], ac[:], t_[:], ALU.add, [ac, t_], [ac])
                    kb.act(zbf[i2][:], ac[:, 2048:3072], AF.Copy, [ac], [zbf[i2]])
                    kb.dma("pool", HG[r0:r0 + 128, :], ac[:, 0:2048], r=[ac])
                    kb.dma("pool", ZF[0][r0:r0 + 128, :], ac[:, 2048:3072], r=[ac])
                    kb.dma("pool", ZB[0][r0:r0 + 128, :], zbf[i2][:], r=[zbf[i2]])
                kb.barrier()

            with ExitStack() as ph:
                hbs = kb.sb([128, 2, 1024], F32, ph)
                for o in range(2):
                    kb.dma("sync", hbs[:, o, :], A["hbias"][l, o].partition_broadcast(128), w=[hbs])
                zbt = [kb.sb([128, 32, 512], BF16, ph) for _ in range(1)]
                ftab = ([kb.sb([128, 4096], BF16, ph) for _ in range(3)], [kb.sb([128, 4096], BF16, ph) for _ in range(3)])
                Yf = kb.sb([128, 24, 2, 512], BF16, ph)
                hft = [kb.sb([128, 2, 512], F32, ph) for _ in range(2)]
                m1 = kb.sb([128, 512], F32, ph)
                m2 = kb.sb([128, 512], F32, ph)
                gt = [kb.sb([128, 512], F32, ph) for _ in range(2)]
                zo = [kb.sb([128, 512], F32, ph) for _ in range(2)]
                zn = [kb.sb([128, 512], F32, ph) for _ in range(2)]
                znb = [kb.sb([128, 512], BF16, ph) for _ in range(2)]
                ocT = [kb.sb([128, 4, 128], BF16, ph) for _ in range(2)]
                cc = [0, 0, 0]
                seqs = [(0, LS)] + [(LS + p * 256, LP) for p in range(4)]
                for o in range(2):
                    for (t0, L) in seqs:
                        n_sc = L // 128
                        FC, FS, GC, GS = (A["FCL"], A["FSL"], A["GCL"], A["GSL"]) if L == LS else (A["FCS"], A["FSS"], A["GCS"], A["GSS"])
                        for chalf in range(2):
                            c0 = chalf * 512
                            zb = zbt[0]
                            cc[0] += 1
                            kb.dma("sync", zb[:, 0:n_sc, :], ZB[o][t0:t0 + L, c0:c0 + 512].rearrange("(s p) c -> p s c", p=128), w=[zb])

                            def consume(ft, pre, pim, o=o, L=L, c0=c0):
                                hf_ = hft[cc[1] % 2]
                                cc[1] += 1
                                kb.dma("sync", hf_[:], HF[L][o, ft, :, :, c0:c0 + 512], w=[hf_])
                                kb.tt(m1[:], pre[:], hf_[:, 0, :], ALU.mult, [pre, hf_], [m1])
                                kb.tt(m2[:], pim[:], hf_[:, 1, :], ALU.mult, [pim, hf_], [m2])
                                kb.tt(Yf[:, ft, 0, :], m1[:], m2[:], ALU.subtract, [m1, m2], [Yf])
                                kb.tt(m1[:], pre[:], hf_[:, 1, :], ALU.mult, [pre, hf_], [m1])
                                kb.tt(m2[:], pim[:], hf_[:, 0, :], ALU.mult, [pim, hf_], [m2])
                                kb.tt(Yf[:, ft, 1, :], m1[:], m2[:], ALU.add, [m1, m2], [Yf])
                                if ft == 0:
                                    kb.tt(Yf[0:1, 0, 0, :], pre[0:1, :], hf_[0:1, 0, :], ALU.mult, [pre, hf_], [Yf])
                                    kb.tt(Yf[0:1, 0, 1, :], pim[0:1, :], hf_[0:1, 1, :], ALU.mult, [pim, hf_], [Yf])
                            fwd_dft(ph, zb, n_sc, NFT[L], FC, FS, ftab, consume)
                            for tt_ in range(n_sc):
                                i2 = cc[2] % 2
                                cc[2] += 1
                                r0 = t0 + tt_ * 128
                                gc_, gs_ = ftab[0][tt_ % 3], ftab[1][tt_ % 3]
                                n_ft = NFT[L]
                                kb.dma("sync", gc_[:, 0:n_ft * 128], GC[tt_], w=[gc_])
                                kb.dma("sync", gs_[:, 0:n_ft * 128], GS[tt_], w=[gs_])
                                kb.dma("sync", gt[i2][:], HG[r0:r0 + 128, o * 1024 + c0:o * 1024 + c0 + 512], w=[gt[i2]])
                                kb.dma("sync", zo[i2][:], ZF[o][r0:r0 + 128, c0:c0 + 512], w=[zo[i2]])
                                py = P[4 + i2]
                                prs = []
                                for f in range(n_ft):
                                    prs.append((gc_[:, f * 128:(f + 1) * 128], Yf[:, f, 0, :]))
                                    prs.append((gs_[:, f * 128:(f + 1) * 128], Yf[:, f, 1, :]))
                                kb.mm(py[:], prs, [gc_, gs_, Yf], [py])
                                kb.tt(zo[i2][:], zo[i2][:], hbs[:, o, c0:c0 + 512], ALU.mult, [zo[i2], hbs], [zo[i2]], eng="pool")
                                kb.tt(zn[i2][:], py[:], zo[i2][:], ALU.add, [py, zo[i2]], [zn[i2]])
                                if o == 0:
                                    kb.tt(zn[i2][:], zn[i2][:], gt[i2][:], ALU.mult, [zn[i2], gt[i2]], [zn[i2]])
                                    kb.act(znb[i2][:], zn[i2][:], AF.Copy, [zn[i2]], [znb[i2]])
                                    kb.dma("pool", ZF[1][r0:r0 + 128, c0:c0 + 512], zn[i2][:], r=[zn[i2]])
                                    kb.dma("pool", ZB[1][r0:r0 + 128, c0:c0 + 512], znb[i2][:], r=[znb[i2]])
                                else:
                                    kb.tt(znb[i2][:], zn[i2][:], gt[i2][:], ALU.mult, [zn[i2], gt[i2]], [znb[i2]])
                                    pb = PB[i2]
                                    for q in range(4):
                                        kb.tr(pb[:, q * 128:(q + 1) * 128], znb[i2][:, q * 128:(q + 1) * 128], IDb, [znb[i2], cpb], [pb])
                                    kb.act(ocT[i2][:].rearrange("p a b -> p (a b)"), pb[:, 0:512], AF.Identity, [pb], [ocT[i2]])
                                    kb.dma("pool", OCT[chalf * 4:(chalf + 1) * 4, :, r0:r0 + 128].rearrange("kc p t -> p kc t"), ocT[i2][:], r=[ocT[i2]])
                    kb.barrier()

        def phase_c(l, A2, B2):
            G = 512
            with ExitStack() as ph:
                arena = kb.sb([128, 32768], BF16, ph)
                h1T = arena[:, 0:8192].rearrange("p (k t) -> p k t", k=16)
                oT = [arena[:, 8192 + b * 4096: 8192 + (b + 1) * 4096].rearrange("p (k t) -> p k t", k=8) for b in range(3)]
                hidT = arena[:, :].rearrange("p (k t) -> p k t", k=64)
                mgT = kb.sb([128, 16, G], BF16, ph)
                h2T = kb.sb([128, 16, G], BF16, ph)
                gch = [kb.sb([128, 512], F32, ph) for _ in range(2)]
                xch = [kb.sb([128, 512], F32, ph) for _ in range(2)]
                xt = kb.sb([128, D], F32, ph)
                tmp = (None, kb.sb([128, 1], F32, ph), kb.sb([128, 1], F32, ph), kb.sb([128, D], BF16, ph))
                wg = [kb.sb([128, 16, 128], BF16, ph) for _ in range(3)]
                wbr = [kb.sb([128, 8, 128], BF16, ph) for _ in range(3)]
                wo2 = [kb.sb([128, 16, 512], BF16, ph) for _ in range(2)]
                w2 = [kb.sb([128, 512], BF16, ph) for _ in range(6)]
                sg = [kb.sb([128, G], F32, ph) for _ in range(2)]
                pr = [kb.sb([128, G], F32, ph) for _ in range(2)]
                macc = kb.sb([128, G], F32, ph)
                ysb = [kb.sb([128, 512], F32, ph) for _ in range(2)]
                hs1 = [kb.sb([128, G], F32, ph) for _ in range(2)]
                x1b = T(None)
                win_src = WB["w_in"][l].rearrange("(kc p) c -> p kc c", p=128)
                brs = [WB["w_br_a"][l].rearrange("(kc p) c -> p kc c", p=128), WB["w_br_b"][l].rearrange("(kc p) c -> p kc c", p=128),
                       WB["w_br_c"][l].rearrange("(kc p) c -> p kc c", p=128)]
                wo_src = WB["w_out"][l].rearrange("(kc p) c -> p kc c", p=128)
                w1_src = WB["w_mlp1"][l].rearrange("(kc p) c -> p kc c", p=128)
                w2_src = WB["w_mlp2"][l]
                srcT = [OAT, OBT, OCT]
                c = {"wg": 0, "w2": 0, "k": 0, "y": 0}
                x_in = A["x"] if l == 0 else XS
                for g in range(NTOK // G):
                    j = 0 if g < 8 else 1
                    tk = slice(g * G, (g + 1) * G)
                    kb.dma("sync", h1T, H1T[:, :, tk].rearrange("kc p t -> p kc t"), w=[arena])
                    for b in range(3):
                        kb.dma("sync", oT[b], srcT[b][:, :, tk].rearrange("kc p t -> p kc t"), w=[arena])
                    for mf in range(16):
                        for b in range(3):
                            wg_ = wg[c["wg"] % 3]
                            wb_ = wbr[c["wg"] % 3]
                            c["wg"] += 1
                            kb.dma("sync", wg_[:], WBG[l, b * 16 + mf], w=[wg_])
                            kb.dma("sync", wb_[:], WBR[b][l, mf], w=[wb_])
                            i2 = c["k"] % 2
                            c["k"] += 1
                            pg, pp = P[i2], P[2 + i2]
                            kb.mm(pg[:], [(wg_[:, kc, :], h1T[:, kc, :]) for kc in range(16)], [wg_, arena], [pg])
                            kb.mm(pp[:], [(wb_[:, kc, :], oT[b][:, kc, :]) for kc in range(8)], [wb_, arena], [pp])
                            kb.act(sg[i2][:], pg[:], AF.Sigmoid, [pg], [sg[i2]])
                            if b == 0:
                                kb.tt(macc[:], pp[:], sg[i2][:], ALU.mult, [pp, sg[i2]], [macc])
                            else:
                                kb.tt(pr[i2][:], pp[:], sg[i2][:], ALU.mult, [pp, sg[i2]], [pr[i2]])
                                if b == 1:
                                    kb.tt(macc[:], macc[:], pr[i2][:], ALU.add, [macc, pr[i2]], [macc], eng="pool")
                                else:
                                    kb.tt(mgT[:, mf, :], macc[:], pr[i2][:], ALU.add, [macc, pr[i2]], [mgT], eng="pool")
                    for oc in range(4):
                        ocs = slice(oc * 512, (oc + 1) * 512)
                        wo = wo2[oc % 2]
                        kb.dma("sync", wo[:], wo_src[:, :, ocs], w=[wo])
                        for ti in range(4):
                            r0 = (g * 4 + ti) * 128
                            i2 = c["y"] % 2
                            c["y"] += 1
                            py, ys_, g_, x_ = P[4 + i2], ysb[i2], gch[i2], xch[i2]
                            kb.dma("sync", g_[:], GSC[j, 0, :, ocs], w=[g_])
                            kb.dma("sync", x_[:], x_in[r0:r0 + 128, ocs], w=[x_])
                            kb.mm(py[:], [(mgT[:, mf, ti * 128:(ti + 1) * 128], wo[:, mf, :]) for mf in range(16)], [mgT, wo], [py])
                            kb.tt(ys_[:], py[:], g_[:], ALU.mult, [py, g_], [ys_])
                            kb.tt(ys_[:], ys_[:], x_[:], ALU.add, [ys_, x_], [ys_], eng="pool")
                            kb.dma("pool", X1[r0:r0 + 128, ocs], ys_[:], r=[ys_], w=[x1b])
                    for ti in range(4):
                        r0 = (g * 4 + ti) * 128
                        kb.dma("sync", xt[:], X1[r0:r0 + 128, :], r=[x1b], w=[xt])
                        norm_to_T(xt, h2T, ti * 128, A2, B2, j, tmp)
                    for ff in range(64):
                        w1_ = wg[c["wg"] % 3]
                        c["wg"] += 1
                        kb.dma("sync", w1_[:], WB1[l, ff], w=[w1_])
                        i2 = c["k"] % 2
                        c["k"] += 1
                        pg = P[i2]
                        kb.mm(pg[:], [(w1_[:, kc, :], h2T[:, kc, :]) for kc in range(16)], [w1_, h2T], [pg])
                        kb.act(hs1[i2][:], pg[:], AF.Relu, [pg], [hs1[i2]])
                        kb.tt(hidT[:, ff, :], hs1[i2][:], hs1[i2][:], ALU.mult, [hs1[i2]], [arena], eng=("dve" if ff % 2 == 0 else "pool"))
                    for oc in range(4):
                        ocs = slice(oc * 512, (oc + 1) * 512)
                        pys = [P[2], P[3], P[4], P[5]]
                        for ff in range(64):
                            w2_ = w2[c["w2"] % 6]
                            c["w2"] += 1
                            kb.dma("sync", w2_[:], w2_src[ff * 128:(ff + 1) * 128, ocs], w=[w2_])
                            for ti in range(4):
                                kb.mm(pys[ti][:], [(hidT[:, ff, ti * 128:(ti + 1) * 128], w2_[:])], [arena, w2_], [pys[ti]],
                                      start=(ff == 0), stop=(ff == 63))
                        for ti in range(4):
                            tile_i = g * 4 + ti
                            r0 = tile_i * 128
                            i2 = c["y"] % 2
                            c["y"] += 1
                            ys_, g_, x_ = ysb[i2], gch[i2], xch[i2]
                            kb.dma("sync", g_[:], GSC[j, 1, :, ocs], w=[g_])
                            kb.dma("sync", x_[:], X1[r0:r0 + 128, ocs], r=[x1b], w=[x_])
                            kb.tt(ys_[:], pys[ti][:], g_[:], ALU.mult, [pys[ti], g_], [ys_])
                            kb.tt(ys_[:], ys_[:], x_[:], ALU.add, [ys_, x_], [ys_], eng="pool")
                            if l == DEPTH - 1:
                                if tile_i < 32:
                                    dst = OA["ys"][r0:r0 + 128, ocs]
                                else:
                                    dst = OA["yp"][r0 - LS:r0 - LS + 128, ocs]
                            else:
                                dst = XS[r0:r0 + 128, ocs]
                            kb.dma("pool", dst, ys_[:], r=[ys_])
                kb.barrier()

        for l in range(DEPTH):
            layer(l)
        kb.finish()
        print("n_inst", kb.n_inst, {e: len(kb.ops[e]) for e in ENGS})
    return nc


def _constants():
    bf = ml_dtypes.bfloat16
    c = {}
    p = np.arange(128)
    ident = np.eye(128, dtype=np.float32)
    tri_le = (p[:, None] <= p[None, :]).astype(np.float32)
    tri_ge = (p[:, None] >= p[None, :]).astype(np.float32)
    jrev = ((p[:, None] // 64 == p[None, :] // 64) & (p[:, None] % 64 == 63 - p[None, :] % 64)).astype(np.float32)
    c["cpack"] = np.concatenate([ident, tri_le, tri_ge, np.ones((128, 128), np.float32), jrev], axis=1)
    pos = np.arange(LS)
    row = (pos // 64).astype(np.float32)
    col = (pos % 64).astype(np.float32)
    inv = (np.float32(10000.0) ** (-np.arange(32, dtype=np.float32) / np.float32(32))).astype(np.float32)
    ang = np.concatenate([row[:, None] * inv, col[:, None] * inv], axis=-1).astype(np.float32)
    c["cos4"] = np.tile(np.cos(ang).astype(np.float32), (1, 4))
    c["sin4"] = np.tile(np.sin(ang).astype(np.float32), (1, 4))
    cc = np.arange(64)
    cstart = np.clip(cc - 8, 0, 48)
    col_in = (cc[None, :] >= cstart[:, None]) & (cc[None, :] < cstart[:, None] + 16)
    cmask = np.where(col_in.T, 0.0, NEG).astype(np.float32)
    c["colmask"] = np.concatenate([cmask, cmask], axis=0)
    for L, nm in ((LS, "L"), (LP, "S")):
        t = np.arange(L, dtype=np.float32)
        tn = (t / np.float32(L)).astype(np.float32)
        bands = np.arange(1, 17, dtype=np.float32)
        a = (np.float32(2.0 * np.pi) * tn[:, None] * bands[None, :]).astype(np.float32)
        z = np.concatenate([tn[:, None], np.cos(a), np.sin(a)], axis=-1).astype(np.float32)
        c["zpos" + nm] = np.ascontiguousarray(z.T)
        dist = (np.abs(t - L // 2) * np.float32(2.0 / L)).astype(np.float32)
        c["ndist" + nm] = np.ascontiguousarray((-dist).reshape(L // 128, 128).T)
        nf = NFT[L] * 128
        N = 2 * nf
        s = np.arange(L, dtype=np.float64)
        f = np.arange(nf, dtype=np.float64)
        th = 2.0 * np.pi * np.outer(s, f) / N
        FC = np.cos(th)
        FS = -np.sin(th)
        FS[:, 0] = np.cos(np.pi * s)
        n = np.arange(L, dtype=np.float64) + L // 2
        th2 = 2.0 * np.pi * np.outer(f, n) / N
        GC = (2.0 / N) * np.cos(th2)
        GS = -(2.0 / N) * np.sin(th2)
        GC[0, :] = 1.0 / N
        GS[0, :] = (1.0 / N) * np.cos(np.pi * n)

        def lay(M):
            R, C = M.shape
            return np.ascontiguousarray(M.reshape(R // 128, 128, C // 128, 128).transpose(2, 1, 0, 3).reshape(C // 128, 128, R)).astype(bf)
        c["FC" + nm], c["FS" + nm], c["GC" + nm], c["GS" + nm] = lay(FC), lay(FS), lay(GC), lay(GS)
    c["deltas"] = np.abs(np.linspace(math.log(1e-2) / 1.5, math.log(1e-2) / 0.3, 1024, dtype=np.float32)).astype(np.float32)
    return c


_CACHE = {}


def kernel(**inp):
    f32 = np.float32
    g = {k: np.asarray(v) for k, v in inp.items()}
    if "nc" not in _CACHE:
        _CACHE["nc"] = build_program()
        _CACHE["const"] = _constants()
    nc = _CACHE["nc"]
    const = _CACHE["const"]

    def colsT(v, n):
        return np.ascontiguousarray(v.reshape(2, n, 128).transpose(0, 2, 1))
    rp = g["na_rpb"][..., ::-1]
    rpbp = np.zeros((2, 8, 15, 127), f32)
    rpbp[..., 48:79] = rp
    shared = {
        "w_mod": g["w_mod"], "b_modT": colsT(g["b_mod"], 96), "b_mod": g["b_mod"],
        "n1T": colsT(g["norm1_g"], 16), "n2T": colsT(g["norm2_g"], 16),
        "w_in": g["w_in"], "wa2": g["gla_wa2"], "ba2": g["gla_ba2"],
        "gng4": np.tile(g["gla_norm_g"], (1, 4)), "qg8": np.tile(g["na_qnorm_g"], (1, 8)), "kg8": np.tile(g["na_knorm_g"], (1, 8)),
        "rpbp": rpbp, "hcw": g["hy_conv_w"], "hcb": g["hy_conv_b"], "f1w": g["hy_f1_w"], "f1b": g["hy_f1_b"][..., None],
        "f2w": g["hy_f2_w"], "f2b": g["hy_f2_b"][..., None], "f3w": g["hy_f3_w"], "hfr": g["hy_freq"][..., None],
        "hbias": g["hy_bias"], "w_br_a": g["w_br_a"], "w_br_b": g["w_br_b"], "w_br_c": g["w_br_c"],
        "w_out": g["w_out"], "w_mlp1": g["w_mlp1"], "w_mlp2": g["w_mlp2"],
    }
    shared.update(const)
    shared = {k: np.ascontiguousarray(v) for k, v in shared.items()}
    in_maps = []
    for i in range(NCORES):
        m = dict(shared)
        m["x"] = np.ascontiguousarray(np.concatenate([g["x_sample"][i], g["x_prompt"][4 * i:4 * i + 4].reshape(4 * LP, D)], axis=0))
        cond = np.stack([g["c"][i], g["c_ctx"]], axis=0)
        m["condT"] = np.ascontiguousarray(cond.reshape(2, 16, 128).transpose(2, 1, 0))
        m["state0"] = np.ascontiguousarray(g["state_gla"][i])
        m["ck"] = np.ascontiguousarray(g["cache_na_k"][i])
        m["cv"] = np.ascontiguousarray(g["cache_na_v"][i])
        in_maps.append(m)
    res = run_bass_kernel_spmd(nc, in_maps, core_ids=list(range(NCORES)), trace=bool(os.environ.get('KTRACE')))
    if NCORES != 8:
        print('exec_time_ns', res.exec_time_ns)
        return res.results
    R = res.results
    y_sample = np.stack([R[i]["ys"] for i in range(8)], axis=0)
    y_prompt = np.concatenate([R[i]["yp"].reshape(4, LP, D) for i in range(8)], axis=0)
    nst = np.concatenate([R[i]["nst"] for i in range(8)], axis=0)
    nk = np.concatenate([R[i]["nk"] for i in range(8)], axis=0)
    nv = np.concatenate([R[i]["nv"] for i in range(8)], axis=0)
    return (y_prompt.astype(f32), y_sample.astype(f32), nst.astype(f32), nk.astype(f32), nv.astype(f32))
```
